# Optimizing a Trainium2 kernel written in Bass

```python
import math
import jax
import jax.numpy as jnp
from jax import lax
import numpy as np

D_MODEL = 1024
BATCH = 8
SEQ = 2048
DEPTH = 2

GRID_W = 64
CTX_LEN = 256

ATTN_QK_DIM = 64
ATTN_V_DIM = 2 * ATTN_QK_DIM
ATTN_WIDTH = D_MODEL // 2
ATTN_HEADS = ATTN_WIDTH // ATTN_V_DIM
QK_WIDTH = ATTN_HEADS * ATTN_QK_DIM
FOURIER_WIDTH = D_MODEL // 4
FOURIER_GROUPS = 4
FOURIER_GROUP_DIM = FOURIER_WIDTH // FOURIER_GROUPS
CONV_WIDTH = D_MODEL // 4
CONV_GROUPS = 4
CONV_GROUP_DIM = CONV_WIDTH // CONV_GROUPS
CONV_K = 31
MIX_WIDTH = ATTN_WIDTH + FOURIER_WIDTH + CONV_WIDTH
IN_WIDTH = 4 * QK_WIDTH + ATTN_WIDTH + FOURIER_WIDTH + 2 * CONV_WIDTH
D_FF = -(-8 * D_MODEL // (3 * 256)) * 256
Q_BLOCK = 128
ROPE_BASE = 10000.0
EPS = 1e-6

kernel_name = "hybrid_diffattn_fnet_conformer_dit"


def rms_norm(x, g):
    xf = x.astype(jnp.float32)
    y = xf * lax.rsqrt(jnp.mean(xf * xf, axis=-1, keepdims=True) + EPS)
    return (y * g.astype(jnp.float32)).astype(x.dtype)


def axial_rope_tables(rows):
    row = jnp.repeat(jnp.arange(rows, dtype=jnp.float32), GRID_W)
    col = jnp.tile(jnp.arange(GRID_W, dtype=jnp.float32), rows)
    n_freq = ATTN_QK_DIM // 4
    inv_freq = ROPE_BASE ** (-jnp.arange(n_freq, dtype=jnp.float32) / n_freq)
    ang = jnp.concatenate([row[:, None] * inv_freq, col[:, None] * inv_freq], axis=-1)
    return jnp.cos(ang), jnp.sin(ang)


def apply_axial_rope(x, cos, sin):
    b, n, h, d = x.shape
    xr = x.reshape(b, n, h, 2, 2, d // 4)
    x1, x2 = xr[..., 0, :], xr[..., 1, :]
    cs = cos.reshape(1, n, 1, 2, d // 4).astype(x.dtype)
    sn = sin.reshape(1, n, 1, 2, d // 4).astype(x.dtype)
    out = jnp.stack([x1 * cs - x2 * sn, x2 * cs + x1 * sn], axis=-2)
    return out.reshape(b, n, h, d)


def diff_attention(q1, q2, k1, k2, v, lam):
    scale = ATTN_QK_DIM ** -0.5
    s1 = jnp.einsum('bhqd,bhkd->bhqk', q1, k1, preferred_element_type=jnp.float32) * scale
    s2 = jnp.einsum('bhqd,bhkd->bhqk', q2, k2, preferred_element_type=jnp.float32) * scale
    p = jax.nn.softmax(s1, axis=-1) - lam * jax.nn.softmax(s2, axis=-1)
    return jnp.einsum('bhqk,bhkv->bhqv', p.astype(v.dtype), v)


def blocked_diff_attention(q1, q2, k1, k2, v, lam):
    b, h, n, dk = q1.shape
    nblk = n // Q_BLOCK
    def to_blocks(q):
        return q.reshape(b, h, nblk, Q_BLOCK, dk).transpose(2, 0, 1, 3, 4)
    out = lax.map(lambda qs: diff_attention(qs[0], qs[1], k1, k2, v, lam), (to_blocks(q1), to_blocks(q2)))
    return out.transpose(1, 2, 0, 3, 4).reshape(b, h, n, ATTN_V_DIM)


def fourier_mix(u, w_f):
    b, n, _ = u.shape
    ug = u.astype(jnp.float32).reshape(b, n, FOURIER_GROUPS, FOURIER_GROUP_DIM)
    f = jnp.fft.fftn(ug, axes=(1, 3), norm='ortho').real.astype(u.dtype)
    y = jnp.einsum('bngc,gcd->bngd', f, w_f)
    return y.reshape(b, n, FOURIER_WIDTH)


def conformer_conv(u, conv_w, conv_b, ln_g, ln_b, w_pw2):
    b, n, _ = u.shape
    a, gate = jnp.split(u, 2, axis=-1)
    z = a * jax.nn.sigmoid(gate)
    z = lax.conv_general_dilated(z, conv_w[:, None, :].astype(z.dtype), window_strides=(1,),
                                 padding=[(CONV_K // 2, CONV_K // 2)],
                                 dimension_numbers=('NWC', 'WIO', 'NWC'),
                                 feature_group_count=CONV_WIDTH) + conv_b
    zf = z.astype(jnp.float32).reshape(b, n, CONV_GROUPS, CONV_GROUP_DIM)
    mu = jnp.mean(zf, axis=-1, keepdims=True)
    var = jnp.mean(jnp.square(zf - mu), axis=-1, keepdims=True)
    zn = ((zf - mu) * lax.rsqrt(var + EPS)).reshape(b, n, CONV_WIDTH)
    zn = zn * ln_g.astype(jnp.float32) + ln_b.astype(jnp.float32)
    return jax.nn.silu(zn).astype(u.dtype) @ w_pw2


def hybrid_mixer(h_lat, h_ctx, cos, sin, w_in, lq1, lk1, lq2, lk2, subln_g, w_fourier,
                 conv_w, conv_b, ln_g, ln_b, w_conv_out, w_out, layer_idx, need_ctx_out):
    lam_init = 0.8 - 0.6 * math.exp(-0.3 * layer_idx)
    lam = (jnp.exp(jnp.sum(lq1.astype(jnp.float32) * lk1.astype(jnp.float32)))
           - jnp.exp(jnp.sum(lq2.astype(jnp.float32) * lk2.astype(jnp.float32))) + lam_init)
    split_at = (QK_WIDTH, 2 * QK_WIDTH, 3 * QK_WIDTH, 4 * QK_WIDTH, 4 * QK_WIDTH + ATTN_WIDTH,
                4 * QK_WIDTH + ATTN_WIDTH + FOURIER_WIDTH)

    def project(h):
        return jnp.split(h @ w_in, split_at, axis=-1)

    def heads(t, dh):
        b, n, _ = t.shape
        return t.reshape(b, n, ATTN_HEADS, dh)

    def plain(t, dh):
        return heads(t, dh).transpose(0, 2, 1, 3)

    def rope(t):
        return apply_axial_rope(heads(t, ATTN_QK_DIM), cos, sin).transpose(0, 2, 1, 3)

    q1l, q2l, k1l, k2l, vl, ufl, ucl = project(h_lat)
    q1c, q2c, k1c, k2c, vc, ufc, ucc = project(h_ctx)
    k1_ctx, k2_ctx, v_ctx = plain(k1c, ATTN_QK_DIM), plain(k2c, ATTN_QK_DIM), plain(vc, ATTN_V_DIM)
    k1_all = jnp.concatenate([k1_ctx, rope(k1l)], axis=2)
    k2_all = jnp.concatenate([k2_ctx, rope(k2l)], axis=2)
    v_all = jnp.concatenate([v_ctx, plain(vl, ATTN_V_DIM)], axis=2)
    o_lat = blocked_diff_attention(rope(q1l), rope(q2l), k1_all, k2_all, v_all, lam)

    def finish(o_attn, uf, uc):
        o = rms_norm(o_attn, subln_g) * (1.0 - lam_init)
        b, h, n, dv = o.shape
        o = o.transpose(0, 2, 1, 3).reshape(b, n, h * dv)
        yf = fourier_mix(uf, w_fourier)
        yc = conformer_conv(uc, conv_w, conv_b, ln_g, ln_b, w_conv_out)
        return jnp.concatenate([o, yf, yc], axis=-1) @ w_out

    y_lat = finish(o_lat, ufl, ucl)
    y_ctx = None
    if need_ctx_out:
        o_ctx = diff_attention(plain(q1c, ATTN_QK_DIM), plain(q2c, ATTN_QK_DIM), k1_ctx, k2_ctx, v_ctx, lam)
        y_ctx = finish(o_ctx, ufc, ucc)
    return y_lat, y_ctx


def swiglu(h, w1, w3, w2):
    return (jax.nn.silu(h @ w1) * (h @ w3)) @ w2


def setup_inputs(seed: int = 0) -> dict:
    key = jax.random.key(seed)
    ks = jax.random.split(key, 25)
    f32 = jnp.float32
    D = D_MODEL

    def nrm(k, shape, s):
        return jax.random.normal(k, shape, f32) * s

    return {
        'x': nrm(ks[0], (BATCH, SEQ, D), 1.0),
        'c': nrm(ks[1], (BATCH, D), 1.0),
        'ctx': nrm(ks[2], (BATCH, CTX_LEN, D), 1.0),
        'c_ctx': nrm(ks[3], (D,), 1.0),
        'w_ada': nrm(ks[4], (DEPTH, D, 6 * D), 0.5 * D ** -0.5),
        'b_ada': nrm(ks[5], (DEPTH, 6 * D), 0.02),
        'norm1_g': 1.0 + nrm(ks[6], (DEPTH, D), 0.05),
        'norm2_g': 1.0 + nrm(ks[7], (DEPTH, D), 0.05),
        'w_in': nrm(ks[8], (DEPTH, D, IN_WIDTH), D ** -0.5),
        'lam_q1': nrm(ks[9], (DEPTH, ATTN_QK_DIM), 0.1),
        'lam_k1': nrm(ks[10], (DEPTH, ATTN_QK_DIM), 0.1),
        'lam_q2': nrm(ks[11], (DEPTH, ATTN_QK_DIM), 0.1),
        'lam_k2': nrm(ks[12], (DEPTH, ATTN_QK_DIM), 0.1),
        'subln_g': 1.0 + nrm(ks[13], (DEPTH, ATTN_V_DIM), 0.05),
        'w_fourier': nrm(ks[14], (DEPTH, FOURIER_GROUPS, FOURIER_GROUP_DIM, FOURIER_GROUP_DIM), FOURIER_GROUP_DIM ** -0.5),
        'conv_w': nrm(ks[15], (DEPTH, CONV_K, CONV_WIDTH), CONV_K ** -0.5),
        'conv_b': nrm(ks[16], (DEPTH, CONV_WIDTH), 0.02),
        'conv_ln_g': 1.0 + nrm(ks[17], (DEPTH, CONV_WIDTH), 0.05),
        'conv_ln_b': nrm(ks[18], (DEPTH, CONV_WIDTH), 0.02),
        'w_conv_out': nrm(ks[19], (DEPTH, CONV_WIDTH, CONV_WIDTH), CONV_WIDTH ** -0.5),
        'w_out': nrm(ks[20], (DEPTH, MIX_WIDTH, D), MIX_WIDTH ** -0.5),
        'w_ffn1': nrm(ks[21], (DEPTH, D, D_FF), D ** -0.5),
        'w_ffn3': nrm(ks[22], (DEPTH, D, D_FF), D ** -0.5),
        'w_ffn2': nrm(ks[23], (DEPTH, D_FF, D), D_FF ** -0.5),
        'final_g': 1.0 + nrm(ks[24], (D,), 0.05),
    }


def reference(x, c, ctx, c_ctx, w_ada, b_ada, norm1_g, norm2_g, w_in, lam_q1, lam_k1, lam_q2, lam_k2,
              subln_g, w_fourier, conv_w, conv_b, conv_ln_g, conv_ln_b, w_conv_out, w_out,
              w_ffn1, w_ffn3, w_ffn2, final_g):
    n_lat = x.shape[1]
    rows = n_lat // GRID_W
    cos, sin = axial_rope_tables(rows)
    for l in range(DEPTH):
        last = l == DEPTH - 1
        mod_lat = (jax.nn.silu(c) @ w_ada[l] + b_ada[l])[:, None, :]
        mod_ctx = (jax.nn.silu(c_ctx) @ w_ada[l] + b_ada[l])[None, None, :]
        sh1, sc1, g1, sh2, sc2, g2 = jnp.split(mod_lat, 6, axis=-1)
        csh1, csc1, cg1, csh2, csc2, cg2 = jnp.split(mod_ctx, 6, axis=-1)
        h_lat = rms_norm(x, norm1_g[l]) * (1.0 + sc1) + sh1
        h_ctx = rms_norm(ctx, norm1_g[l]) * (1.0 + csc1) + csh1
        y_lat, y_ctx = hybrid_mixer(h_lat, h_ctx, cos, sin, w_in[l], lam_q1[l], lam_k1[l], lam_q2[l], lam_k2[l],
                                    subln_g[l], w_fourier[l], conv_w[l], conv_b[l], conv_ln_g[l], conv_ln_b[l],
                                    w_conv_out[l], w_out[l], l, not last)
        x = x + g1 * y_lat
        h_lat = rms_norm(x, norm2_g[l]) * (1.0 + sc2) + sh2
        x = x + g2 * swiglu(h_lat, w_ffn1[l], w_ffn3[l], w_ffn2[l])
        if not last:
            ctx = ctx + cg1 * y_ctx
            h_ctx = rms_norm(ctx, norm2_g[l]) * (1.0 + csc2) + csh2
            ctx = ctx + cg2 * swiglu(h_ctx, w_ffn1[l], w_ffn3[l], w_ffn2[l])
    return rms_norm(x, final_g)
```

```python
import numpy as np
import ml_dtypes
import concourse.bass as bass
import concourse.mybir as mybir
from concourse.bass_utils import run_bass_kernel_spmd

F32 = mybir.dt.float32
BF16 = mybir.dt.bfloat16
AF = mybir.ActivationFunctionType
ALU = mybir.AluOpType


class _Op:
    __slots__ = ("eng", "fn", "deps", "sem", "val", "signal", "slot", "group", "idx", "dbg")


def _box(ap):
    t = ap.tensor
    shp = tuple(t.shape)
    pat = ap.ap
    off = int(ap.offset)
    sp = str(ap.space)
    esz = mybir.dt.size(ap.dtype)
    if sp == "DRAM":
        lo = off * esz
        hi = (off + sum((c - 1) * abs(s) for s, c in pat) + 1) * esz - 1
        return (ap.name, 0, 0, lo, hi, False)
    pstride = 1
    for d in shp[1:]:
        pstride *= d
    plo = off // pstride
    flo = off % pstride
    assert pat[0][0] == pstride or pat[0][1] == 1, (pat, pstride)
    phi = plo + pat[0][1] - 1
    fhi = flo + sum((c - 1) * abs(s) for s, c in pat[1:])
    lo = flo * esz
    hi = (fhi + 1) * esz - 1
    if sp == "PSUM":
        lo = (lo // 2048) * 2048
        hi = (hi // 2048) * 2048 + 2047
        return (ap.name + "@" + sp, 0, 127, lo, hi, True)
    return (ap.name + "@" + sp, plo, phi, lo, hi, False)


class Prog:
    ENGS = ("pe", "act", "dve", "pool", "sp")

    def __init__(self, nc):
        self.nc = nc
        self.ops = []
        self.track = {}
        self.slot_groups = {}
        self.slot_cur = {}

    def add(self, eng, fn, reads=(), writes=(), slot=None, group=None):
        op = _Op()
        op.eng = eng
        op.fn = fn
        op.deps = set()
        op.signal = False
        op.slot = slot
        op.idx = len(self.ops)
        op.group = None
        op.dbg = None
        if DEBUG_LINES:
            import sys as _sys
            fr = _sys._getframe(2)
            op.dbg = (fr.f_lineno, fr.f_back.f_lineno if fr.f_back else None)
        if slot is not None:
            groups = self.slot_groups.setdefault(slot, [])
            if groups and self.slot_cur.get(slot) == group and group is not None:
                groups[-1].append(op.idx)
            else:
                if groups:
                    op.deps.add(groups[-1][-1])
                groups.append([op.idx])
                self.slot_cur[slot] = group
            op.group = (slot, len(groups) - 1)
        for ap in reads:
            self._access(op, ap, False)
        for ap in writes:
            self._access(op, ap, True)
        self.ops.append(op)
        return op

    def _same_stream(self, a, b):
        if a.slot is not None or b.slot is not None:
            return a.slot is not None and a.slot == b.slot
        return a.eng == b.eng

    def _access(self, op, ap, is_write):
        key, plo, phi, lo, hi, excl = _box(ap)
        is_write = is_write or excl
        lst = self.track.setdefault(key, [])
        keep = []
        for rec in lst:
            rplo, rphi, rlo, rhi, ridx, rw = rec
            overlap = not (rphi < plo or rplo > phi or rhi < lo or rlo > hi)
            if overlap and (is_write or rw) and ridx != op.idx:
                op.deps.add(ridx)
            contained = rplo >= plo and rphi <= phi and rlo >= lo and rhi <= hi
            if contained and ridx != op.idx:
                if is_write:
                    continue
                if (not rw) and self._same_stream(self.ops[ridx], op):
                    continue
            keep.append(rec)
        keep.append([plo, phi, lo, hi, op.idx, is_write])
        self.track[key] = keep

    def finalize(self, block_sems):
        ops = self.ops
        def stream(o):
            return ("slot", o.slot) if o.slot is not None else ("eng", o.eng)
        pos = {}
        cnt = {}
        for o in ops:
            s = stream(o)
            cnt[s] = cnt.get(s, 0) + 1
            pos[o.idx] = cnt[s]
        def target(j):
            o = ops[j]
            if o.slot is not None:
                g = self.slot_groups[o.slot][o.group[1]]
                return g[-1]
            return j
        waited = {e: {} for e in self.ENGS}
        need = {}
        for o in ops:
            w = waited[o.eng]
            req = {}
            for j in o.deps:
                tj = target(j)
                if tj == o.idx:
                    continue
                if tj > o.idx:
                    assert ops[tj].slot is not None and ops[tj].group == o.group, "dep on future op"
                    continue
                s = stream(ops[tj])
                if s == ("eng", o.eng) and o.slot is None and not SAME_ENGINE_SYNC:
                    continue
                if s == ("eng", o.eng) and o.eng == "pe" and o.slot is None:
                    continue
                if s == ("eng", o.eng) and o.slot is not None and ops[tj].slot is None:
                    pass
                p = pos[tj]
                if w.get(s, 0) >= p:
                    continue
                if req.get(s, (0, None))[0] < p:
                    req[s] = (p, tj)
            lst = []
            for s, (p, tj) in req.items():
                w[s] = p
                lst.append((s, tj))
                ops[tj].signal = True
            need[o.idx] = lst
        for o in ops:
            if o.slot is not None:
                o.signal = True
        val = {}
        c = {}
        for o in ops:
            s = stream(o)
            if o.signal:
                c[s] = c.get(s, 0) + (16 if o.slot is not None else 1)
            val[o.idx] = c.get(s, 0)
        self._need, self._val, self._stream = need, val, stream
        self.max_vals = dict(c)
        return need, val

    def emit(self, engines, sems):
        need, val, stream = self._need, self._val, self._stream
        for o in self.ops:
            e = engines[o.eng]
            for s, tj in need[o.idx]:
                e.wait_ge(sems[s], val[tj])
            ins = o.fn(e)
            if o.signal:
                ins.then_inc(sems[stream(o)], 16 if o.slot is not None else 1)

    def emit_engine(self, eng_name, e, sems):
        need, val, stream = self._need, self._val, self._stream
        for o in self.ops:
            if o.eng != eng_name:
                continue
            for s, tj in need[o.idx]:
                e.wait_ge(sems[s], val[tj])
            ins = o.fn(e)
            if DEBUG_LINES:
                NAMES[getattr(ins.ins, "name", None)] = (o.idx, o.dbg)
            if o.signal:
                ins.then_inc(sems[stream(o)], 16 if o.slot is not None else 1)


SAME_ENGINE_SYNC = True
DEBUG_LINES = False
NAMES = {}


D = 1024
KD = D // 128
GRID_W = 64
INW = 2304
DFF = 2816
NFF = DFF // 128
CONV_K = 31
EPS = 1e-6
ROPE_BASE = 10000.0
VROWS = 132
R_BADA, R_N1, R_N2, R_CB, R_LG, R_LB, R_CW = 0, 48, 56, 64, 66, 68, 70


class _Rot:
    def __init__(self, items):
        self.items = list(items)
        self.i = 0

    def next(self):
        v = self.items[self.i % len(self.items)]
        self.i += 1
        return v


def build_program(N, NC, DEPTH):
    T = N + NC
    NB = N // 128
    NCB = NC // 128
    TB = NB + NCB
    lat_tiles = [(t0, 512) for t0 in range(0, N, 512)]
    all_tiles = lat_tiles + [(N, NC)]
    nc = bass.Bass("TRN2", target_bir_lowering=False)
    P = Prog(nc)

    def din(name, shape, dt=F32):
        return nc.dram_tensor(name, list(shape), dt, kind="ExternalInput").ap()

    x_d = din("x", [N, D]); ctx_d = din("ctx", [NC, D]); cc_d = din("cc", [D, 2])
    w_ada_d = din("w_ada", [DEPTH, D, 6 * D]); vecs_d = din("vecs", [384, 128])
    w_in_d = din("w_in", [DEPTH, D, INW]); lamv_d = din("lamv", [DEPTH, 4, 64])
    subln_d = din("subln_g", [DEPTH, 128]); wf_d = din("w_fourier", [DEPTH, 4, 64, 64])
    wco_d = din("w_conv_out", [DEPTH, 256, 256]); w_out_d = din("w_out", [DEPTH, D, D])
    w1_d = din("w_ffn1", [DEPTH, D, DFF]); w3_d = din("w_ffn3", [DEPTH, D, DFF]); w2_d = din("w_ffn2", [DEPTH, DFF, D])
    ropec_d = din("ropec", [128, N], BF16); ropes_d = din("ropes", [128, N], BF16)
    NKC = N // 256
    dftc_d = din("dftc", [2, NKC // 2, 128, NB // 2, 256], BF16); dfts_d = din("dfts", [2, NKC // 2, 128, NB // 2, 256], BF16)
    dcc_d = din("dcc", [128, NCB, NC], BF16); dcs_d = din("dcs", [128, NCB, NC], BF16)
    cmat_d = din("cmat", [128, 5, 128], BF16)
    cmatf_d = din("cmatf", [128, 2, 128], F32)
    out_d = nc.dram_tensor("out", [N, D], F32, kind="ExternalOutput").ap()
    NT_ALL = len(all_tiles)
    xs_d = nc.dram_tensor("xs", [NT_ALL, 128, KD, 512], F32).ap()

    def xs_tile(t0, W):
        ti_ = (t0 // 512) if t0 < N else (NT_ALL - 1)
        return xs_d[ti_, :, :, 0:W]
    modrow_d = nc.dram_tensor("modrow", [DEPTH, 2, 6 * D], F32).ap()

    from contextlib import ExitStack
    es = ExitStack()
    ARENA = 184320
    arena = es.enter_context(nc.sbuf_tensor("arena", [128, ARENA // 4], F32))
    psum = es.enter_context(nc.psum_tensor("ps", [128, 8, 512], F32))

    def AV(off, shape, dt, parts=128):
        esz = mybir.dt.size(dt)
        n = 1
        for d_ in shape:
            n *= d_
        assert off % 4 == 0 and off + n * esz <= ARENA, (off, shape)
        a = arena[:, off // 4:(off + n * esz + 3) // 4]
        if dt != F32:
            a = a.bitcast(dt)
        a = a[0:parts, 0:n]
        if len(shape) == 2:
            a = a.rearrange("p (a b) -> p a b", b=shape[1])
        elif len(shape) == 3:
            a = a.rearrange("p (a b c) -> p a b c", b=shape[1], c=shape[2])
        elif len(shape) == 4:
            a = a.rearrange("p (a b c d) -> p a b c d", b=shape[1], c=shape[2], d=shape[3])
        return a

    def PS(bank, dt=F32):
        a = psum[:, bank, :]
        return a if dt == F32 else a.bitcast(dt)

    def ST(name, shape, dt):
        return es.enter_context(nc.sbuf_tensor("s_" + name, list(shape), dt))

    cmat = ST("cmat", [128, 5, 128], BF16); cmatf = ST("cmatf", [128, 2, 128], F32)
    ident_bf, ones_bf, perm_bf = cmat[:, 0, :], cmat[:, 1, :], cmat[:, 2, :]
    ident_f, gln_f = cmatf[:, 0, :], cmatf[:, 1, :]
    ropec = ST("ropec", [128, N], BF16); ropes = ST("ropes", [128, N], BF16)
    vT = ST("vT", [128, 384], F32)
    modT = ST("modT", [128, DEPTH, 48, 2], F32)
    gm = ST("gm", [128, DEPTH, 2, KD, 2], F32)
    dcc = ST("dcc", [128, NCB, NC], BF16); dcs = ST("dcs", [128, NCB, NC], BF16)
    gcol = ST("gcol", [128, DEPTH], F32)
    lamb = ST("lamb", [128, DEPTH, 4, 64], F32); lamt = ST("lamt", [128, 2, 64], F32)
    lams = ST("lams", [128, 8], F32); neglam = ST("neglam", [128, DEPTH], F32)
    sc_f = ST("sc_f", [128, KD, 2], F32); sc_bf = ST("sc_bf", [128, KD, 2], BF16)
    small = ST("small", [128, 64], F32)
    wfbd = ST("wfbd", [128, 2, 128], BF16); wfst = ST("wfst", [128, 2, 128], F32)
    wpw2 = ST("wpw2", [128, 2, 256], BF16)
    epsT = ST("epsT", [128, 1], F32)

    def mm(out, lhsT, rhs, start=True, stop=True):
        P.add("pe", lambda e: e.matmul(out, lhsT=lhsT, rhs=rhs, start=start, stop=stop), [lhsT, rhs], [out])

    def tr(out, in_, ident):
        P.add("pe", lambda e: e.transpose(out, in_, ident), [in_, ident], [out])

    def act(out, in_, func, scale=1.0, bias=None, accum=None, eng="act"):
        rd = [in_] + [a for a in (scale, bias) if not isinstance(a, (int, float, type(None)))]
        wr = [out] + ([accum] if accum is not None else [])
        kw = {}
        if bias is not None:
            kw["bias"] = bias
        if accum is not None:
            kw["accum_out"] = accum
        P.add("act", lambda e: e.activation(out=out, in_=in_, func=func, scale=scale, **kw), rd, wr)

    def tt(eng, out, a, b, op):
        P.add(eng, lambda e: e.tensor_tensor(out=out, in0=a, in1=b, op=op), [a, b], [out])

    def ts(eng, out, a, s1, op0, s2=None, op1=None):
        rd = [a] + [s for s in (s1, s2) if not isinstance(s, (int, float, type(None)))]
        if op1 is None:
            P.add(eng, lambda e: e.tensor_scalar(out=out, in0=a, scalar1=s1, scalar2=None, op0=op0), rd, [out])
        else:
            P.add(eng, lambda e: e.tensor_scalar(out=out, in0=a, scalar1=s1, scalar2=s2, op0=op0, op1=op1), rd, [out])

    def stt(out, a, s, b, op0, op1):
        rd = [a, b] + ([s] if not isinstance(s, (int, float)) else [])
        P.add("dve", lambda e: e.scalar_tensor_tensor(out=out, in0=a, scalar=s, in1=b, op0=op0, op1=op1), rd, [out])

    def cp(eng, out, in_):
        if eng == "act":
            P.add("act", lambda e: e.activation(out=out, in_=in_, func=AF.Copy), [in_], [out])
        else:
            P.add(eng, lambda e: e.tensor_copy(out=out, in_=in_), [in_], [out])

    def recip(out, in_):
        P.add("dve", lambda e: e.reciprocal(out=out, in_=in_), [in_], [out])

    def mset(eng, ap, val):
        P.add(eng, lambda e: e.memset(ap, val), [], [ap])

    def dma(eng, out, in_, slot, group=None):
        P.add(eng, lambda e: e.dma_start(out=out, in_=in_), [in_], [out], slot=slot, group=group)

    cpr = _Rot(["act", "dve"])

    dma("sp", cmat[:], cmat_d, "c0"); dma("sp", cmatf[:], cmatf_d, "c1")
    dma("sp", ropec[:], ropec_d, "c2"); dma("sp", ropes[:], ropes_d, "c3")
    dma("sp", dcc[:], dcc_d, "c0"); dma("sp", dcs[:], dcs_d, "c1")
    mset("dve", epsT[:], EPS)
    mset("dve", small[:, 0:8], -0.5)
    vst = AV(0, (3, 128), F32)
    dma("sp", vst, vecs_d.rearrange("(j p) f -> p j f", p=128), "c2")
    for j in range(3):
        tr(PS(0)[:, j * 128:(j + 1) * 128], vst[:, j, :], ident_f)
    cp("dve", vT[:], PS(0)[:, 0:384])
    for l in range(DEPTH):
        dma("sp", lamb[:, l], lamv_d[l].partition_broadcast(128), "c3")
        dma("sp", gcol[:, l:l + 1], subln_d[l].rearrange("(p o) -> p o", o=1), "c0")
        lam_init = 0.8 - 0.6 * float(np.exp(-0.3 * l))
        tt("dve", lamt[:, 0, :], lamb[:, l, 0, :], lamb[:, l, 1, :], ALU.mult)
        tt("dve", lamt[:, 1, :], lamb[:, l, 2, :], lamb[:, l, 3, :], ALU.mult)
        P.add("dve", lambda e: e.tensor_reduce(out=lams[:, 0:2], in_=lamt[:], axis=mybir.AxisListType.X, op=ALU.add), [lamt[:]], [lams[:, 0:2]])
        act(lams[:, 2:4], lams[:, 0:2], AF.Exp)
        tt("dve", lams[:, 4:5], lams[:, 3:4], lams[:, 2:3], ALU.subtract)
        ts("dve", neglam[:, l:l + 1], lams[:, 4:5], -lam_init, ALU.add)
        ts("dve", gcol[:, l:l + 1], gcol[:, l:l + 1], 1.0 - lam_init, ALU.mult)
    dma("sp", sc_f[:], cc_d.rearrange("(k p) t -> p k t", p=128), "c1")
    act(sc_bf[:], sc_f[:], AF.Silu)
    WA_O = 64800
    wa_bufs = [AV(WA_O + i * 4096, (KD, 256), BF16) for i in range(2)]
    mstage_t = ST("mstage", [2, 256], F32)
    mstage = mstage_t[:, :]
    mrow_t = ST("mrow", [96, 128], F32)
    mrow = mrow_t[:, :]
    pbr = _Rot(range(1, 8))
    all_pieces = [(0, pc) for pc in range(24)]
    for l_ in range(1, DEPTH):
        all_pieces += [(l_, pc) for pc in range(24)]
    mstate = {"dma": 0, "mm": 0}

    def mods_dma():
        i = mstate["dma"]
        if i >= len(all_pieces):
            return
        l_, pc = all_pieces[i]
        dma("pool", wa_bufs[i % 2], w_ada_d[l_, :, pc * 256:(pc + 1) * 256].rearrange("(k p) n -> p k n", p=128), "wa%d" % (i % 2))
        mstate["dma"] = i + 1

    def mods_piece(bank):
        i = mstate["mm"]
        while mstate["dma"] <= min(i + 1, len(all_pieces) - 1):
            mods_dma()
        l_, pc = all_pieces[i]
        wa = wa_bufs[i % 2]
        pb = PS(bank)
        for k in range(KD):
            mm(pb[0:2, 0:256], sc_bf[:, k, :], wa[:, k, :], start=(k == 0), stop=(k == KD - 1))
        cp("dve", mstage, pb[0:2, 0:256])
        dma("sp", modrow_d[l_, :, pc * 256:(pc + 1) * 256], mstage, "mst")
        mstate["mm"] = i + 1

    def mods_finish(l, s0, s1, bank):
        r0, r1 = s0 * 8, s1 * 8
        nr = r1 - r0
        for v in range(2):
            dma("sp", mrow[v * nr:(v + 1) * nr, :], modrow_d[l, v, r0 * 128:r1 * 128].rearrange("(r p) -> r p", p=128), "mld", group="mf%d_%d" % (l, s0))
        pb = PS(bank)
        tr(pb[:, 0:2 * nr], mrow[0:2 * nr, :], ident_f[0:2 * nr, 0:2 * nr])
        for v in range(2):
            tt("dve", modT[:, l, r0:r1, v], pb[:, v * nr:(v + 1) * nr], vT[:, l * VROWS + R_BADA + r0: l * VROWS + R_BADA + r1], ALU.add)
        for n_i, (sec_sc, rg) in enumerate(((1, R_N1), (4, R_N2))):
            if not (s0 <= sec_sc < s1):
                continue
            for k in range(KD):
                ts("dve", gm[:, l, n_i, k, :], modT[:, l, sec_sc * 8 + k, :], 1.0, ALU.add)
                ts("dve", gm[:, l, n_i, k, :], gm[:, l, n_i, k, :], vT[:, l * VROWS + rg + k: l * VROWS + rg + k + 1], ALU.mult)

    def load_w_in(l_):
        w_in_ = AV(110592, (KD, INW), BF16)
        for hf in range(2):
            for k in range(KD):
                dma("pool", w_in_[:, k, hf * 1152:(hf + 1) * 1152], w_in_d[l_, k * 128:(k + 1) * 128, hf * 1152:(hf + 1) * 1152], "win%d" % hf, group="l%d" % l_)

    load_w_in(0)

    def bg_mods(max_layer, bank):
        i = mstate["mm"]
        if i < len(all_pieces) and all_pieces[i][0] <= max_layer:
            mods_piece(bank)
            if i == 23:
                mods_finish(0, 2, 6, bank)
            elif i > 23 and (i + 1) % 24 == 0:
                mods_finish(all_pieces[i][0], 0, 6, bank)
            return True
        return False

    XIN = 90112
    xtoks = [AV(16384, (4, D), F32), AV(73728, (4, D), F32)]
    xins = [AV(XIN, (KD, 512), F32), AV(32768, (KD, 512), F32)]

    def x_load(ti):
        t0, W = all_tiles[ti]
        src = x_d[t0:t0 + W, :] if t0 < N else ctx_d[:, :]
        dma("sp", xtoks[ti % 2][:, 0:W // 128, :], src.rearrange("(s p) d -> p s d", p=128), "xl%d" % (ti % 2))

    x_load(0)
    mods_done = 0
    for ti, (t0, W) in enumerate(all_tiles):
        nsub = W // 128
        if ti + 1 < len(all_tiles):
            x_load(ti + 1)
        xtok = xtoks[ti % 2]
        xin = xins[ti % 2][:, :, 0:W]
        for k in range(KD):
            pb = PS(pbr.next())
            for s_ in range(nsub):
                tr(pb[:, s_ * 128:(s_ + 1) * 128], xtok[:, s_, k * 128:(k + 1) * 128], ident_f)
            cp(cpr.next(), xin[:, k, :], pb[:, 0:W])
        dma("sp", xs_tile(t0, W), xin, "xst%d" % ((t0 // 512) % 2))
        for _ in range(2):
            if mods_done < 8:
                mods_piece(pbr.next()); mods_done += 1
    while mods_done < 8:
        mods_piece(pbr.next()); mods_done += 1
    mods_finish(0, 0, 2, pbr.next())

    QT_O, KT_O, V_O, UF_O = 0, 18432, 36864, 55584
    MIX_O = 73728
    WIN_O = 110592
    Z_O = 147456
    QT = AV(QT_O, (4, T), BF16); KT = AV(KT_O, (4, T), BF16)
    V = AV(V_O, (TB, 4, 130), BF16)
    ufT = AV(UF_O, (2, T), BF16)
    mixT = AV(MIX_O, (KD, T), BF16)
    zl = AV(Z_O, (2, N + 30), BF16)
    zc_ = AV(Z_O + 2 * (N + 30) * 2, (2, NC + 30), BF16)
    XRES = AV(0, (KD, T), F32)

    def norm_sq(xin, W, sq):
        act(sq[:, :, 0:W], xin, AF.Square)

    def norm_rest(l, n_i, sec_sh, xin, W, col, hT, sq, rstd, sqrtt, tmps, bank=None):
        pb = PS(pbr.next() if bank is None else bank)
        for k in range(KD):
            mm(pb[:, 0:W], ones_bf, sq[:, k, 0:W], start=(k == 0), stop=(k == KD - 1))
        act(sqrtt[:, 0:W], pb[:, 0:W], AF.Ln, scale=1.0 / D, bias=epsT[:, 0:1])
        act(rstd[:, 0:W], sqrtt[:, 0:W], AF.Exp, scale=-0.5)
        for k in range(KD):
            tmp = tmps.next()
            stt(tmp[:, 0:W], xin[:, k, :], gm[:, l, n_i, k, col:col + 1], rstd[:, 0:W], ALU.mult, ALU.mult)
            act(hT[:, k, 0:W], tmp[:, 0:W], AF.Identity, bias=modT[:, l, sec_sh * 8 + k, col:col + 1])

    def norm_tile(l, n_i, sec_sh, xin, W, col, hT, sq, rstd, sqrtt, tmps):
        norm_sq(xin, W, sq)
        norm_rest(l, n_i, sec_sh, xin, W, col, hT, sq, rstd, sqrtt, tmps)

    for l in range(DEPTH):
        last = (l == DEPTH - 1)
        vb = l * VROWS
        w_in = AV(WIN_O, (KD, INW), BF16)
        if l > 0:
            load_w_in(l)
        mset("pool", wfst[:], 0.0)
        for g in range(4):
            c_, o_ = g // 2, (g % 2) * 64
            dma("sp", wfst[o_:o_ + 64, c_, o_:o_ + 64], wf_d[l, g], "c2", group="wf%d" % l)
        cp("dve", wfbd[:], wfst[:])
        dma("pool", wpw2[:], wco_d[l].rearrange("(c p) d -> p c d", p=128), "wpw")
        for zz, W_ in ((zl, N), (zc_, NC)):
            mset("pool", zz[:, :, 0:15], 0.0)
            mset("pool", zz[:, :, W_ + 15:W_ + 30], 0.0)
        sq = AV(MIX_O + 16384, (KD, 512), BF16)
        hTs = [AV(MIX_O + 24576, (KD, 512), BF16), AV(Z_O + 9472, (KD, 512), BF16)]
        rstd = AV(MIX_O + 32768, (512,), F32); sqrtt = AV(MIX_O + 34816, (512,), F32)
        TO = Z_O + 9472 + 8192
        tmpsA = _Rot([AV(TO, (512,), F32), AV(TO + 2048, (512,), F32)])
        qraws = _Rot([AV(TO + 4096, (512,), BF16), AV(TO + 5120, (512,), BF16)])
        t1s = _Rot([AV(TO + 6144, (512,), F32), AV(TO + 8192, (512,), F32)])
        t2s = _Rot([AV(TO + 10240, (512,), F32), AV(TO + 12288, (512,), F32)])
        sigs = _Rot([AV(TO + 14336, (512,), F32), AV(TO + 16384, (512,), F32)])
        def p1_load(ti):
            t0, W = all_tiles[ti]
            xin = AV(MIX_O, (KD, W), F32)
            dma("sp", xin, xs_tile(t0, W), "xin")

        def p1_norm(ti):
            t0, W = all_tiles[ti]
            col = 1 if t0 >= N else 0
            xin = AV(MIX_O, (KD, W), F32)
            norm_tile(l, 0, 0, xin, W, col, hTs[ti % 2], sq, rstd, sqrtt, tmpsA)

        def p1_norm_sq(ti):
            t0, W = all_tiles[ti]
            norm_sq(AV(MIX_O, (KD, W), F32), W, sq)

        def p1_norm_rest(ti):
            t0, W = all_tiles[ti]
            col = 1 if t0 >= N else 0
            norm_rest(l, 0, 0, AV(MIX_O, (KD, W), F32), W, col, hTs[ti % 2], sq, rstd, sqrtt, tmpsA)

        def rope_tail(qraw, t1, dst, t0, W):
            t2 = t2s.next()
            pb2 = PS(pbr.next())
            mm(pb2[:, 0:W], perm_bf, qraw[:, 0:W])
            tt("dve", t2[:, 0:W], pb2[:, 0:W], ropes[:, t0:t0 + W], ALU.mult)
            tt("pool", dst, t1[:, 0:W], t2[:, 0:W], ALU.add)

        nt_ = len(all_tiles)
        p1_load(0); p1_norm(0)
        if nt_ > 1:
            p1_load(1)
        for ti, (t0, W) in enumerate(all_tiles):
            is_ctx = t0 >= N
            col = 1 if is_ctx else 0
            hT = hTs[ti % 2]
            pend_rope = None
            if ti + 1 < nt_:
                p1_norm_sq(ti + 1)
            ctx_kv_only = is_ctx and last
            for ch in range(8):
                if ctx_kv_only and ch < 4:
                    continue
                pb = PS(pbr.next())
                for k in range(KD):
                    mm(pb[:, 0:W], w_in[:, k, ch * 128:(ch + 1) * 128], hT[:, k, 0:W], start=(k == 0), stop=(k == KD - 1))
                dst = (QT if ch < 4 else KT)[:, ch % 4, t0:t0 + W]
                if is_ctx:
                    cp(cpr.next(), dst, pb[:, 0:W])
                else:
                    qraw = qraws.next(); t1 = t1s.next()
                    cp("act", qraw[:, 0:W], pb[:, 0:W])
                    tt("dve", t1[:, 0:W], pb[:, 0:W], ropec[:, t0:t0 + W], ALU.mult)
                    if pend_rope is not None:
                        rope_tail(*pend_rope)
                    pend_rope = (qraw, t1, dst, t0, W)
            bg_mods(l, pbr.next())
            if ti + 1 < nt_:
                p1_norm_rest(ti + 1)
                if ti + 2 < nt_:
                    p1_load(ti + 2)
            for s_ in range(W // 128):
                pb = PS(pbr.next())
                for k in range(KD):
                    mm(pb[:, :], hT[:, k, s_ * 128:(s_ + 1) * 128], w_in[:, k, 1024:1536], start=(k == 0), stop=(k == KD - 1))
                tb = (t0 // 128) + s_
                cp(cpr.next(), V[:, tb, :, 0:128], pb[:, :].rearrange("p (h d) -> p h d", d=128))
                if s_ == 0 and pend_rope is not None:
                    rope_tail(*pend_rope)
                    pend_rope = None
            if ctx_kv_only:
                bg_mods(l, pbr.next())
                continue
            for c_ in range(2):
                pb = PS(pbr.next())
                for k in range(KD):
                    mm(pb[:, 0:W], w_in[:, k, 1536 + c_ * 128:1536 + (c_ + 1) * 128], hT[:, k, 0:W], start=(k == 0), stop=(k == KD - 1))
                cp(cpr.next(), ufT[:, c_, t0:t0 + W], pb[:, 0:W])
            for c_ in range(2):
                pa = PS(pbr.next()); pg = PS(pbr.next())
                for k in range(KD):
                    mm(pa[:, 0:W], w_in[:, k, 1792 + c_ * 128:1792 + (c_ + 1) * 128], hT[:, k, 0:W], start=(k == 0), stop=(k == KD - 1))
                for k in range(KD):
                    mm(pg[:, 0:W], w_in[:, k, 2048 + c_ * 128:2048 + (c_ + 1) * 128], hT[:, k, 0:W], start=(k == 0), stop=(k == KD - 1))
                sg = sigs.next()
                act(sg[:, 0:W], pg[:, 0:W], AF.Sigmoid)
                zdst = zc_[:, c_, 15:15 + W] if is_ctx else zl[:, c_, 15 + t0:15 + t0 + W]
                tt("dve", zdst, pa[:, 0:W], sg[:, 0:W], ALU.mult)
            bg_mods(l, pbr.next())

        WOUT_O = 159744
        w_out = AV(WOUT_O, (KD, D), BF16)
        for k in range(KD):
            dma("pool", w_out[:, k, :], w_out_d[l, k * 128:(k + 1) * 128, :], "wout%d" % (k % 2), group="l%d" % l)

        E_O = WIN_O
        QT_W = 256
        Ebufs = _Rot([AV(E_O + 24576 + i * 1024, (512,), BF16) for i in range(6)])
        Esums = _Rot([AV(E_O + 30720 + i * 1024, (512,), BF16) for i in range(4)])
        addr = _Rot(["dve", "dve", "pool"])
        rrs = _Rot([AV(E_O + 3072 + i * 2048, (512,), F32) for i in range(2)])
        t12s = _Rot([AV(E_O + 7168 + i * 2048, (512,), F32) for i in range(2)])
        ofs = _Rot([AV(E_O + 11264 + i * 1024, (256,), F32) for i in range(3)])
        sqb = _Rot([AV(E_O + 14336 + i * 512, (256,), BF16) for i in range(3)])
        lnb = _Rot([AV(E_O + 15872 + i * 1024, (256,), F32) for i in range(2)])
        rsb = _Rot([AV(E_O + 17920 + i * 1024, (256,), F32) for i in range(2)])
        sbr = _Rot([0, 1, 6])
        ssr = _Rot([7])
        qsets = []
        for q0 in range(0, N, QT_W):
            qsets.append((q0, QT_W, list(range(TB))))
        if not last:
            for q0 in range(0, NC, QT_W):
                qsets.append((N + q0, min(QT_W, NC - q0), list(range(NB, TB))))

        Qz = [[AV(E_O + 20480 + (hpar * 2 + tp_) * 1024, (2, 256), BF16) for tp_ in range(2)] for hpar in range(2)]
        for hpar in range(2):
            for tp_ in range(2):
                mset("pool", Qz[hpar][tp_][:, :, :], 0.0)

        def att_Q(h, q0, QW, tpar):
            hp = slice((h % 2) * 64, (h % 2) * 64 + 64)
            c1, c2 = h // 2, 2 + h // 2
            qz = Qz[h % 2][tpar]
            cp("pool", qz[hp, 0, 0:QW], QT[hp, c1, q0:q0 + QW])
            cp("pool", qz[hp, 1, 0:QW], QT[hp, c2, q0:q0 + QW])

        def att_S(h, q0, QW, kb, tpar):
            c1, c2 = h // 2, 2 + h // 2
            qz = Qz[h % 2][tpar]
            sb = PS(sbr.next())
            mm(sb[:, 0:QW], KT[:, c1, kb * 128:(kb + 1) * 128], qz[:, 0, 0:QW])
            mm(sb[:, 256:256 + QW], KT[:, c2, kb * 128:(kb + 1) * 128], qz[:, 1, 0:QW])
            E = Ebufs.next()
            if QW == 256:
                act(E[:, 0:512], sb[:, 0:512], AF.Exp, scale=0.125)
            else:
                act(E[:, :].rearrange("p (m q) -> p m q", m=2)[:, :, 0:QW], sb[:, :].rearrange("p (m q) -> p m q", m=2)[:, :, 0:QW], AF.Exp, scale=0.125)
            return E

        def att_PV(h, QW, ki, nk, kb, E, par):
            ob = PS(2 + par)
            f, l_ = (ki == 0), (ki == nk - 1)
            mm(ob[:, 0:QW], V[:, kb, h, 0:128], E[:, 0:QW], start=f, stop=False)
            mm(ob[:, 256:256 + QW], V[:, kb, h, 0:128], E[:, 256:256 + QW], start=False, stop=l_)

        def att_SUMS(QW, first, lastj, Es, par):
            sk = PS(4 + par)
            mm(sk[:, 0:QW], ones_bf, Es[:, 0:QW], start=first, stop=False)
            mm(sk[:, 256:256 + QW], ones_bf, Es[:, 256:256 + QW], start=False, stop=lastj)

        def att_fin_a(h, q0, QW, par):
            ob = PS(2 + par); sk = PS(4 + par)
            rr = rrs.next(); t12 = t12s.next(); of = ofs.next(); sq_ = sqb.next()
            if QW == 256:
                recip(rr[:, 0:512], sk[:, 0:512])
                tt("dve", t12[:, 0:512], ob[:, 0:512], rr[:, 0:512], ALU.mult)
            else:
                v3 = lambda a_: a_[:, :].rearrange("p (m q) -> p m q", m=2)[:, :, 0:QW]
                recip(v3(rr), v3(sk))
                tt("dve", v3(t12), v3(ob), v3(rr), ALU.mult)
            stt(of[:, 0:QW], t12[:, 256:256 + QW], neglam[:, l:l + 1], t12[:, 0:QW], ALU.mult, ALU.add)
            tt("dve", sq_[:, 0:QW], of[:, 0:QW], of[:, 0:QW], ALU.mult)
            return (h, q0, QW, of, sq_)

        def att_fin_b(h, q0, QW, of, sq_):
            ssb = PS(ssr.next())
            mm(ssb[:, 0:QW], ones_bf, sq_[:, 0:QW])
            ln_ = lnb.next(); rs_ = rsb.next()
            act(ln_[:, 0:QW], ssb[:, 0:QW], AF.Ln, scale=1.0 / 128, bias=epsT[:, 0:1])
            act(rs_[:, 0:QW], ln_[:, 0:QW], AF.Exp, scale=-0.5)
            stt(mixT[:, h, q0:q0 + QW], of[:, 0:QW], gcol[:, l:l + 1], rs_[:, 0:QW], ALU.mult, ALU.mult)

        stages = []
        tiles_ = []
        tno = 0
        for h in range(4):
            for (q0, QW, kbs) in qsets:
                tiles_.append((h, q0, QW, tno))
                for ki, kb in enumerate(kbs):
                    stages.append((h, q0, QW, ki, len(kbs), kb, tno))
                tno += 1
        hcnt = [0, 0]
        tpar_of = {}
        for (h, q0, QW, tn) in tiles_:
            tpar_of[tn] = hcnt[h % 2] % 2
            hcnt[h % 2] += 1
        att_Q(tiles_[0][0], tiles_[0][1], tiles_[0][2], tpar_of[0])
        LA = 2
        fifo = []
        pend = []

        sums_q = []
        hold = {}

        def run_sums(job):
            jQW, jfirst, jlast, jEs, jpar, jfin = job
            att_SUMS(jQW, jfirst, jlast, jEs, jpar)
            if jlast:
                pend.append([10, att_fin_a(*jfin)])

        def do_pv(it):
            ph, pq0, pQW, pki, pnk, pkb, ppar, pE = it
            att_PV(ph, pQW, pki, pnk, pkb, pE, ppar)
            while sums_q:
                run_sums(sums_q.pop(0))
            lastk = (pki == pnk - 1)
            fin = (ph, pq0, pQW, ppar)
            if pki % 2 == 0 and not lastk:
                hold["E"] = pE
            elif pki % 2 == 1:
                Es = Esums.next()
                if pQW == 256:
                    tt(addr.next(), Es[:, 0:512], hold["E"][:, 0:512], pE[:, 0:512], ALU.add)
                else:
                    v3_ = lambda a_: a_[:, :].rearrange("p (m q) -> p m q", m=2)[:, :, 0:pQW]
                    tt(addr.next(), v3_(Es), v3_(hold["E"]), v3_(pE), ALU.add)
                sums_q.append((pQW, pki == 1, lastk, Es, ppar, fin))
            else:
                while sums_q:
                    run_sums(sums_q.pop(0))
                run_sums((pQW, pki == 0, True, pE, ppar, fin))

        for st in stages:
            h, q0, QW, ki, nk, kb, tn = st
            par = tn % 2
            if ki == 0 and tn + 1 < len(tiles_):
                nh, nq0, nQW, ntn = tiles_[tn + 1]
                att_Q(nh, nq0, nQW, tpar_of[ntn])
            if ki == 4:
                bg_mods(l + 1, 7)
            E = att_S(h, q0, QW, kb, tpar_of[tn])
            fifo.append((h, q0, QW, ki, nk, kb, par, E))
            if len(fifo) > LA:
                do_pv(fifo.pop(0))
            for it in pend:
                it[0] -= 1
            while pend and pend[0][0] <= 0:
                att_fin_b(*pend.pop(0)[1])
        while fifo:
            do_pv(fifo.pop(0))
        while sums_q:
            run_sums(sums_q.pop(0))

        while bg_mods(l + 1, 7):
            pass

        AB = AV(0, (TB, 512), BF16)
        DFB = [(AV(18432 + i * 8192, (NB // 2, 256), BF16), AV(18432 + i * 8192 + 4096, (NB // 2, 256), BF16)) for i in range(2)]
        FTs = _Rot([AV(E_O + i * 1024, (2, 256), BF16) for i in range(2)])
        pbr3 = _Rot(range(8))
        NH = NB // 2
        ABp = AV(34816, (NH, 512), BF16); ABm = AV(34816 + NH * 1024, (NH, 512), BF16)

        def stage_a(tb):
            pb = PS(pbr3.next())
            for c_ in range(2):
                mm(pb[:, c_ * 256:(c_ + 1) * 256], ufT[:, c_, tb * 128:(tb + 1) * 128], cmat[:, 3:5, :].rearrange("p a b -> p (a b)"))
            cp(cpr.next(), AB[:, tb, :], pb[:, :])

        for i in range(NH):
            stage_a(i); stage_a(i + NH)
            tt("dve", ABp[:, i, :], AB[:, i, :], AB[:, i + NH, :], ALU.add)
            tt("pool", ABm[:, i, :], AB[:, i, :], AB[:, i + NH, :], ALU.subtract)
        if not last:
            for tb in range(NB, TB):
                stage_a(tb)
        while pend:
            att_fin_b(*pend.pop(0)[1])

        pend_f = None

        def fourier_out1(pb, Wk, tok0):
            FT = FTs.next()
            cp("act", FT[:, :, 0:Wk], pb[:, :].rearrange("p (c k) -> p c k", c=2)[:, :, 0:Wk])
            return (FT, Wk, tok0)

        def fourier_out(pb, Wk, tok0):
            fourier_out2(*fourier_out1(pb, Wk, tok0))

        def fourier_out2(FT, Wk, tok0):
            pb2 = PS(pbr3.next())
            for c_ in range(2):
                mm(pb2[:, c_ * 256:c_ * 256 + Wk], wfbd[:, c_, :], FT[:, c_, 0:Wk])
            dst = tok0 if not isinstance(tok0, int) else mixT[:, 4:6, tok0:tok0 + Wk]
            cp("dve", dst, pb2[:, :].rearrange("p (c k) -> p c k", c=2)[:, :, 0:Wk])

        step = 0
        for par, ABx in ((0, ABp), (1, ABm)):
            for kc in range(NKC // 2):
                Cb, Sb = DFB[step % 2]
                dma("sp", Cb, dftc_d[par, kc], "dfc%d" % (step % 2)); dma("sp", Sb, dfts_d[par, kc], "dfs%d" % (step % 2))
                step += 1
                pb = PS(pbr3.next())
                for c_ in range(2):
                    for nb in range(NH):
                        mm(pb[:, c_ * 256:(c_ + 1) * 256], ABx[:, nb, c_ * 256:c_ * 256 + 128], Cb[:, nb, :], start=(nb == 0), stop=False)
                        mm(pb[:, c_ * 256:(c_ + 1) * 256], ABx[:, nb, c_ * 256 + 128:c_ * 256 + 256], Sb[:, nb, :], start=False, stop=(nb == NH - 1))
                if pend_f is not None:
                    fourier_out2(*pend_f)
                dst = mixT[:, 4:6, kc * 512:(kc + 1) * 512].rearrange("p c (j two) -> p c j two", two=2)[:, :, :, par]
                pend_f = fourier_out1(pb, 256, dst)
        if pend_f is not None:
            fourier_out2(*pend_f)
            pend_f = None
        if not last:
            for k0 in range(0, NC, 256):
                Wk = min(256, NC - k0)
                pb = PS(pbr3.next())
                for c_ in range(2):
                    for nb in range(NCB):
                        mm(pb[:, c_ * 256:c_ * 256 + Wk], AB[:, NB + nb, c_ * 256:c_ * 256 + 128], dcc[:, nb, k0:k0 + Wk], start=(nb == 0), stop=False)
                        mm(pb[:, c_ * 256:c_ * 256 + Wk], AB[:, NB + nb, c_ * 256 + 128:c_ * 256 + 256], dcs[:, nb, k0:k0 + Wk], start=False, stop=(nb == NCB - 1))
                fourier_out(pb, Wk, N + k0)

        DG_O = WIN_O + 16384
        diag = AV(DG_O, (CONV_K, 2, 128), BF16)
        for j in range(CONV_K):
            for c_ in range(2):
                r = vb + R_CW + j * 2 + c_
                ts("dve", diag[:, j, c_, :], ident_bf, vT[:, r:r + 1], ALU.mult)
        C_O = 51200
        zcs = _Rot([AV(C_O + i * 2048, (512,), F32) for i in range(3)])
        cens = _Rot([AV(C_O + 6144 + i * 2048, (512,), F32) for i in range(3)])
        sqs = _Rot([AV(C_O + 12288 + i * 2048, (512,), F32) for i in range(3)])
        rsd = _Rot([AV(C_O + 18432 + i * 2048, (512,), F32) for i in range(2)])
        sils = _Rot([AV(E_O + 4096 + i * 2048, (2, 512), BF16) for i in range(2)])
        segs = [(zl, t0, W, t0) for (t0, W) in lat_tiles]
        if not last:
            segs.append((zc_, 0, NC, N))
        items = []
        for si, (zz, s0, W, tok0) in enumerate(segs):
            sil = sils.next()
            for c_ in range(2):
                items.append(dict(zz=zz, s0=s0, W=W, tok0=tok0, c=c_, sil=sil))

        def cv_A(it):
            W = it["W"]; c_ = it["c"]
            pb = PS(pbr3.next())
            for j in range(CONV_K):
                mm(pb[:, 0:W], diag[:, j, c_, :], it["zz"][:, c_, it["s0"] + j:it["s0"] + j + W], start=(j == 0), stop=(j == CONV_K - 1))
            it["zcv"] = zcs.next()
            act(it["zcv"][:, 0:W], pb[:, 0:W], AF.Identity, bias=vT[:, vb + R_CB + c_: vb + R_CB + c_ + 1])

        def cv_B(it):
            W = it["W"]
            pm = PS(pbr3.next())
            mm(pm[:, 0:W], gln_f, it["zcv"][:, 0:W])
            it["cen"] = cens.next(); it["sq"] = sqs.next()
            tt("dve", it["cen"][:, 0:W], it["zcv"][:, 0:W], pm[:, 0:W], ALU.subtract)
            act(it["sq"][:, 0:W], it["cen"][:, 0:W], AF.Square)

        def cv_C(it):
            W = it["W"]; c_ = it["c"]
            pv_ = PS(pbr3.next())
            mm(pv_[:, 0:W], gln_f, it["sq"][:, 0:W])
            rs_ = rsd.next()
            act(rs_[:, 0:W], pv_[:, 0:W], AF.Ln, bias=epsT[:, 0:1])
            act(rs_[:, 0:W], rs_[:, 0:W], AF.Exp, scale=-0.5)
            zn = it["sq"]
            tt("dve", zn[:, 0:W], it["cen"][:, 0:W], rs_[:, 0:W], ALU.mult)
            act(it["sil"][:, c_, 0:W], zn[:, 0:W], AF.Silu, scale=vT[:, vb + R_LG + c_: vb + R_LG + c_ + 1], bias=vT[:, vb + R_LB + c_: vb + R_LB + c_ + 1])

        def cv_D(it):
            W = it["W"]
            for dc in range(2):
                pb = PS(pbr3.next())
                for c_ in range(2):
                    mm(pb[:, 0:W], wpw2[:, c_, dc * 128:(dc + 1) * 128], it["sil"][:, c_, 0:W], start=(c_ == 0), stop=(c_ == 1))
                cp(cpr.next(), mixT[:, 6 + dc, it["tok0"]:it["tok0"] + W], pb[:, 0:W])

        ni = len(items)
        for i in range(ni + 4):
            if i < ni:
                cv_A(items[i])
            if 0 <= i - 1 < ni:
                cv_B(items[i - 1])
            if 0 <= i - 2 < ni:
                cv_C(items[i - 2])
            if 0 <= i - 3 < ni and items[i - 3]["c"] == 1:
                cv_D(items[i - 3])

        FW_O = WIN_O
        XIN5 = FW_O + 24576
        SQ5 = XIN5 + 16384
        T5 = 176128
        rstd5 = AV(T5, (512,), F32); sqrtt5 = AV(T5 + 2048, (512,), F32)
        tmps5 = _Rot([AV(T5 + 4096, (512,), F32), AV(T5 + 6144, (512,), F32)])
        sq5 = AV(SQ5, (KD, 512), BF16)
        tiles5 = all_tiles if not last else lat_tiles
        pend5 = None
        for (t0, W) in tiles5:
            col = 1 if t0 >= N else 0
            xin = AV(XIN5, (KD, W), F32)
            for m_ in range(KD):
                dma("sp", xin[:, m_, :], xs_tile(t0, W)[:, m_, :], "xin%d" % (m_ % 2))
            if pend5 is not None:
                norm_sq(XRES[:, :, pend5[0]:pend5[0] + pend5[1]], pend5[1], sq5)
            for m_ in range(KD):
                pb = PS(pbr3.next())
                for k in range(KD):
                    mm(pb[:, 0:W], w_out[:, k, m_ * 128:(m_ + 1) * 128], mixT[:, k, t0:t0 + W], start=(k == 0), stop=(k == KD - 1))
                stt(XRES[:, m_, t0:t0 + W], pb[:, 0:W], modT[:, l, 2 * 8 + m_, col:col + 1], xin[:, m_, :], ALU.mult, ALU.add)
                if m_ == 3 and pend5 is not None:
                    pt0, pW, pcol = pend5
                    norm_rest(l, 1, 3, XRES[:, :, pt0:pt0 + pW], pW, pcol, mixT[:, :, pt0:pt0 + pW], sq5, rstd5, sqrtt5, tmps5, bank=pbr3.next())
                    pend5 = None
            pend5 = (t0, W, col)
        pt0, pW, pcol = pend5
        norm_sq(XRES[:, :, pt0:pt0 + pW], pW, sq5)
        norm_rest(l, 1, 3, XRES[:, :, pt0:pt0 + pW], pW, pcol, mixT[:, :, pt0:pt0 + pW], sq5, rstd5, sqrtt5, tmps5, bank=pbr3.next())
        h2T = mixT

        G = 4
        groups = [(g0, min(G, NFF - g0)) for g0 in range(0, NFF, G)]
        if groups[-1][1] < G:
            groups = [groups[-1]] + groups[:-1]
        fwb = [(AV(FW_O + i * 24576, (KD, 512), BF16), AV(FW_O + i * 24576 + 8192, (KD, 512), BF16), AV(FW_O + i * 24576 + 16384, (G, D), BF16)) for i in range(2)]
        ACT_O = 159744
        actb = _Rot([AV(ACT_O + i * 4096, (G, 512), BF16) for i in range(2)])
        sgs = _Rot([AV(ACT_O + 8192 + i * 2048, (512,), F32) for i in range(2)])
        upb = _Rot([0, 1, 2, 3]); dnb = _Rot([4, 5, 6, 7])
        pending = None

        def down(g0, gn, t0, W, ab, w2b, col, is_last_group=False):
            for m_ in range(KD):
                pb = PS(dnb.next())
                for jj in range(gn):
                    mm(pb[:, 0:W], w2b[:, jj, m_ * 128:(m_ + 1) * 128], ab[:, jj, 0:W], start=(jj == 0), stop=(jj == gn - 1))
                stt(XRES[:, m_, t0:t0 + W], pb[:, 0:W], modT[:, l, 5 * 8 + m_, col:col + 1], XRES[:, m_, t0:t0 + W], ALU.mult, ALU.add)

        for gi, (g0, gn) in enumerate(groups):
            w1b, w3b, w2b = fwb[gi % 2]
            pass
            cw = gn * 128
            dma("pool", w1b[:, :, 0:cw], w1_d[l, :, g0 * 128:g0 * 128 + cw].rearrange("(k p) n -> p k n", p=128), "fw1_%d" % (gi % 2))
            dma("pool", w3b[:, :, 0:cw], w3_d[l, :, g0 * 128:g0 * 128 + cw].rearrange("(k p) n -> p k n", p=128), "fw3_%d" % (gi % 2))
            dma("pool", w2b[:, 0:gn, :], w2_d[l, g0 * 128:g0 * 128 + cw, :].rearrange("(j p) n -> p j n", p=128), "fw2_%d" % (gi % 2))
            for (t0, W) in tiles5:
                col = 1 if t0 >= N else 0
                ab = actb.next()
                for jj in range(gn):
                    pa = PS(upb.next()); pg = PS(upb.next())
                    for k in range(KD):
                        mm(pa[:, 0:W], w1b[:, k, jj * 128:(jj + 1) * 128], h2T[:, k, t0:t0 + W], start=(k == 0), stop=(k == KD - 1))
                    for k in range(KD):
                        mm(pg[:, 0:W], w3b[:, k, jj * 128:(jj + 1) * 128], h2T[:, k, t0:t0 + W], start=(k == 0), stop=(k == KD - 1))
                    sg = sgs.next()
                    act(sg[:, 0:W], pa[:, 0:W], AF.Silu)
                    tt("dve", ab[:, jj, 0:W], pg[:, 0:W], sg[:, 0:W], ALU.mult)
                if pending is not None:
                    down(*pending)
                    if pending[-1] and not last:
                        pt0, pW = pending[2], pending[3]
                        dma("sp", xs_tile(pt0, pW), XRES[:, :, pt0:pt0 + pW], "xst%d" % ((pt0 // 512) % 2))
                pending = (g0, gn, t0, W, ab, w2b, col, gi == len(groups) - 1)
        down(*pending)
        if not last:
            pt0, pW = pending[2], pending[3]
            dma("sp", xs_tile(pt0, pW), XRES[:, :, pt0:pt0 + pW], "xst%d" % ((pt0 // 512) % 2))
        pending = None

    FB = 73728
    fgr = DEPTH * VROWS
    gbc = AV(FB, (D,), F32)
    junkF = AV(FB + 4096, (512,), F32)
    otl = [AV(FB + 6144 + i * 4096, (D,), F32) for i in range(3)]
    ssqF = ST("ssqF", [128, NB, 4], F32)
    dma("sp", gbc, vecs_d[fgr:fgr + 8, :].rearrange("a b -> (a b)").partition_broadcast(128), "c0")
    pbrF = _Rot(range(8))

    def fin_T(blk):
        banks = []
        for kq in range(2):
            pb = PS(pbrF.next())
            for kk in range(4):
                tr(pb[:, kk * 128:(kk + 1) * 128], XRES[:, kq * 4 + kk, blk * 128:(blk + 1) * 128], ident_f)
            act(junkF, pb[:, :], AF.Square, accum=ssqF[:, blk, kq:kq + 1])
            banks.append(pb)
        return banks

    def fin_E(blk, banks):
        tt("dve", ssqF[:, blk, 2:3], ssqF[:, blk, 0:1], ssqF[:, blk, 1:2], ALU.add)
        ts("dve", ssqF[:, blk, 2:3], ssqF[:, blk, 2:3], 1.0 / D, ALU.mult, EPS, ALU.add)
        tt("pool", ssqF[:, blk, 3:4], ssqF[:, blk, 2:3], small[:, 0:1], ALU.pow)
        ot = otl[blk % 3]
        for kq in range(2):
            stt(ot[:, kq * 512:(kq + 1) * 512], banks[kq][:, :], ssqF[:, blk, 3:4], gbc[:, kq * 512:(kq + 1) * 512], ALU.mult, ALU.mult)
        dma("sp", out_d[blk * 128:(blk + 1) * 128, :], ot, "ost%d" % (blk % 3))

    prevF = None
    for blk in range(NB):
        banks = fin_T(blk)
        if prevF is not None:
            fin_E(*prevF)
        prevF = (blk, banks)
    fin_E(*prevF)

    P.finalize(None)
    streams = []
    for o in P.ops:
        s = P._stream(o)
        if s not in streams:
            streams.append(s)
    sems = {s: es.enter_context(nc.semaphore("sem_%s_%s" % s)) for s in streams}
    out_slots = [s for s in streams if s[0] == "slot" and s[1].startswith("ost")]
    block = es.enter_context(nc.Block())

    @block.tensor
    def _(e):
        P.emit_engine("pe", e, sems)

    @block.scalar
    def _(e):
        P.emit_engine("act", e, sems)

    @block.vector
    def _(e):
        P.emit_engine("dve", e, sems)

    @block.gpsimd
    def _(e):
        P.emit_engine("pool", e, sems)

    @block.sync
    def _(e):
        P.emit_engine("sp", e, sems)
        for s in out_slots:
            e.wait_ge(sems[s], P.max_vals[s])

    es.close()
    return nc, P


def _const_tables(N, NC):
    bf = ml_dtypes.bfloat16
    rows = N // GRID_W
    t = np.arange(N)
    row = (t // GRID_W).astype(np.float64)
    colp = (t % GRID_W).astype(np.float64)
    inv_freq = ROPE_BASE ** (-np.arange(16, dtype=np.float64) / 16)
    ropec = np.zeros((128, N)); ropes = np.zeros((128, N))
    perm = np.zeros((128, 128))
    for p in range(128):
        d = p % 64
        axis, half, f = d // 32, (d % 32) // 16, d % 16
        ang = (row if axis == 0 else colp) * inv_freq[f]
        ropec[p] = np.cos(ang)
        ropes[p] = -np.sin(ang) if half == 0 else np.sin(ang)
        partner = p + 16 if half == 0 else p - 16
        perm[partner, p] = 1.0
    ident = np.eye(128)
    ones = np.ones((128, 128))
    cc = np.arange(64)
    a64 = 2 * np.pi * np.outer(cc, cc) / 64
    c64 = np.cos(a64) / 8.0; s64 = np.sin(a64) / 8.0
    c64bd = np.zeros((128, 128)); s64bd = np.zeros((128, 128))
    gln = np.zeros((128, 128))
    for g in range(2):
        c64bd[g * 64:(g + 1) * 64, g * 64:(g + 1) * 64] = c64
        s64bd[g * 64:(g + 1) * 64, g * 64:(g + 1) * 64] = s64
        gln[g * 64:(g + 1) * 64, g * 64:(g + 1) * 64] = 1.0 / 64
    cmat = np.stack([ident, ones, perm, c64bd, s64bd], axis=1).astype(bf)
    cmatf = np.stack([ident, gln], axis=1).astype(np.float32)

    NB = N // 128
    NHt = N // 2
    n_ = np.arange(NHt)
    kp = np.arange(NHt)
    ang_e = 2 * np.pi * (np.outer(n_, kp) % NHt) / NHt
    ang_o = 2 * np.pi * (np.outer(n_, 2 * kp + 1) % N) / N
    sc_ = 1.0 / np.sqrt(N)

    def lay(Mx):
        return np.ascontiguousarray(Mx.reshape(NB // 2, 128, NHt // 256, 256).transpose(2, 1, 0, 3))

    dftc_h = np.stack([lay(np.cos(ang_e) * sc_), lay(np.cos(ang_o) * sc_)], axis=0).astype(bf)
    dfts_h = np.stack([lay(-np.sin(ang_e) * sc_), lay(-np.sin(ang_o) * sc_)], axis=0).astype(bf)

    def dft(M):
        n = np.arange(M)
        ang = 2 * np.pi * (np.outer(n, n) % M) / M
        return np.cos(ang) / np.sqrt(M), -np.sin(ang) / np.sqrt(M)

    Cc, Sc = dft(NC)
    NCB = NC // 128

    def layc(Mx):
        return np.ascontiguousarray(Mx.reshape(NCB, 128, NC).transpose(1, 0, 2)).astype(bf)

    return dict(ropec=ropec.astype(bf), ropes=ropes.astype(bf), cmat=cmat, cmatf=cmatf,
                dftc=dftc_h, dfts=dfts_h, dcc=layc(Cc), dcs=layc(Sc))


def _pack_vecs(inp, DEPTH):
    vecs = np.zeros((384, 128), np.float32)
    for l in range(DEPTH):
        b = l * VROWS
        vecs[b + R_BADA:b + R_BADA + 48] = np.asarray(inp["b_ada"][l], np.float32).reshape(48, 128)
        vecs[b + R_N1:b + R_N1 + 8] = np.asarray(inp["norm1_g"][l], np.float32).reshape(8, 128)
        vecs[b + R_N2:b + R_N2 + 8] = np.asarray(inp["norm2_g"][l], np.float32).reshape(8, 128)
        vecs[b + R_CB:b + R_CB + 2] = np.asarray(inp["conv_b"][l], np.float32).reshape(2, 128)
        vecs[b + R_LG:b + R_LG + 2] = np.asarray(inp["conv_ln_g"][l], np.float32).reshape(2, 128)
        vecs[b + R_LB:b + R_LB + 2] = np.asarray(inp["conv_ln_b"][l], np.float32).reshape(2, 128)
        vecs[b + R_CW:b + R_CW + 62] = np.asarray(inp["conv_w"][l], np.float32).reshape(62, 128)
    vecs[DEPTH * VROWS:DEPTH * VROWS + 8] = np.asarray(inp["final_g"], np.float32).reshape(8, 128)
    return vecs


def make_in_maps(inp, N, NC, DEPTH, B):
    f = lambda a: np.ascontiguousarray(np.asarray(a, np.float32))
    consts = _const_tables(N, NC)
    shared = dict(consts)
    shared["vecs"] = _pack_vecs(inp, DEPTH)
    for k in ("w_ada", "w_in", "subln_g", "w_fourier", "w_conv_out", "w_out", "w_ffn1", "w_ffn3", "w_ffn2"):
        shared[k] = f(inp[k])
    shared["lamv"] = np.ascontiguousarray(np.stack([f(inp["lam_q1"]), f(inp["lam_k1"]), f(inp["lam_q2"]), f(inp["lam_k2"])], axis=1))
    x = f(inp["x"]); ctx = f(inp["ctx"]); c = f(inp["c"]); c_ctx = f(inp["c_ctx"])
    maps = []
    for b in range(B):
        m = dict(shared)
        m["x"] = x[b]; m["ctx"] = ctx[b]
        m["cc"] = np.ascontiguousarray(np.stack([c[b], c_ctx], axis=1))
        maps.append(m)
    return maps


_CACHE = {}


def kernel(**inputs):
    N, NC, DEPTH, B = 2048, 256, 2, 8
    if "nc" not in _CACHE:
        _CACHE["nc"] = build_program(N, NC, DEPTH)[0]
    nc = _CACHE["nc"]
    maps = make_in_maps(inputs, N, NC, DEPTH, B)
    res = run_bass_kernel_spmd(nc, maps, core_ids=list(range(B)))
    return np.stack([np.asarray(r["out"], np.float32) for r in res.results], axis=0)
```

```python
import numpy as np
import ml_dtypes
import concourse.bass as bass
import concourse.mybir as mybir
from concourse.bass_utils import run_bass_kernel_spmd

F32 = mybir.dt.float32
BF16 = mybir.dt.bfloat16
AF = mybir.ActivationFunctionType
ALU = mybir.AluOpType


class _Op:
    __slots__ = ("eng", "fn", "deps", "sem", "val", "signal", "slot", "group", "idx", "dbg")


def _box(ap):
    t = ap.tensor
    shp = tuple(t.shape)
    pat = ap.ap
    off = int(ap.offset)
    sp = str(ap.space)
    esz = mybir.dt.size(ap.dtype)
    if sp == "DRAM":
        lo = off * esz
        hi = (off + sum((c - 1) * abs(s) for s, c in pat) + 1) * esz - 1
        return (ap.name, 0, 0, lo, hi, False)
    pstride = 1
    for d in shp[1:]:
        pstride *= d
    plo = off // pstride
    flo = off % pstride
    assert pat[0][0] == pstride or pat[0][1] == 1, (pat, pstride)
    phi = plo + pat[0][1] - 1
    fhi = flo + sum((c - 1) * abs(s) for s, c in pat[1:])
    lo = flo * esz
    hi = (fhi + 1) * esz - 1
    if sp == "PSUM":
        lo = (lo // 2048) * 2048
        hi = (hi // 2048) * 2048 + 2047
        return (ap.name + "@" + sp, 0, 127, lo, hi, True)
    return (ap.name + "@" + sp, plo, phi, lo, hi, False)


class Prog:
    ENGS = ("pe", "act", "dve", "pool", "sp")

    def __init__(self, nc):
        self.nc = nc
        self.ops = []
        self.track = {}
        self.slot_groups = {}
        self.slot_cur = {}

    def add(self, eng, fn, reads=(), writes=(), slot=None, group=None):
        op = _Op()
        op.eng = eng
        op.fn = fn
        op.deps = set()
        op.signal = False
        op.slot = slot
        op.idx = len(self.ops)
        op.group = None
        op.dbg = None
        if DEBUG_LINES:
            import sys as _sys
            fr = _sys._getframe(2)
            op.dbg = (fr.f_lineno, fr.f_back.f_lineno if fr.f_back else None)
        if slot is not None:
            groups = self.slot_groups.setdefault(slot, [])
            if groups and self.slot_cur.get(slot) == group and group is not None:
                groups[-1].append(op.idx)
            else:
                if groups:
                    op.deps.add(groups[-1][-1])
                groups.append([op.idx])
                self.slot_cur[slot] = group
            op.group = (slot, len(groups) - 1)
        for ap in reads:
            self._access(op, ap, False)
        for ap in writes:
            self._access(op, ap, True)
        self.ops.append(op)
        return op

    def _same_stream(self, a, b):
        if a.slot is not None or b.slot is not None:
            return a.slot is not None and a.slot == b.slot
        return a.eng == b.eng

    def _access(self, op, ap, is_write):
        key, plo, phi, lo, hi, excl = _box(ap)
        is_write = is_write or excl
        lst = self.track.setdefault(key, [])
        keep = []
        for rec in lst:
            rplo, rphi, rlo, rhi, ridx, rw = rec
            overlap = not (rphi < plo or rplo > phi or rhi < lo or rlo > hi)
            if overlap and (is_write or rw) and ridx != op.idx:
                op.deps.add(ridx)
            contained = rplo >= plo and rphi <= phi and rlo >= lo and rhi <= hi
            if contained and ridx != op.idx:
                if is_write:
                    continue
                if (not rw) and self._same_stream(self.ops[ridx], op):
                    continue
            keep.append(rec)
        keep.append([plo, phi, lo, hi, op.idx, is_write])
        self.track[key] = keep

    def finalize(self, block_sems):
        ops = self.ops
        def stream(o):
            return ("slot", o.slot) if o.slot is not None else ("eng", o.eng)
        pos = {}
        cnt = {}
        for o in ops:
            s = stream(o)
            cnt[s] = cnt.get(s, 0) + 1
            pos[o.idx] = cnt[s]
        def target(j):
            o = ops[j]
            if o.slot is not None:
                g = self.slot_groups[o.slot][o.group[1]]
                return g[-1]
            return j
        waited = {e: {} for e in self.ENGS}
        need = {}
        for o in ops:
            w = waited[o.eng]
            req = {}
            for j in o.deps:
                tj = target(j)
                if tj == o.idx:
                    continue
                if tj > o.idx:
                    assert ops[tj].slot is not None and ops[tj].group == o.group, "dep on future op"
                    continue
                s = stream(ops[tj])
                if s == ("eng", o.eng) and o.slot is None and not SAME_ENGINE_SYNC:
                    continue
                if s == ("eng", o.eng) and o.eng == "pe" and o.slot is None:
                    continue
                if s == ("eng", o.eng) and o.slot is not None and ops[tj].slot is None:
                    pass
                p = pos[tj]
                if w.get(s, 0) >= p:
                    continue
                if req.get(s, (0, None))[0] < p:
                    req[s] = (p, tj)
            lst = []
            for s, (p, tj) in req.items():
                w[s] = p
                lst.append((s, tj))
                ops[tj].signal = True
            need[o.idx] = lst
        for o in ops:
            if o.slot is not None:
                o.signal = True
        val = {}
        c = {}
        for o in ops:
            s = stream(o)
            if o.signal:
                c[s] = c.get(s, 0) + (16 if o.slot is not None else 1)
            val[o.idx] = c.get(s, 0)
        self._need, self._val, self._stream = need, val, stream
        self.max_vals = dict(c)
        return need, val

    def emit(self, engines, sems):
        need, val, stream = self._need, self._val, self._stream
        for o in self.ops:
            e = engines[o.eng]
            for s, tj in need[o.idx]:
                e.wait_ge(sems[s], val[tj])
            ins = o.fn(e)
            if o.signal:
                ins.then_inc(sems[stream(o)], 16 if o.slot is not None else 1)

    def emit_engine(self, eng_name, e, sems):
        need, val, stream = self._need, self._val, self._stream
        for o in self.ops:
            if o.eng != eng_name:
                continue
            for s, tj in need[o.idx]:
                e.wait_ge(sems[s], val[tj])
            ins = o.fn(e)
            if DEBUG_LINES:
                NAMES[getattr(ins.ins, "name", None)] = (o.idx, o.dbg)
            if o.signal:
                ins.then_inc(sems[stream(o)], 16 if o.slot is not None else 1)


SAME_ENGINE_SYNC = True
DEBUG_LINES = False
NAMES = {}


D = 1024
KD = D // 128
GRID_W = 64
INW = 2304
DFF = 2816
NFF = DFF // 128
CONV_K = 31
EPS = 1e-6
ROPE_BASE = 10000.0
VROWS = 132
R_BADA, R_N1, R_N2, R_CB, R_LG, R_LB, R_CW = 0, 48, 56, 64, 66, 68, 70


class _Rot:
    def __init__(self, items):
        self.items = list(items)
        self.i = 0

    def next(self):
        v = self.items[self.i % len(self.items)]
        self.i += 1
        return v


def build_program(N, NC, DEPTH):
    T = N + NC
    NB = N // 128
    NCB = NC // 128
    TB = NB + NCB
    lat_tiles = [(t0, 512) for t0 in range(0, N, 512)]
    all_tiles = lat_tiles + [(N, NC)]
    nc = bass.Bass("TRN2", target_bir_lowering=False)
    P = Prog(nc)

    def din(name, shape, dt=F32):
        return nc.dram_tensor(name, list(shape), dt, kind="ExternalInput").ap()

    x_d = din("x", [N, D]); ctx_d = din("ctx", [NC, D]); cc_d = din("cc", [D, 2])
    w_ada_d = din("w_ada", [DEPTH, D, 6 * D]); vecs_d = din("vecs", [384, 128])
    w_in_d = din("w_in", [DEPTH, D, INW]); lamv_d = din("lamv", [DEPTH, 4, 64])
    subln_d = din("subln_g", [DEPTH, 128]); wf_d = din("w_fourier", [DEPTH, 4, 64, 64])
    wco_d = din("w_conv_out", [DEPTH, 256, 256]); w_out_d = din("w_out", [DEPTH, D, D])
    w1_d = din("w_ffn1", [DEPTH, D, DFF]); w3_d = din("w_ffn3", [DEPTH, D, DFF]); w2_d = din("w_ffn2", [DEPTH, DFF, D])
    ropec_d = din("ropec", [128, N], BF16); ropes_d = din("ropes", [128, N], BF16)
    NKC = N // 256
    dftc_d = din("dftc", [2, NKC // 2, 128, NB // 2, 256], BF16); dfts_d = din("dfts", [2, NKC // 2, 128, NB // 2, 256], BF16)
    dcc_d = din("dcc", [128, NCB, NC], BF16); dcs_d = din("dcs", [128, NCB, NC], BF16)
    cmat_d = din("cmat", [128, 5, 128], BF16)
    cmatf_d = din("cmatf", [128, 2, 128], F32)
    out_d = nc.dram_tensor("out", [N, D], F32, kind="ExternalOutput").ap()
    NT_ALL = len(all_tiles)
    xs_d = nc.dram_tensor("xs", [NT_ALL, 128, KD, 512], F32).ap()

    def xs_tile(t0, W):
        ti_ = (t0 // 512) if t0 < N else (NT_ALL - 1)
        return xs_d[ti_, :, :, 0:W]
    modrow_d = nc.dram_tensor("modrow", [DEPTH, 2, 6 * D], F32).ap()

    from contextlib import ExitStack
    es = ExitStack()
    ARENA = 184320
    arena = es.enter_context(nc.sbuf_tensor("arena", [128, ARENA // 4], F32))
    psum = es.enter_context(nc.psum_tensor("ps", [128, 8, 512], F32))

    def AV(off, shape, dt, parts=128):
        esz = mybir.dt.size(dt)
        n = 1
        for d_ in shape:
            n *= d_
        assert off % 4 == 0 and off + n * esz <= ARENA, (off, shape)
        a = arena[:, off // 4:(off + n * esz + 3) // 4]
        if dt != F32:
            a = a.bitcast(dt)
        a = a[0:parts, 0:n]
        if len(shape) == 2:
            a = a.rearrange("p (a b) -> p a b", b=shape[1])
        elif len(shape) == 3:
            a = a.rearrange("p (a b c) -> p a b c", b=shape[1], c=shape[2])
        elif len(shape) == 4:
            a = a.rearrange("p (a b c d) -> p a b c d", b=shape[1], c=shape[2], d=shape[3])
        return a

    def PS(bank, dt=F32):
        a = psum[:, bank, :]
        return a if dt == F32 else a.bitcast(dt)

    def ST(name, shape, dt):
        return es.enter_context(nc.sbuf_tensor("s_" + name, list(shape), dt))

    cmat = ST("cmat", [128, 5, 128], BF16); cmatf = ST("cmatf", [128, 2, 128], F32)
    ident_bf, ones_bf, perm_bf = cmat[:, 0, :], cmat[:, 1, :], cmat[:, 2, :]
    ident_f, gln_f = cmatf[:, 0, :], cmatf[:, 1, :]
    ropec = ST("ropec", [128, N], BF16); ropes = ST("ropes", [128, N], BF16)
    vT = ST("vT", [128, 384], F32)
    modT = ST("modT", [128, DEPTH, 48, 2], F32)
    gm = ST("gm", [128, DEPTH, 2, KD, 2], F32)
    dcc = ST("dcc", [128, NCB, NC], BF16); dcs = ST("dcs", [128, NCB, NC], BF16)
    gcol = ST("gcol", [128, DEPTH], F32)
    lamb = ST("lamb", [128, DEPTH, 4, 64], F32); lamt = ST("lamt", [128, 2, 64], F32)
    lams = ST("lams", [128, 8], F32); neglam = ST("neglam", [128, DEPTH], F32)
    sc_f = ST("sc_f", [128, KD, 2], F32); sc_bf = ST("sc_bf", [128, KD, 2], BF16)
    small = ST("small", [128, 64], F32)
    wfbd = ST("wfbd", [128, 2, 128], BF16); wfst = ST("wfst", [128, 2, 128], F32)
    wpw2 = ST("wpw2", [128, 2, 256], BF16)
    epsT = ST("epsT", [128, 1], F32)

    def mm(out, lhsT, rhs, start=True, stop=True):
        P.add("pe", lambda e: e.matmul(out, lhsT=lhsT, rhs=rhs, start=start, stop=stop), [lhsT, rhs], [out])

    def tr(out, in_, ident):
        P.add("pe", lambda e: e.transpose(out, in_, ident), [in_, ident], [out])

    def act(out, in_, func, scale=1.0, bias=None, accum=None, eng="act"):
        rd = [in_] + [a for a in (scale, bias) if not isinstance(a, (int, float, type(None)))]
        wr = [out] + ([accum] if accum is not None else [])
        kw = {}
        if bias is not None:
            kw["bias"] = bias
        if accum is not None:
            kw["accum_out"] = accum
        P.add("act", lambda e: e.activation(out=out, in_=in_, func=func, scale=scale, **kw), rd, wr)

    def tt(eng, out, a, b, op):
        P.add(eng, lambda e: e.tensor_tensor(out=out, in0=a, in1=b, op=op), [a, b], [out])

    def ts(eng, out, a, s1, op0, s2=None, op1=None):
        rd = [a] + [s for s in (s1, s2) if not isinstance(s, (int, float, type(None)))]
        if op1 is None:
            P.add(eng, lambda e: e.tensor_scalar(out=out, in0=a, scalar1=s1, scalar2=None, op0=op0), rd, [out])
        else:
            P.add(eng, lambda e: e.tensor_scalar(out=out, in0=a, scalar1=s1, scalar2=s2, op0=op0, op1=op1), rd, [out])

    def stt(out, a, s, b, op0, op1):
        rd = [a, b] + ([s] if not isinstance(s, (int, float)) else [])
        P.add("dve", lambda e: e.scalar_tensor_tensor(out=out, in0=a, scalar=s, in1=b, op0=op0, op1=op1), rd, [out])

    def cp(eng, out, in_):
        if eng == "act":
            P.add("act", lambda e: e.activation(out=out, in_=in_, func=AF.Copy), [in_], [out])
        else:
            P.add(eng, lambda e: e.tensor_copy(out=out, in_=in_), [in_], [out])

    def recip(out, in_):
        P.add("dve", lambda e: e.reciprocal(out=out, in_=in_), [in_], [out])

    def mset(eng, ap, val):
        P.add(eng, lambda e: e.memset(ap, val), [], [ap])

    def dma(eng, out, in_, slot, group=None):
        P.add(eng, lambda e: e.dma_start(out=out, in_=in_), [in_], [out], slot=slot, group=group)

    cpr = _Rot(["act", "dve"])

    dma("sp", cmat[:], cmat_d, "c0"); dma("sp", cmatf[:], cmatf_d, "c1")
    dma("sp", ropec[:], ropec_d, "c2"); dma("sp", ropes[:], ropes_d, "c3")
    dma("sp", dcc[:], dcc_d, "c0"); dma("sp", dcs[:], dcs_d, "c1")
    mset("dve", epsT[:], EPS)
    mset("dve", small[:, 0:8], -0.5)
    vst = AV(0, (3, 128), F32)
    dma("sp", vst, vecs_d.rearrange("(j p) f -> p j f", p=128), "c2")
    for j in range(3):
        tr(PS(0)[:, j * 128:(j + 1) * 128], vst[:, j, :], ident_f)
    cp("dve", vT[:], PS(0)[:, 0:384])
    for l in range(DEPTH):
        dma("sp", lamb[:, l], lamv_d[l].partition_broadcast(128), "c3")
        dma("sp", gcol[:, l:l + 1], subln_d[l].rearrange("(p o) -> p o", o=1), "c0")
        lam_init = 0.8 - 0.6 * float(np.exp(-0.3 * l))
        tt("dve", lamt[:, 0, :], lamb[:, l, 0, :], lamb[:, l, 1, :], ALU.mult)
        tt("dve", lamt[:, 1, :], lamb[:, l, 2, :], lamb[:, l, 3, :], ALU.mult)
        P.add("dve", lambda e: e.tensor_reduce(out=lams[:, 0:2], in_=lamt[:], axis=mybir.AxisListType.X, op=ALU.add), [lamt[:]], [lams[:, 0:2]])
        act(lams[:, 2:4], lams[:, 0:2], AF.Exp)
        tt("dve", lams[:, 4:5], lams[:, 3:4], lams[:, 2:3], ALU.subtract)
        ts("dve", neglam[:, l:l + 1], lams[:, 4:5], -lam_init, ALU.add)
        ts("dve", gcol[:, l:l + 1], gcol[:, l:l + 1], 1.0 - lam_init, ALU.mult)
    dma("sp", sc_f[:], cc_d.rearrange("(k p) t -> p k t", p=128), "c1")
    act(sc_bf[:], sc_f[:], AF.Silu)
    WA_O = 64800
    wa_bufs = [AV(WA_O + i * 4096, (KD, 256), BF16) for i in range(2)]
    mstage_t = ST("mstage", [2, 256], F32)
    mstage = mstage_t[:, :]
    mrow_t = ST("mrow", [96, 128], F32)
    mrow = mrow_t[:, :]
    pbr = _Rot(range(1, 8))
    all_pieces = [(0, pc) for pc in range(24)]
    for l_ in range(1, DEPTH):
        all_pieces += [(l_, pc) for pc in range(24)]
    mstate = {"dma": 0, "mm": 0}

    def mods_dma():
        i = mstate["dma"]
        if i >= len(all_pieces):
            return
        l_, pc = all_pieces[i]
        dma("pool", wa_bufs[i % 2], w_ada_d[l_, :, pc * 256:(pc + 1) * 256].rearrange("(k p) n -> p k n", p=128), "wa%d" % (i % 2))
        mstate["dma"] = i + 1

    def mods_piece(bank):
        i = mstate["mm"]
        while mstate["dma"] <= min(i + 1, len(all_pieces) - 1):
            mods_dma()
        l_, pc = all_pieces[i]
        wa = wa_bufs[i % 2]
        pb = PS(bank)
        for k in range(KD):
            mm(pb[0:2, 0:256], sc_bf[:, k, :], wa[:, k, :], start=(k == 0), stop=(k == KD - 1))
        cp("dve", mstage, pb[0:2, 0:256])
        dma("sp", modrow_d[l_, :, pc * 256:(pc + 1) * 256], mstage, "mst")
        mstate["mm"] = i + 1

    def mods_finish(l, s0, s1, bank):
        r0, r1 = s0 * 8, s1 * 8
        nr = r1 - r0
        for v in range(2):
            dma("sp", mrow[v * nr:(v + 1) * nr, :], modrow_d[l, v, r0 * 128:r1 * 128].rearrange("(r p) -> r p", p=128), "mld", group="mf%d_%d" % (l, s0))
        pb = PS(bank)
        tr(pb[:, 0:2 * nr], mrow[0:2 * nr, :], ident_f[0:2 * nr, 0:2 * nr])
        for v in range(2):
            tt("dve", modT[:, l, r0:r1, v], pb[:, v * nr:(v + 1) * nr], vT[:, l * VROWS + R_BADA + r0: l * VROWS + R_BADA + r1], ALU.add)
        for n_i, (sec_sc, rg) in enumerate(((1, R_N1), (4, R_N2))):
            if not (s0 <= sec_sc < s1):
                continue
            for k in range(KD):
                ts("dve", gm[:, l, n_i, k, :], modT[:, l, sec_sc * 8 + k, :], 1.0, ALU.add)
                ts("dve", gm[:, l, n_i, k, :], gm[:, l, n_i, k, :], vT[:, l * VROWS + rg + k: l * VROWS + rg + k + 1], ALU.mult)

    def load_w_in(l_):
        w_in_ = AV(110592, (KD, INW), BF16)
        if l_ == 0:
            order = [(hf, k) for hf in range(2) for k in range(KD)]
        else:
            order = [(hf, k) for hf in range(2) for k in range(5)] + [(hf, k) for hf in range(2) for k in range(5, KD)]
        for (hf, k) in order:
            dma("pool", w_in_[:, k, hf * 1152:(hf + 1) * 1152], w_in_d[l_, k * 128:(k + 1) * 128, hf * 1152:(hf + 1) * 1152],
                "win%d" % ((hf if l_ == 0 else (0 if k < 5 else 1))), group="l%d" % l_)

    load_w_in(0)

    def bg_mods(max_layer, bank):
        i = mstate["mm"]
        if i < len(all_pieces) and all_pieces[i][0] <= max_layer:
            mods_piece(bank)
            if i == 23:
                mods_finish(0, 2, 6, bank)
            elif i > 23 and (i + 1) % 24 == 0:
                mods_finish(all_pieces[i][0], 0, 6, bank)
            return True
        return False

    XIN = 90112
    xtoks = [AV(16384, (4, D), F32), AV(73728, (4, D), F32)]
    xins = [AV(XIN, (KD, 512), F32), AV(32768, (KD, 512), F32)]

    def x_load(ti):
        t0, W = all_tiles[ti]
        src = x_d[t0:t0 + W, :] if t0 < N else ctx_d[:, :]
        dma("sp", xtoks[ti % 2][:, 0:W // 128, :], src.rearrange("(s p) d -> p s d", p=128), "xl%d" % (ti % 2))

    x_load(0)
    mods_done = 0
    for ti, (t0, W) in enumerate(all_tiles):
        nsub = W // 128
        if ti + 1 < len(all_tiles):
            x_load(ti + 1)
        xtok = xtoks[ti % 2]
        xin = xins[ti % 2][:, :, 0:W]
        for k in range(KD):
            pb = PS(pbr.next())
            for s_ in range(nsub):
                tr(pb[:, s_ * 128:(s_ + 1) * 128], xtok[:, s_, k * 128:(k + 1) * 128], ident_f)
            cp(cpr.next(), xin[:, k, :], pb[:, 0:W])
        dma("sp", xs_tile(t0, W), xin, "xst%d" % ((t0 // 512) % 2))
        for _ in range(2):
            if mods_done < 8:
                mods_piece(pbr.next()); mods_done += 1
    while mods_done < 8:
        mods_piece(pbr.next()); mods_done += 1
    mods_finish(0, 0, 2, pbr.next())

    QT_O, KT_O, V_O, UF_O = 0, 18432, 36864, 55584
    MIX_O = 73728
    WIN_O = 110592
    Z_O = 147456
    QT = AV(QT_O, (4, T), BF16); KT = AV(KT_O, (4, T), BF16)
    V = AV(V_O, (TB, 4, 130), BF16)
    ufT = AV(UF_O, (2, T), BF16)
    mixT = AV(MIX_O, (KD, T), BF16)
    zl = AV(Z_O, (2, N + 30), BF16)
    zc_ = AV(Z_O + 2 * (N + 30) * 2, (2, NC + 30), BF16)
    XRES = AV(0, (KD, T), F32)

    def norm_sq(xin, W, sq):
        act(sq[:, :, 0:W], xin, AF.Square)

    def norm_rest(l, n_i, sec_sh, xin, W, col, hT, sq, rstd, sqrtt, tmps, bank=None):
        pb = PS(pbr.next() if bank is None else bank)
        for k in range(KD):
            mm(pb[:, 0:W], ones_bf, sq[:, k, 0:W], start=(k == 0), stop=(k == KD - 1))
        act(sqrtt[:, 0:W], pb[:, 0:W], AF.Ln, scale=1.0 / D, bias=epsT[:, 0:1])
        act(rstd[:, 0:W], sqrtt[:, 0:W], AF.Exp, scale=-0.5)
        for k in range(KD):
            tmp = tmps.next()
            stt(tmp[:, 0:W], xin[:, k, :], gm[:, l, n_i, k, col:col + 1], rstd[:, 0:W], ALU.mult, ALU.mult)
            act(hT[:, k, 0:W], tmp[:, 0:W], AF.Identity, bias=modT[:, l, sec_sh * 8 + k, col:col + 1])

    def norm_tile(l, n_i, sec_sh, xin, W, col, hT, sq, rstd, sqrtt, tmps):
        norm_sq(xin, W, sq)
        norm_rest(l, n_i, sec_sh, xin, W, col, hT, sq, rstd, sqrtt, tmps)

    for l in range(DEPTH):
        last = (l == DEPTH - 1)
        vb = l * VROWS
        w_in = AV(WIN_O, (KD, INW), BF16)
        if l > 0:
            load_w_in(l)
        mset("pool", wfst[:], 0.0)
        for g in range(4):
            c_, o_ = g // 2, (g % 2) * 64
            dma("sp", wfst[o_:o_ + 64, c_, o_:o_ + 64], wf_d[l, g], "c2", group="wf%d" % l)
        cp("dve", wfbd[:], wfst[:])
        dma("pool", wpw2[:], wco_d[l].rearrange("(c p) d -> p c d", p=128), "wpw")
        for zz, W_ in ((zl, N), (zc_, NC)):
            mset("pool", zz[:, :, 0:15], 0.0)
            mset("pool", zz[:, :, W_ + 15:W_ + 30], 0.0)
        sq = AV(MIX_O + 16384, (KD, 512), BF16)
        hTs = [AV(MIX_O + 24576, (KD, 512), BF16), AV(Z_O + 9472, (KD, 512), BF16)]
        rstd = AV(MIX_O + 32768, (512,), F32); sqrtt = AV(MIX_O + 34816, (512,), F32)
        TO = Z_O + 9472 + 8192
        tmpsA = _Rot([AV(TO, (512,), F32), AV(TO + 2048, (512,), F32)])
        qraws = _Rot([AV(TO + 4096, (512,), BF16), AV(TO + 5120, (512,), BF16)])
        t1s = _Rot([AV(TO + 6144, (512,), F32), AV(TO + 8192, (512,), F32)])
        t2s = _Rot([AV(TO + 10240, (512,), F32), AV(TO + 12288, (512,), F32)])
        sigs = _Rot([AV(TO + 14336, (512,), F32), AV(TO + 16384, (512,), F32)])
        def p1_load(ti):
            t0, W = all_tiles[ti]
            xin = AV(MIX_O, (KD, W), F32)
            dma("sp", xin, xs_tile(t0, W), "xin")

        def p1_norm(ti):
            t0, W = all_tiles[ti]
            col = 1 if t0 >= N else 0
            xin = AV(MIX_O, (KD, W), F32)
            norm_tile(l, 0, 0, xin, W, col, hTs[ti % 2], sq, rstd, sqrtt, tmpsA)

        def p1_norm_sq(ti):
            t0, W = all_tiles[ti]
            norm_sq(AV(MIX_O, (KD, W), F32), W, sq)

        def p1_norm_rest(ti):
            t0, W = all_tiles[ti]
            col = 1 if t0 >= N else 0
            norm_rest(l, 0, 0, AV(MIX_O, (KD, W), F32), W, col, hTs[ti % 2], sq, rstd, sqrtt, tmpsA)

        def rope_tail(qraw, t1, dst, t0, W):
            t2 = t2s.next()
            pb2 = PS(pbr.next())
            mm(pb2[:, 0:W], perm_bf, qraw[:, 0:W])
            tt("dve", t2[:, 0:W], pb2[:, 0:W], ropes[:, t0:t0 + W], ALU.mult)
            tt("pool", dst, t1[:, 0:W], t2[:, 0:W], ALU.add)

        nt_ = len(all_tiles)
        p1_load(0); p1_norm(0)
        if nt_ > 1:
            p1_load(1)
        for ti, (t0, W) in enumerate(all_tiles):
            is_ctx = t0 >= N
            col = 1 if is_ctx else 0
            hT = hTs[ti % 2]
            pend_rope = None
            if ti + 1 < nt_:
                p1_norm_sq(ti + 1)
            ctx_kv_only = is_ctx and last
            for ch in range(8):
                if ctx_kv_only and ch < 4:
                    continue
                pb = PS(pbr.next())
                for k in range(KD):
                    mm(pb[:, 0:W], w_in[:, k, ch * 128:(ch + 1) * 128], hT[:, k, 0:W], start=(k == 0), stop=(k == KD - 1))
                dst = (QT if ch < 4 else KT)[:, ch % 4, t0:t0 + W]
                if is_ctx:
                    cp(cpr.next(), dst, pb[:, 0:W])
                else:
                    qraw = qraws.next(); t1 = t1s.next()
                    cp("act", qraw[:, 0:W], pb[:, 0:W])
                    tt("dve", t1[:, 0:W], pb[:, 0:W], ropec[:, t0:t0 + W], ALU.mult)
                    if pend_rope is not None:
                        rope_tail(*pend_rope)
                    pend_rope = (qraw, t1, dst, t0, W)
            bg_mods(l, pbr.next())
            if ti + 1 < nt_:
                p1_norm_rest(ti + 1)
                if ti + 2 < nt_:
                    p1_load(ti + 2)
            for s_ in range(W // 128):
                pb = PS(pbr.next())
                for k in range(KD):
                    mm(pb[:, :], hT[:, k, s_ * 128:(s_ + 1) * 128], w_in[:, k, 1024:1536], start=(k == 0), stop=(k == KD - 1))
                tb = (t0 // 128) + s_
                cp(cpr.next(), V[:, tb, :, 0:128], pb[:, :].rearrange("p (h d) -> p h d", d=128))
                if s_ == 0 and pend_rope is not None:
                    rope_tail(*pend_rope)
                    pend_rope = None
            if ctx_kv_only:
                bg_mods(l, pbr.next())
                continue
            for c_ in range(2):
                pb = PS(pbr.next())
                for k in range(KD):
                    mm(pb[:, 0:W], w_in[:, k, 1536 + c_ * 128:1536 + (c_ + 1) * 128], hT[:, k, 0:W], start=(k == 0), stop=(k == KD - 1))
                cp(cpr.next(), ufT[:, c_, t0:t0 + W], pb[:, 0:W])
            for c_ in range(2):
                pa = PS(pbr.next()); pg = PS(pbr.next())
                for k in range(KD):
                    mm(pa[:, 0:W], w_in[:, k, 1792 + c_ * 128:1792 + (c_ + 1) * 128], hT[:, k, 0:W], start=(k == 0), stop=(k == KD - 1))
                for k in range(KD):
                    mm(pg[:, 0:W], w_in[:, k, 2048 + c_ * 128:2048 + (c_ + 1) * 128], hT[:, k, 0:W], start=(k == 0), stop=(k == KD - 1))
                sg = sigs.next()
                act(sg[:, 0:W], pg[:, 0:W], AF.Sigmoid)
                zdst = zc_[:, c_, 15:15 + W] if is_ctx else zl[:, c_, 15 + t0:15 + t0 + W]
                tt("dve", zdst, pa[:, 0:W], sg[:, 0:W], ALU.mult)
            bg_mods(l, pbr.next())

        WOUT_O = 159744
        w_out = AV(WOUT_O, (KD, D), BF16)
        for k in range(KD):
            dma("pool", w_out[:, k, :], w_out_d[l, k * 128:(k + 1) * 128, :], "wout%d" % (k % 2), group="l%d" % l)

        E_O = WIN_O
        QT_W = 256
        Ebufs = _Rot([AV(E_O + 24576 + i * 1024, (512,), BF16) for i in range(4)])
        rrs = _Rot([AV(E_O + 3072 + i * 2048, (512,), F32) for i in range(2)])
        t12s = _Rot([AV(E_O + 7168 + i * 2048, (512,), F32) for i in range(2)])
        ofs = _Rot([AV(E_O + 11264 + i * 1024, (256,), F32) for i in range(3)])
        sqb = _Rot([AV(E_O + 14336 + i * 512, (256,), BF16) for i in range(3)])
        lnb = _Rot([AV(E_O + 15872 + i * 1024, (256,), F32) for i in range(2)])
        rsb = _Rot([AV(E_O + 17920 + i * 1024, (256,), F32) for i in range(2)])
        sbr = _Rot([0, 1, 6])
        ssr = _Rot([7])
        qsets = []
        for q0 in range(0, N, QT_W):
            qsets.append((q0, QT_W, list(range(TB))))
        if not last:
            for q0 in range(0, NC, QT_W):
                qsets.append((N + q0, min(QT_W, NC - q0), list(range(NB, TB))))

        Qz = [[AV(E_O + 20480 + (hpar * 2 + tp_) * 1024, (2, 256), BF16) for tp_ in range(2)] for hpar in range(2)]
        for hpar in range(2):
            for tp_ in range(2):
                mset("pool", Qz[hpar][tp_][:, :, :], 0.0)

        def att_Q(h, q0, QW, tpar):
            hp = slice((h % 2) * 64, (h % 2) * 64 + 64)
            c1, c2 = h // 2, 2 + h // 2
            qz = Qz[h % 2][tpar]
            cp("pool", qz[hp, 0, 0:QW], QT[hp, c1, q0:q0 + QW])
            cp("pool", qz[hp, 1, 0:QW], QT[hp, c2, q0:q0 + QW])

        def att_S(h, q0, QW, kb, tpar):
            c1, c2 = h // 2, 2 + h // 2
            qz = Qz[h % 2][tpar]
            sb = PS(sbr.next())
            mm(sb[:, 0:QW], KT[:, c1, kb * 128:(kb + 1) * 128], qz[:, 0, 0:QW])
            mm(sb[:, 256:256 + QW], KT[:, c2, kb * 128:(kb + 1) * 128], qz[:, 1, 0:QW])
            E = Ebufs.next()
            if QW == 256:
                act(E[:, 0:512], sb[:, 0:512], AF.Exp, scale=0.125)
            else:
                act(E[:, :].rearrange("p (m q) -> p m q", m=2)[:, :, 0:QW], sb[:, :].rearrange("p (m q) -> p m q", m=2)[:, :, 0:QW], AF.Exp, scale=0.125)
            return E

        def att_PV(h, QW, ki, nk, kb, E, par):
            ob = PS(2 + par); sk = PS(4 + par)
            f, l_ = (ki == 0), (ki == nk - 1)
            mm(ob[:, 0:QW], V[:, kb, h, 0:128], E[:, 0:QW], start=f, stop=False)
            mm(ob[:, 256:256 + QW], V[:, kb, h, 0:128], E[:, 256:256 + QW], start=False, stop=l_)
            mm(sk[:, 0:QW], ones_bf, E[:, 0:QW], start=f, stop=False)
            mm(sk[:, 256:256 + QW], ones_bf, E[:, 256:256 + QW], start=False, stop=l_)

        def att_fin_a(h, q0, QW, par):
            ob = PS(2 + par); sk = PS(4 + par)
            rr = rrs.next(); t12 = t12s.next(); of = ofs.next(); sq_ = sqb.next()
            if QW == 256:
                recip(rr[:, 0:512], sk[:, 0:512])
                tt("dve", t12[:, 0:512], ob[:, 0:512], rr[:, 0:512], ALU.mult)
            else:
                v3 = lambda a_: a_[:, :].rearrange("p (m q) -> p m q", m=2)[:, :, 0:QW]
                recip(v3(rr), v3(sk))
                tt("dve", v3(t12), v3(ob), v3(rr), ALU.mult)
            stt(of[:, 0:QW], t12[:, 256:256 + QW], neglam[:, l:l + 1], t12[:, 0:QW], ALU.mult, ALU.add)
            tt("dve", sq_[:, 0:QW], of[:, 0:QW], of[:, 0:QW], ALU.mult)
            return (h, q0, QW, of, sq_)

        def att_fin_b(h, q0, QW, of, sq_):
            ssb = PS(ssr.next())
            mm(ssb[:, 0:QW], ones_bf, sq_[:, 0:QW])
            ln_ = lnb.next(); rs_ = rsb.next()
            act(ln_[:, 0:QW], ssb[:, 0:QW], AF.Ln, scale=1.0 / 128, bias=epsT[:, 0:1])
            act(rs_[:, 0:QW], ln_[:, 0:QW], AF.Exp, scale=-0.5)
            stt(mixT[:, h, q0:q0 + QW], of[:, 0:QW], gcol[:, l:l + 1], rs_[:, 0:QW], ALU.mult, ALU.mult)

        stages = []
        tiles_ = []
        tno = 0
        for h in range(4):
            for (q0, QW, kbs) in qsets:
                tiles_.append((h, q0, QW, tno))
                for ki, kb in enumerate(kbs):
                    stages.append((h, q0, QW, ki, len(kbs), kb, tno))
                tno += 1
        hcnt = [0, 0]
        tpar_of = {}
        for (h, q0, QW, tn) in tiles_:
            tpar_of[tn] = hcnt[h % 2] % 2
            hcnt[h % 2] += 1
        att_Q(tiles_[0][0], tiles_[0][1], tiles_[0][2], tpar_of[0])
        LA = 2
        fifo = []
        pend = []

        def do_pv(it):
            ph, pq0, pQW, pki, pnk, pkb, ppar, pE = it
            att_PV(ph, pQW, pki, pnk, pkb, pE, ppar)
            if pki == pnk - 1:
                pend.append([10, att_fin_a(ph, pq0, pQW, ppar)])

        for st in stages:
            h, q0, QW, ki, nk, kb, tn = st
            par = tn % 2
            if ki == 0 and tn + 1 < len(tiles_):
                nh, nq0, nQW, ntn = tiles_[tn + 1]
                att_Q(nh, nq0, nQW, tpar_of[ntn])
            if ki == 4:
                bg_mods(l + 1, 7)
            E = att_S(h, q0, QW, kb, tpar_of[tn])
            fifo.append((h, q0, QW, ki, nk, kb, par, E))
            if len(fifo) > LA:
                do_pv(fifo.pop(0))
            for it in pend:
                it[0] -= 1
            while pend and pend[0][0] <= 0:
                att_fin_b(*pend.pop(0)[1])
        while fifo:
            do_pv(fifo.pop(0))

        while bg_mods(l + 1, 7):
            pass

        AB = AV(0, (TB, 512), BF16)
        DFB = [(AV(18432 + i * 8192, (NB // 2, 256), BF16), AV(18432 + i * 8192 + 4096, (NB // 2, 256), BF16)) for i in range(2)]
        FTs = _Rot([AV(E_O + i * 1024, (2, 256), BF16) for i in range(2)])
        pbr3 = _Rot(range(8))
        NH = NB // 2
        ABp = AV(34816, (NH, 512), BF16); ABm = AV(34816 + NH * 1024, (NH, 512), BF16)

        def stage_a(tb):
            pb = PS(pbr3.next())
            for c_ in range(2):
                mm(pb[:, c_ * 256:(c_ + 1) * 256], ufT[:, c_, tb * 128:(tb + 1) * 128], cmat[:, 3:5, :].rearrange("p a b -> p (a b)"))
            cp(cpr.next(), AB[:, tb, :], pb[:, :])

        for i in range(NH):
            stage_a(i); stage_a(i + NH)
            tt("dve", ABp[:, i, :], AB[:, i, :], AB[:, i + NH, :], ALU.add)
            tt("pool", ABm[:, i, :], AB[:, i, :], AB[:, i + NH, :], ALU.subtract)
        if not last:
            for tb in range(NB, TB):
                stage_a(tb)
        while pend:
            att_fin_b(*pend.pop(0)[1])

        pend_f = None

        def fourier_out1(pb, Wk, tok0):
            FT = FTs.next()
            cp("act", FT[:, :, 0:Wk], pb[:, :].rearrange("p (c k) -> p c k", c=2)[:, :, 0:Wk])
            return (FT, Wk, tok0)

        def fourier_out(pb, Wk, tok0):
            fourier_out2(*fourier_out1(pb, Wk, tok0))

        def fourier_out2(FT, Wk, tok0):
            pb2 = PS(pbr3.next())
            for c_ in range(2):
                mm(pb2[:, c_ * 256:c_ * 256 + Wk], wfbd[:, c_, :], FT[:, c_, 0:Wk])
            dst = tok0 if not isinstance(tok0, int) else mixT[:, 4:6, tok0:tok0 + Wk]
            cp("dve", dst, pb2[:, :].rearrange("p (c k) -> p c k", c=2)[:, :, 0:Wk])

        step = 0
        for par, ABx in ((0, ABp), (1, ABm)):
            for kc in range(NKC // 2):
                Cb, Sb = DFB[step % 2]
                dma("sp", Cb, dftc_d[par, kc], "dfc%d" % (step % 2)); dma("sp", Sb, dfts_d[par, kc], "dfs%d" % (step % 2))
                step += 1
                pb = PS(pbr3.next())
                for c_ in range(2):
                    for nb in range(NH):
                        mm(pb[:, c_ * 256:(c_ + 1) * 256], ABx[:, nb, c_ * 256:c_ * 256 + 128], Cb[:, nb, :], start=(nb == 0), stop=False)
                        mm(pb[:, c_ * 256:(c_ + 1) * 256], ABx[:, nb, c_ * 256 + 128:c_ * 256 + 256], Sb[:, nb, :], start=False, stop=(nb == NH - 1))
                if pend_f is not None:
                    fourier_out2(*pend_f)
                dst = mixT[:, 4:6, kc * 512:(kc + 1) * 512].rearrange("p c (j two) -> p c j two", two=2)[:, :, :, par]
                pend_f = fourier_out1(pb, 256, dst)
        if pend_f is not None:
            fourier_out2(*pend_f)
            pend_f = None
        if not last:
            for k0 in range(0, NC, 256):
                Wk = min(256, NC - k0)
                pb = PS(pbr3.next())
                for c_ in range(2):
                    for nb in range(NCB):
                        mm(pb[:, c_ * 256:c_ * 256 + Wk], AB[:, NB + nb, c_ * 256:c_ * 256 + 128], dcc[:, nb, k0:k0 + Wk], start=(nb == 0), stop=False)
                        mm(pb[:, c_ * 256:c_ * 256 + Wk], AB[:, NB + nb, c_ * 256 + 128:c_ * 256 + 256], dcs[:, nb, k0:k0 + Wk], start=False, stop=(nb == NCB - 1))
                fourier_out(pb, Wk, N + k0)

        DG_O = WIN_O + 16384
        diag = AV(DG_O, (CONV_K, 2, 128), BF16)
        for j in range(CONV_K):
            for c_ in range(2):
                r = vb + R_CW + j * 2 + c_
                ts("dve", diag[:, j, c_, :], ident_bf, vT[:, r:r + 1], ALU.mult)
        C_O = 51200
        zcs = _Rot([AV(C_O + i * 2048, (512,), F32) for i in range(3)])
        cens = _Rot([AV(C_O + 6144 + i * 2048, (512,), F32) for i in range(3)])
        sqs = _Rot([AV(C_O + 12288 + i * 2048, (512,), F32) for i in range(3)])
        rsd = _Rot([AV(C_O + 18432 + i * 2048, (512,), F32) for i in range(2)])
        sils = _Rot([AV(E_O + 4096 + i * 2048, (2, 512), BF16) for i in range(2)])
        segs = [(zl, t0, W, t0) for (t0, W) in lat_tiles]
        if not last:
            segs.append((zc_, 0, NC, N))
        items = []
        for si, (zz, s0, W, tok0) in enumerate(segs):
            sil = sils.next()
            for c_ in range(2):
                items.append(dict(zz=zz, s0=s0, W=W, tok0=tok0, c=c_, sil=sil))

        def cv_A(it):
            W = it["W"]; c_ = it["c"]
            pb = PS(pbr3.next())
            for j in range(CONV_K):
                mm(pb[:, 0:W], diag[:, j, c_, :], it["zz"][:, c_, it["s0"] + j:it["s0"] + j + W], start=(j == 0), stop=(j == CONV_K - 1))
            it["zcv"] = zcs.next()
            act(it["zcv"][:, 0:W], pb[:, 0:W], AF.Identity, bias=vT[:, vb + R_CB + c_: vb + R_CB + c_ + 1])

        def cv_B(it):
            W = it["W"]
            pm = PS(pbr3.next())
            mm(pm[:, 0:W], gln_f, it["zcv"][:, 0:W])
            it["cen"] = cens.next(); it["sq"] = sqs.next()
            tt("dve", it["cen"][:, 0:W], it["zcv"][:, 0:W], pm[:, 0:W], ALU.subtract)
            act(it["sq"][:, 0:W], it["cen"][:, 0:W], AF.Square)

        def cv_C(it):
            W = it["W"]; c_ = it["c"]
            pv_ = PS(pbr3.next())
            mm(pv_[:, 0:W], gln_f, it["sq"][:, 0:W])
            rs_ = rsd.next()
            act(rs_[:, 0:W], pv_[:, 0:W], AF.Ln, bias=epsT[:, 0:1])
            act(rs_[:, 0:W], rs_[:, 0:W], AF.Exp, scale=-0.5)
            zn = it["sq"]
            tt("dve", zn[:, 0:W], it["cen"][:, 0:W], rs_[:, 0:W], ALU.mult)
            act(it["sil"][:, c_, 0:W], zn[:, 0:W], AF.Silu, scale=vT[:, vb + R_LG + c_: vb + R_LG + c_ + 1], bias=vT[:, vb + R_LB + c_: vb + R_LB + c_ + 1])

        def cv_D(it):
            W = it["W"]
            for dc in range(2):
                pb = PS(pbr3.next())
                for c_ in range(2):
                    mm(pb[:, 0:W], wpw2[:, c_, dc * 128:(dc + 1) * 128], it["sil"][:, c_, 0:W], start=(c_ == 0), stop=(c_ == 1))
                cp(cpr.next(), mixT[:, 6 + dc, it["tok0"]:it["tok0"] + W], pb[:, 0:W])

        ni = len(items)
        for i in range(ni + 4):
            if i < ni:
                cv_A(items[i])
            if 0 <= i - 1 < ni:
                cv_B(items[i - 1])
            if 0 <= i - 2 < ni:
                cv_C(items[i - 2])
            if 0 <= i - 3 < ni and items[i - 3]["c"] == 1:
                cv_D(items[i - 3])

        FW_O = WIN_O
        XIN5 = FW_O + 24576
        SQ5 = XIN5 + 16384
        T5 = 176128
        rstd5 = AV(T5, (512,), F32); sqrtt5 = AV(T5 + 2048, (512,), F32)
        tmps5 = _Rot([AV(T5 + 4096, (512,), F32), AV(T5 + 6144, (512,), F32)])
        sq5 = AV(SQ5, (KD, 512), BF16)
        tiles5 = all_tiles if not last else lat_tiles
        pend5 = None
        for (t0, W) in tiles5:
            col = 1 if t0 >= N else 0
            xin = AV(XIN5, (KD, W), F32)
            for m_ in range(KD):
                dma("sp", xin[:, m_, :], xs_tile(t0, W)[:, m_, :], "xin%d" % (m_ % 2))
            if pend5 is not None:
                norm_sq(XRES[:, :, pend5[0]:pend5[0] + pend5[1]], pend5[1], sq5)
            for m_ in range(KD):
                pb = PS(pbr3.next())
                for k in range(KD):
                    mm(pb[:, 0:W], w_out[:, k, m_ * 128:(m_ + 1) * 128], mixT[:, k, t0:t0 + W], start=(k == 0), stop=(k == KD - 1))
                stt(XRES[:, m_, t0:t0 + W], pb[:, 0:W], modT[:, l, 2 * 8 + m_, col:col + 1], xin[:, m_, :], ALU.mult, ALU.add)
                if m_ == 3 and pend5 is not None:
                    pt0, pW, pcol = pend5
                    norm_rest(l, 1, 3, XRES[:, :, pt0:pt0 + pW], pW, pcol, mixT[:, :, pt0:pt0 + pW], sq5, rstd5, sqrtt5, tmps5, bank=pbr3.next())
                    pend5 = None
            pend5 = (t0, W, col)
        pt0, pW, pcol = pend5
        norm_sq(XRES[:, :, pt0:pt0 + pW], pW, sq5)
        norm_rest(l, 1, 3, XRES[:, :, pt0:pt0 + pW], pW, pcol, mixT[:, :, pt0:pt0 + pW], sq5, rstd5, sqrtt5, tmps5, bank=pbr3.next())
        h2T = mixT

        G = 4
        groups = [(g0, min(G, NFF - g0)) for g0 in range(0, NFF, G)]
        if groups[-1][1] < G:
            groups = [groups[-1]] + groups[:-1]
        fwb = [(AV(FW_O + i * 24576, (KD, 512), BF16), AV(FW_O + i * 24576 + 8192, (KD, 512), BF16), AV(FW_O + i * 24576 + 16384, (G, D), BF16)) for i in range(2)]
        ACT_O = 159744
        actb = _Rot([AV(ACT_O + i * 4096, (G, 512), BF16) for i in range(2)])
        sgs = _Rot([AV(ACT_O + 8192 + i * 2048, (512,), F32) for i in range(2)])
        upb = _Rot([0, 1, 2, 3]); dnb = _Rot([4, 5, 6, 7])
        pending = None

        def down(g0, gn, t0, W, ab, w2b, col, is_last_group=False):
            for m_ in range(KD):
                pb = PS(dnb.next())
                for jj in range(gn):
                    mm(pb[:, 0:W], w2b[:, jj, m_ * 128:(m_ + 1) * 128], ab[:, jj, 0:W], start=(jj == 0), stop=(jj == gn - 1))
                stt(XRES[:, m_, t0:t0 + W], pb[:, 0:W], modT[:, l, 5 * 8 + m_, col:col + 1], XRES[:, m_, t0:t0 + W], ALU.mult, ALU.add)

        for gi, (g0, gn) in enumerate(groups):
            w1b, w3b, w2b = fwb[gi % 2]
            pass
            cw = gn * 128
            dma("pool", w1b[:, :, 0:cw], w1_d[l, :, g0 * 128:g0 * 128 + cw].rearrange("(k p) n -> p k n", p=128), "fw1_%d" % (gi % 2))
            dma("pool", w3b[:, :, 0:cw], w3_d[l, :, g0 * 128:g0 * 128 + cw].rearrange("(k p) n -> p k n", p=128), "fw3_%d" % (gi % 2))
            dma("pool", w2b[:, 0:gn, :], w2_d[l, g0 * 128:g0 * 128 + cw, :].rearrange("(j p) n -> p j n", p=128), "fw2_%d" % (gi % 2))
            for (t0, W) in tiles5:
                col = 1 if t0 >= N else 0
                ab = actb.next()
                for jj in range(gn):
                    pa = PS(upb.next()); pg = PS(upb.next())
                    for k in range(KD):
                        mm(pa[:, 0:W], w1b[:, k, jj * 128:(jj + 1) * 128], h2T[:, k, t0:t0 + W], start=(k == 0), stop=(k == KD - 1))
                    for k in range(KD):
                        mm(pg[:, 0:W], w3b[:, k, jj * 128:(jj + 1) * 128], h2T[:, k, t0:t0 + W], start=(k == 0), stop=(k == KD - 1))
                    sg = sgs.next()
                    act(sg[:, 0:W], pa[:, 0:W], AF.Silu)
                    tt("dve", ab[:, jj, 0:W], pg[:, 0:W], sg[:, 0:W], ALU.mult)
                if pending is not None:
                    down(*pending)
                    if pending[-1] and not last:
                        pt0, pW = pending[2], pending[3]
                        dma("sp", xs_tile(pt0, pW), XRES[:, :, pt0:pt0 + pW], "xst%d" % ((pt0 // 512) % 2))
                pending = (g0, gn, t0, W, ab, w2b, col, gi == len(groups) - 1)
        down(*pending)
        if not last:
            pt0, pW = pending[2], pending[3]
            dma("sp", xs_tile(pt0, pW), XRES[:, :, pt0:pt0 + pW], "xst%d" % ((pt0 // 512) % 2))
        pending = None

    FB = 73728
    fgr = DEPTH * VROWS
    gbc = AV(FB, (D,), F32)
    junkF = AV(FB + 4096, (512,), F32)
    otl = [AV(FB + 6144 + i * 4096, (D,), F32) for i in range(3)]
    ssqF = ST("ssqF", [128, NB, 4], F32)
    dma("sp", gbc, vecs_d[fgr:fgr + 8, :].rearrange("a b -> (a b)").partition_broadcast(128), "c0")
    pbrF = _Rot(range(8))

    def fin_T(blk):
        banks = []
        for kq in range(2):
            pb = PS(pbrF.next())
            for kk in range(4):
                tr(pb[:, kk * 128:(kk + 1) * 128], XRES[:, kq * 4 + kk, blk * 128:(blk + 1) * 128], ident_f)
            act(junkF, pb[:, :], AF.Square, accum=ssqF[:, blk, kq:kq + 1])
            banks.append(pb)
        return banks

    def fin_E(blk, banks):
        tt("dve", ssqF[:, blk, 2:3], ssqF[:, blk, 0:1], ssqF[:, blk, 1:2], ALU.add)
        ts("dve", ssqF[:, blk, 2:3], ssqF[:, blk, 2:3], 1.0 / D, ALU.mult, EPS, ALU.add)
        tt("pool", ssqF[:, blk, 3:4], ssqF[:, blk, 2:3], small[:, 0:1], ALU.pow)
        ot = otl[blk % 3]
        for kq in range(2):
            stt(ot[:, kq * 512:(kq + 1) * 512], banks[kq][:, :], ssqF[:, blk, 3:4], gbc[:, kq * 512:(kq + 1) * 512], ALU.mult, ALU.mult)
        dma("sp", out_d[blk * 128:(blk + 1) * 128, :], ot, "ost%d" % (blk % 3))

    prevF = None
    for blk in range(NB):
        banks = fin_T(blk)
        if prevF is not None:
            fin_E(*prevF)
        prevF = (blk, banks)
    fin_E(*prevF)

    P.finalize(None)
    streams = []
    for o in P.ops:
        s = P._stream(o)
        if s not in streams:
            streams.append(s)
    sems = {s: es.enter_context(nc.semaphore("sem_%s_%s" % s)) for s in streams}
    out_slots = [s for s in streams if s[0] == "slot" and s[1].startswith("ost")]
    block = es.enter_context(nc.Block())

    @block.tensor
    def _(e):
        P.emit_engine("pe", e, sems)

    @block.scalar
    def _(e):
        P.emit_engine("act", e, sems)

    @block.vector
    def _(e):
        P.emit_engine("dve", e, sems)

    @block.gpsimd
    def _(e):
        P.emit_engine("pool", e, sems)

    @block.sync
    def _(e):
        P.emit_engine("sp", e, sems)
        for s in out_slots:
            e.wait_ge(sems[s], P.max_vals[s])

    es.close()
    return nc, P


def _const_tables(N, NC):
    bf = ml_dtypes.bfloat16
    rows = N // GRID_W
    t = np.arange(N)
    row = (t // GRID_W).astype(np.float64)
    colp = (t % GRID_W).astype(np.float64)
    inv_freq = ROPE_BASE ** (-np.arange(16, dtype=np.float64) / 16)
    ropec = np.zeros((128, N)); ropes = np.zeros((128, N))
    perm = np.zeros((128, 128))
    for p in range(128):
        d = p % 64
        axis, half, f = d // 32, (d % 32) // 16, d % 16
        ang = (row if axis == 0 else colp) * inv_freq[f]
        ropec[p] = np.cos(ang)
        ropes[p] = -np.sin(ang) if half == 0 else np.sin(ang)
        partner = p + 16 if half == 0 else p - 16
        perm[partner, p] = 1.0
    ident = np.eye(128)
    ones = np.ones((128, 128))
    cc = np.arange(64)
    a64 = 2 * np.pi * np.outer(cc, cc) / 64
    c64 = np.cos(a64) / 8.0; s64 = np.sin(a64) / 8.0
    c64bd = np.zeros((128, 128)); s64bd = np.zeros((128, 128))
    gln = np.zeros((128, 128))
    for g in range(2):
        c64bd[g * 64:(g + 1) * 64, g * 64:(g + 1) * 64] = c64
        s64bd[g * 64:(g + 1) * 64, g * 64:(g + 1) * 64] = s64
        gln[g * 64:(g + 1) * 64, g * 64:(g + 1) * 64] = 1.0 / 64
    cmat = np.stack([ident, ones, perm, c64bd, s64bd], axis=1).astype(bf)
    cmatf = np.stack([ident, gln], axis=1).astype(np.float32)

    NB = N // 128
    NHt = N // 2
    n_ = np.arange(NHt)
    kp = np.arange(NHt)
    ang_e = 2 * np.pi * (np.outer(n_, kp) % NHt) / NHt
    ang_o = 2 * np.pi * (np.outer(n_, 2 * kp + 1) % N) / N
    sc_ = 1.0 / np.sqrt(N)

    def lay(Mx):
        return np.ascontiguousarray(Mx.reshape(NB // 2, 128, NHt // 256, 256).transpose(2, 1, 0, 3))

    dftc_h = np.stack([lay(np.cos(ang_e) * sc_), lay(np.cos(ang_o) * sc_)], axis=0).astype(bf)
    dfts_h = np.stack([lay(-np.sin(ang_e) * sc_), lay(-np.sin(ang_o) * sc_)], axis=0).astype(bf)

    def dft(M):
        n = np.arange(M)
        ang = 2 * np.pi * (np.outer(n, n) % M) / M
        return np.cos(ang) / np.sqrt(M), -np.sin(ang) / np.sqrt(M)

    Cc, Sc = dft(NC)
    NCB = NC // 128

    def layc(Mx):
        return np.ascontiguousarray(Mx.reshape(NCB, 128, NC).transpose(1, 0, 2)).astype(bf)

    return dict(ropec=ropec.astype(bf), ropes=ropes.astype(bf), cmat=cmat, cmatf=cmatf,
                dftc=dftc_h, dfts=dfts_h, dcc=layc(Cc), dcs=layc(Sc))


def _pack_vecs(inp, DEPTH):
    vecs = np.zeros((384, 128), np.float32)
    for l in range(DEPTH):
        b = l * VROWS
        vecs[b + R_BADA:b + R_BADA + 48] = np.asarray(inp["b_ada"][l], np.float32).reshape(48, 128)
        vecs[b + R_N1:b + R_N1 + 8] = np.asarray(inp["norm1_g"][l], np.float32).reshape(8, 128)
        vecs[b + R_N2:b + R_N2 + 8] = np.asarray(inp["norm2_g"][l], np.float32).reshape(8, 128)
        vecs[b + R_CB:b + R_CB + 2] = np.asarray(inp["conv_b"][l], np.float32).reshape(2, 128)
        vecs[b + R_LG:b + R_LG + 2] = np.asarray(inp["conv_ln_g"][l], np.float32).reshape(2, 128)
        vecs[b + R_LB:b + R_LB + 2] = np.asarray(inp["conv_ln_b"][l], np.float32).reshape(2, 128)
        vecs[b + R_CW:b + R_CW + 62] = np.asarray(inp["conv_w"][l], np.float32).reshape(62, 128)
    vecs[DEPTH * VROWS:DEPTH * VROWS + 8] = np.asarray(inp["final_g"], np.float32).reshape(8, 128)
    return vecs


def make_in_maps(inp, N, NC, DEPTH, B):
    f = lambda a: np.ascontiguousarray(np.asarray(a, np.float32))
    consts = _const_tables(N, NC)
    shared = dict(consts)
    shared["vecs"] = _pack_vecs(inp, DEPTH)
    for k in ("w_ada", "w_in", "subln_g", "w_fourier", "w_conv_out", "w_out", "w_ffn1", "w_ffn3", "w_ffn2"):
        shared[k] = f(inp[k])
    shared["lamv"] = np.ascontiguousarray(np.stack([f(inp["lam_q1"]), f(inp["lam_k1"]), f(inp["lam_q2"]), f(inp["lam_k2"])], axis=1))
    x = f(inp["x"]); ctx = f(inp["ctx"]); c = f(inp["c"]); c_ctx = f(inp["c_ctx"])
    maps = []
    for b in range(B):
        m = dict(shared)
        m["x"] = x[b]; m["ctx"] = ctx[b]
        m["cc"] = np.ascontiguousarray(np.stack([c[b], c_ctx], axis=1))
        maps.append(m)
    return maps


_CACHE = {}


def kernel(**inputs):
    N, NC, DEPTH, B = 2048, 256, 2, 8
    if "nc" not in _CACHE:
        _CACHE["nc"] = build_program(N, NC, DEPTH)[0]
    nc = _CACHE["nc"]
    maps = make_in_maps(inputs, N, NC, DEPTH, B)
    res = run_bass_kernel_spmd(nc, maps, core_ids=list(range(B)))
    return np.stack([np.asarray(r["out"], np.float32) for r in res.results], axis=0)
```

```python
import numpy as np
import ml_dtypes
import concourse.bass as bass
import concourse.mybir as mybir
from concourse.bass_utils import run_bass_kernel_spmd

F32 = mybir.dt.float32
BF16 = mybir.dt.bfloat16
AF = mybir.ActivationFunctionType
ALU = mybir.AluOpType


class _Op:
    __slots__ = ("eng", "fn", "deps", "sem", "val", "signal", "slot", "group", "idx", "dbg")


def _box(ap):
    t = ap.tensor
    shp = tuple(t.shape)
    pat = ap.ap
    off = int(ap.offset)
    sp = str(ap.space)
    esz = mybir.dt.size(ap.dtype)
    if sp == "DRAM":
        lo = off * esz
        hi = (off + sum((c - 1) * abs(s) for s, c in pat) + 1) * esz - 1
        return (ap.name, 0, 0, lo, hi, False)
    pstride = 1
    for d in shp[1:]:
        pstride *= d
    plo = off // pstride
    flo = off % pstride
    assert pat[0][0] == pstride or pat[0][1] == 1, (pat, pstride)
    phi = plo + pat[0][1] - 1
    fhi = flo + sum((c - 1) * abs(s) for s, c in pat[1:])
    lo = flo * esz
    hi = (fhi + 1) * esz - 1
    if sp == "PSUM":
        lo = (lo // 2048) * 2048
        hi = (hi // 2048) * 2048 + 2047
        return (ap.name + "@" + sp, 0, 127, lo, hi, True)
    return (ap.name + "@" + sp, plo, phi, lo, hi, False)


class Prog:
    ENGS = ("pe", "act", "dve", "pool", "sp")

    def __init__(self, nc):
        self.nc = nc
        self.ops = []
        self.track = {}
        self.slot_groups = {}
        self.slot_cur = {}

    def add(self, eng, fn, reads=(), writes=(), slot=None, group=None):
        op = _Op()
        op.eng = eng
        op.fn = fn
        op.deps = set()
        op.signal = False
        op.slot = slot
        op.idx = len(self.ops)
        op.group = None
        op.dbg = None
        if DEBUG_LINES:
            import sys as _sys
            fr = _sys._getframe(2)
            op.dbg = (fr.f_lineno, fr.f_back.f_lineno if fr.f_back else None)
        if slot is not None:
            groups = self.slot_groups.setdefault(slot, [])
            if groups and self.slot_cur.get(slot) == group and group is not None:
                groups[-1].append(op.idx)
            else:
                if groups:
                    op.deps.add(groups[-1][-1])
                groups.append([op.idx])
                self.slot_cur[slot] = group
            op.group = (slot, len(groups) - 1)
        for ap in reads:
            self._access(op, ap, False)
        for ap in writes:
            self._access(op, ap, True)
        self.ops.append(op)
        return op

    def _same_stream(self, a, b):
        if a.slot is not None or b.slot is not None:
            return a.slot is not None and a.slot == b.slot
        return a.eng == b.eng

    def _access(self, op, ap, is_write):
        key, plo, phi, lo, hi, excl = _box(ap)
        is_write = is_write or excl
        lst = self.track.setdefault(key, [])
        keep = []
        for rec in lst:
            rplo, rphi, rlo, rhi, ridx, rw = rec
            overlap = not (rphi < plo or rplo > phi or rhi < lo or rlo > hi)
            if overlap and (is_write or rw) and ridx != op.idx:
                op.deps.add(ridx)
            contained = rplo >= plo and rphi <= phi and rlo >= lo and rhi <= hi
            if contained and ridx != op.idx:
                if is_write:
                    continue
                if (not rw) and self._same_stream(self.ops[ridx], op):
                    continue
            keep.append(rec)
        keep.append([plo, phi, lo, hi, op.idx, is_write])
        self.track[key] = keep

    def finalize(self, block_sems):
        ops = self.ops
        def stream(o):
            return ("slot", o.slot) if o.slot is not None else ("eng", o.eng)
        pos = {}
        cnt = {}
        for o in ops:
            s = stream(o)
            cnt[s] = cnt.get(s, 0) + 1
            pos[o.idx] = cnt[s]
        def target(j):
            o = ops[j]
            if o.slot is not None:
                g = self.slot_groups[o.slot][o.group[1]]
                return g[-1]
            return j
        waited = {e: {} for e in self.ENGS}
        need = {}
        for o in ops:
            w = waited[o.eng]
            req = {}
            for j in o.deps:
                tj = target(j)
                if tj == o.idx:
                    continue
                if tj > o.idx:
                    assert ops[tj].slot is not None and ops[tj].group == o.group, "dep on future op"
                    continue
                s = stream(ops[tj])
                if s == ("eng", o.eng) and o.slot is None and not SAME_ENGINE_SYNC:
                    continue
                if s == ("eng", o.eng) and o.eng == "pe" and o.slot is None:
                    continue
                if s == ("eng", o.eng) and o.slot is not None and ops[tj].slot is None:
                    pass
                p = pos[tj]
                if w.get(s, 0) >= p:
                    continue
                if req.get(s, (0, None))[0] < p:
                    req[s] = (p, tj)
            lst = []
            for s, (p, tj) in req.items():
                w[s] = p
                lst.append((s, tj))
                ops[tj].signal = True
            need[o.idx] = lst
        for o in ops:
            if o.slot is not None:
                o.signal = True
        val = {}
        c = {}
        for o in ops:
            s = stream(o)
            if o.signal:
                c[s] = c.get(s, 0) + (16 if o.slot is not None else 1)
            val[o.idx] = c.get(s, 0)
        self._need, self._val, self._stream = need, val, stream
        self.max_vals = dict(c)
        return need, val

    def emit(self, engines, sems):
        need, val, stream = self._need, self._val, self._stream
        for o in self.ops:
            e = engines[o.eng]
            for s, tj in need[o.idx]:
                e.wait_ge(sems[s], val[tj])
            ins = o.fn(e)
            if o.signal:
                ins.then_inc(sems[stream(o)], 16 if o.slot is not None else 1)

    def emit_engine(self, eng_name, e, sems):
        need, val, stream = self._need, self._val, self._stream
        for o in self.ops:
            if o.eng != eng_name:
                continue
            for s, tj in need[o.idx]:
                e.wait_ge(sems[s], val[tj])
            ins = o.fn(e)
            if DEBUG_LINES:
                NAMES[getattr(ins.ins, "name", None)] = (o.idx, o.dbg)
            if o.signal:
                ins.then_inc(sems[stream(o)], 16 if o.slot is not None else 1)


SAME_ENGINE_SYNC = True
DEBUG_LINES = False
NAMES = {}


D = 1024
KD = D // 128
GRID_W = 64
INW = 2304
DFF = 2816
NFF = DFF // 128
CONV_K = 31
EPS = 1e-6
ROPE_BASE = 10000.0
VROWS = 132
R_BADA, R_N1, R_N2, R_CB, R_LG, R_LB, R_CW = 0, 48, 56, 64, 66, 68, 70


class _Rot:
    def __init__(self, items):
        self.items = list(items)
        self.i = 0

    def next(self):
        v = self.items[self.i % len(self.items)]
        self.i += 1
        return v


def build_program(N, NC, DEPTH):
    T = N + NC
    NB = N // 128
    NCB = NC // 128
    TB = NB + NCB
    lat_tiles = [(t0, 512) for t0 in range(0, N, 512)]
    all_tiles = lat_tiles + [(N, NC)]
    nc = bass.Bass("TRN2", target_bir_lowering=False)
    P = Prog(nc)

    def din(name, shape, dt=F32):
        return nc.dram_tensor(name, list(shape), dt, kind="ExternalInput").ap()

    x_d = din("x", [N, D]); ctx_d = din("ctx", [NC, D]); cc_d = din("cc", [D, 2])
    w_ada_d = din("w_ada", [DEPTH, D, 6 * D]); vecs_d = din("vecs", [384, 128])
    w_in_d = din("w_in", [DEPTH, D, INW]); lamv_d = din("lamv", [DEPTH, 4, 64])
    subln_d = din("subln_g", [DEPTH, 128]); wf_d = din("w_fourier", [DEPTH, 4, 64, 64])
    wco_d = din("w_conv_out", [DEPTH, 256, 256]); w_out_d = din("w_out", [DEPTH, D, D])
    w1_d = din("w_ffn1", [DEPTH, D, DFF]); w3_d = din("w_ffn3", [DEPTH, D, DFF]); w2_d = din("w_ffn2", [DEPTH, DFF, D])
    ropec_d = din("ropec", [128, N], BF16); ropes_d = din("ropes", [128, N], BF16)
    NKC = N // 256
    dftc_d = din("dftc", [2, NKC // 2, 128, NB // 2, 256], BF16); dfts_d = din("dfts", [2, NKC // 2, 128, NB // 2, 256], BF16)
    dcc_d = din("dcc", [128, NCB, NC], BF16); dcs_d = din("dcs", [128, NCB, NC], BF16)
    cmat_d = din("cmat", [128, 5, 128], BF16)
    cmatf_d = din("cmatf", [128, 2, 128], F32)
    out_d = nc.dram_tensor("out", [N, D], F32, kind="ExternalOutput").ap()
    NT_ALL = len(all_tiles)
    xs_d = nc.dram_tensor("xs", [NT_ALL, 128, KD, 512], F32).ap()

    def xs_tile(t0, W):
        ti_ = (t0 // 512) if t0 < N else (NT_ALL - 1)
        return xs_d[ti_, :, :, 0:W]
    modrow_d = nc.dram_tensor("modrow", [DEPTH, 2, 6 * D], F32).ap()

    from contextlib import ExitStack
    es = ExitStack()
    ARENA = 184320
    arena = es.enter_context(nc.sbuf_tensor("arena", [128, ARENA // 4], F32))
    psum = es.enter_context(nc.psum_tensor("ps", [128, 8, 512], F32))

    def AV(off, shape, dt, parts=128):
        esz = mybir.dt.size(dt)
        n = 1
        for d_ in shape:
            n *= d_
        assert off % 4 == 0 and off + n * esz <= ARENA, (off, shape)
        a = arena[:, off // 4:(off + n * esz + 3) // 4]
        if dt != F32:
            a = a.bitcast(dt)
        a = a[0:parts, 0:n]
        if len(shape) == 2:
            a = a.rearrange("p (a b) -> p a b", b=shape[1])
        elif len(shape) == 3:
            a = a.rearrange("p (a b c) -> p a b c", b=shape[1], c=shape[2])
        elif len(shape) == 4:
            a = a.rearrange("p (a b c d) -> p a b c d", b=shape[1], c=shape[2], d=shape[3])
        return a

    def PS(bank, dt=F32):
        a = psum[:, bank, :]
        return a if dt == F32 else a.bitcast(dt)

    def ST(name, shape, dt):
        return es.enter_context(nc.sbuf_tensor("s_" + name, list(shape), dt))

    cmat = ST("cmat", [128, 5, 128], BF16); cmatf = ST("cmatf", [128, 2, 128], F32)
    ident_bf, ones_bf, perm_bf = cmat[:, 0, :], cmat[:, 1, :], cmat[:, 2, :]
    ident_f, gln_f = cmatf[:, 0, :], cmatf[:, 1, :]
    ropec = ST("ropec", [128, N], BF16); ropes = ST("ropes", [128, N], BF16)
    vT = ST("vT", [128, 384], F32)
    modT = ST("modT", [128, DEPTH, 48, 2], F32)
    gm = ST("gm", [128, DEPTH, 2, KD, 2], F32)
    dcc = ST("dcc", [128, NCB, NC], BF16); dcs = ST("dcs", [128, NCB, NC], BF16)
    gcol = ST("gcol", [128, DEPTH], F32)
    lamb = ST("lamb", [128, DEPTH, 4, 64], F32); lamt = ST("lamt", [128, 2, 64], F32)
    lams = ST("lams", [128, 8], F32); neglam = ST("neglam", [128, DEPTH], F32)
    sc_f = ST("sc_f", [128, KD, 2], F32); sc_bf = ST("sc_bf", [128, KD, 2], BF16)
    small = ST("small", [128, 64], F32)
    wfbd = ST("wfbd", [128, 2, 128], BF16); wfst = ST("wfst", [128, 2, 128], F32)
    wpw2 = ST("wpw2", [128, 2, 256], BF16)
    epsT = ST("epsT", [128, 1], F32)

    def mm(out, lhsT, rhs, start=True, stop=True):
        P.add("pe", lambda e: e.matmul(out, lhsT=lhsT, rhs=rhs, start=start, stop=stop), [lhsT, rhs], [out])

    def tr(out, in_, ident):
        P.add("pe", lambda e: e.transpose(out, in_, ident), [in_, ident], [out])

    def act(out, in_, func, scale=1.0, bias=None, accum=None, eng="act"):
        rd = [in_] + [a for a in (scale, bias) if not isinstance(a, (int, float, type(None)))]
        wr = [out] + ([accum] if accum is not None else [])
        kw = {}
        if bias is not None:
            kw["bias"] = bias
        if accum is not None:
            kw["accum_out"] = accum
        P.add("act", lambda e: e.activation(out=out, in_=in_, func=func, scale=scale, **kw), rd, wr)

    def tt(eng, out, a, b, op):
        P.add(eng, lambda e: e.tensor_tensor(out=out, in0=a, in1=b, op=op), [a, b], [out])

    def ts(eng, out, a, s1, op0, s2=None, op1=None):
        rd = [a] + [s for s in (s1, s2) if not isinstance(s, (int, float, type(None)))]
        if op1 is None:
            P.add(eng, lambda e: e.tensor_scalar(out=out, in0=a, scalar1=s1, scalar2=None, op0=op0), rd, [out])
        else:
            P.add(eng, lambda e: e.tensor_scalar(out=out, in0=a, scalar1=s1, scalar2=s2, op0=op0, op1=op1), rd, [out])

    def stt(out, a, s, b, op0, op1):
        rd = [a, b] + ([s] if not isinstance(s, (int, float)) else [])
        P.add("dve", lambda e: e.scalar_tensor_tensor(out=out, in0=a, scalar=s, in1=b, op0=op0, op1=op1), rd, [out])

    def cp(eng, out, in_):
        if eng == "act":
            P.add("act", lambda e: e.activation(out=out, in_=in_, func=AF.Copy), [in_], [out])
        else:
            P.add(eng, lambda e: e.tensor_copy(out=out, in_=in_), [in_], [out])

    def recip(out, in_):
        P.add("dve", lambda e: e.reciprocal(out=out, in_=in_), [in_], [out])

    def mset(eng, ap, val):
        P.add(eng, lambda e: e.memset(ap, val), [], [ap])

    def dma(eng, out, in_, slot, group=None):
        P.add(eng, lambda e: e.dma_start(out=out, in_=in_), [in_], [out], slot=slot, group=group)

    cpr = _Rot(["act", "dve"])

    dma("sp", cmat[:], cmat_d, "c0"); dma("sp", cmatf[:], cmatf_d, "c1")
    dma("sp", ropec[:], ropec_d, "c2"); dma("sp", ropes[:], ropes_d, "c3")
    dma("sp", dcc[:], dcc_d, "c0"); dma("sp", dcs[:], dcs_d, "c1")
    mset("dve", epsT[:], EPS)
    mset("dve", small[:, 0:8], -0.5)
    vst = AV(0, (3, 128), F32)
    dma("sp", vst, vecs_d.rearrange("(j p) f -> p j f", p=128), "c2")
    for j in range(3):
        tr(PS(0)[:, j * 128:(j + 1) * 128], vst[:, j, :], ident_f)
    cp("dve", vT[:], PS(0)[:, 0:384])
    for l in range(DEPTH):
        dma("sp", lamb[:, l], lamv_d[l].partition_broadcast(128), "c3")
        dma("sp", gcol[:, l:l + 1], subln_d[l].rearrange("(p o) -> p o", o=1), "c0")
        lam_init = 0.8 - 0.6 * float(np.exp(-0.3 * l))
        tt("dve", lamt[:, 0, :], lamb[:, l, 0, :], lamb[:, l, 1, :], ALU.mult)
        tt("dve", lamt[:, 1, :], lamb[:, l, 2, :], lamb[:, l, 3, :], ALU.mult)
        P.add("dve", lambda e: e.tensor_reduce(out=lams[:, 0:2], in_=lamt[:], axis=mybir.AxisListType.X, op=ALU.add), [lamt[:]], [lams[:, 0:2]])
        act(lams[:, 2:4], lams[:, 0:2], AF.Exp)
        tt("dve", lams[:, 4:5], lams[:, 3:4], lams[:, 2:3], ALU.subtract)
        ts("dve", neglam[:, l:l + 1], lams[:, 4:5], -lam_init, ALU.add)
        ts("dve", gcol[:, l:l + 1], gcol[:, l:l + 1], 1.0 - lam_init, ALU.mult)
    dma("sp", sc_f[:], cc_d.rearrange("(k p) t -> p k t", p=128), "c1")
    act(sc_bf[:], sc_f[:], AF.Silu)
    WA_O = 64800
    wa_bufs = [AV(WA_O + i * 4096, (KD, 256), BF16) for i in range(2)]
    mstage_t = ST("mstage", [2, 256], F32)
    mstage = mstage_t[:, :]
    mrow_t = ST("mrow", [96, 128], F32)
    mrow = mrow_t[:, :]
    pbr = _Rot(range(1, 8))
    all_pieces = [(0, pc) for pc in range(24)]
    for l_ in range(1, DEPTH):
        all_pieces += [(l_, pc) for pc in range(24)]
    mstate = {"dma": 0, "mm": 0}

    def mods_dma():
        i = mstate["dma"]
        if i >= len(all_pieces):
            return
        l_, pc = all_pieces[i]
        dma("pool", wa_bufs[i % 2], w_ada_d[l_, :, pc * 256:(pc + 1) * 256].rearrange("(k p) n -> p k n", p=128), "wa%d" % (i % 2))
        mstate["dma"] = i + 1

    def mods_piece(bank):
        i = mstate["mm"]
        while mstate["dma"] <= min(i + 1, len(all_pieces) - 1):
            mods_dma()
        l_, pc = all_pieces[i]
        wa = wa_bufs[i % 2]
        pb = PS(bank)
        for k in range(KD):
            mm(pb[0:2, 0:256], sc_bf[:, k, :], wa[:, k, :], start=(k == 0), stop=(k == KD - 1))
        cp("dve", mstage, pb[0:2, 0:256])
        dma("sp", modrow_d[l_, :, pc * 256:(pc + 1) * 256], mstage, "mst")
        mstate["mm"] = i + 1

    def mods_finish(l, s0, s1, bank):
        r0, r1 = s0 * 8, s1 * 8
        nr = r1 - r0
        for v in range(2):
            dma("sp", mrow[v * nr:(v + 1) * nr, :], modrow_d[l, v, r0 * 128:r1 * 128].rearrange("(r p) -> r p", p=128), "mld", group="mf%d_%d" % (l, s0))
        pb = PS(bank)
        tr(pb[:, 0:2 * nr], mrow[0:2 * nr, :], ident_f[0:2 * nr, 0:2 * nr])
        for v in range(2):
            tt("dve", modT[:, l, r0:r1, v], pb[:, v * nr:(v + 1) * nr], vT[:, l * VROWS + R_BADA + r0: l * VROWS + R_BADA + r1], ALU.add)
        for n_i, (sec_sc, rg) in enumerate(((1, R_N1), (4, R_N2))):
            if not (s0 <= sec_sc < s1):
                continue
            for k in range(KD):
                ts("dve", gm[:, l, n_i, k, :], modT[:, l, sec_sc * 8 + k, :], 1.0, ALU.add)
                ts("dve", gm[:, l, n_i, k, :], gm[:, l, n_i, k, :], vT[:, l * VROWS + rg + k: l * VROWS + rg + k + 1], ALU.mult)

    def load_w_in(l_):
        w_in_ = AV(110592, (KD, INW), BF16)
        if l_ == 0:
            order = [(hf, k) for hf in range(2) for k in range(KD)]
        else:
            order = [(hf, k) for hf in range(2) for k in range(5)] + [(hf, k) for hf in range(2) for k in range(5, KD)]
        for (hf, k) in order:
            dma("pool", w_in_[:, k, hf * 1152:(hf + 1) * 1152], w_in_d[l_, k * 128:(k + 1) * 128, hf * 1152:(hf + 1) * 1152],
                "win%d" % ((hf if l_ == 0 else (0 if k < 5 else 1))), group="l%d" % l_)

    load_w_in(0)

    def bg_mods(max_layer, bank):
        i = mstate["mm"]
        if i < len(all_pieces) and all_pieces[i][0] <= max_layer:
            mods_piece(bank)
            if i == 23:
                mods_finish(0, 2, 6, bank)
            elif i > 23 and (i + 1) % 24 == 0:
                mods_finish(all_pieces[i][0], 0, 6, bank)
            return True
        return False

    XIN = 90112
    xtoks = [AV(16384, (4, D), F32), AV(73728, (4, D), F32)]
    xins = [AV(XIN, (KD, 512), F32), AV(32768, (KD, 512), F32)]

    def x_load(ti):
        t0, W = all_tiles[ti]
        src = x_d[t0:t0 + W, :] if t0 < N else ctx_d[:, :]
        dma("sp", xtoks[ti % 2][:, 0:W // 128, :], src.rearrange("(s p) d -> p s d", p=128), "xl%d" % (ti % 2))

    x_load(0)
    mods_done = 0
    for ti, (t0, W) in enumerate(all_tiles):
        nsub = W // 128
        if ti + 1 < len(all_tiles):
            x_load(ti + 1)
        xtok = xtoks[ti % 2]
        xin = xins[ti % 2][:, :, 0:W]
        for k in range(KD):
            pb = PS(pbr.next())
            for s_ in range(nsub):
                tr(pb[:, s_ * 128:(s_ + 1) * 128], xtok[:, s_, k * 128:(k + 1) * 128], ident_f)
            cp(cpr.next(), xin[:, k, :], pb[:, 0:W])
        dma("sp", xs_tile(t0, W), xin, "xst%d" % ((t0 // 512) % 2))
        for _ in range(2):
            if mods_done < 8:
                mods_piece(pbr.next()); mods_done += 1
    while mods_done < 8:
        mods_piece(pbr.next()); mods_done += 1
    mods_finish(0, 0, 2, pbr.next())

    QT_O, KT_O, V_O, UF_O = 0, 18432, 36864, 55584
    MIX_O = 73728
    WIN_O = 110592
    Z_O = 147456
    QT = AV(QT_O, (4, T), BF16); KT = AV(KT_O, (4, T), BF16)
    V = AV(V_O, (TB, 4, 130), BF16)
    ufT = AV(UF_O, (2, T), BF16)
    mixT = AV(MIX_O, (KD, T), BF16)
    zl = AV(Z_O, (2, N + 30), BF16)
    zc_ = AV(Z_O + 2 * (N + 30) * 2, (2, NC + 30), BF16)
    XRES = AV(0, (KD, T), F32)

    def norm_sq(xin, W, sq):
        act(sq[:, :, 0:W], xin, AF.Square)

    def norm_rest(l, n_i, sec_sh, xin, W, col, hT, sq, rstd, sqrtt, tmps, bank=None):
        pb = PS(pbr.next() if bank is None else bank)
        for k in range(KD):
            mm(pb[:, 0:W], ones_bf, sq[:, k, 0:W], start=(k == 0), stop=(k == KD - 1))
        act(sqrtt[:, 0:W], pb[:, 0:W], AF.Ln, scale=1.0 / D, bias=epsT[:, 0:1])
        act(rstd[:, 0:W], sqrtt[:, 0:W], AF.Exp, scale=-0.5)
        for k in range(KD):
            tmp = tmps.next()
            stt(tmp[:, 0:W], xin[:, k, :], gm[:, l, n_i, k, col:col + 1], rstd[:, 0:W], ALU.mult, ALU.mult)
            act(hT[:, k, 0:W], tmp[:, 0:W], AF.Identity, bias=modT[:, l, sec_sh * 8 + k, col:col + 1])

    def norm_tile(l, n_i, sec_sh, xin, W, col, hT, sq, rstd, sqrtt, tmps):
        norm_sq(xin, W, sq)
        norm_rest(l, n_i, sec_sh, xin, W, col, hT, sq, rstd, sqrtt, tmps)

    for l in range(DEPTH):
        last = (l == DEPTH - 1)
        vb = l * VROWS
        w_in = AV(WIN_O, (KD, INW), BF16)
        if l > 0:
            load_w_in(l)
        mset("pool", wfst[:], 0.0)
        for g in range(4):
            c_, o_ = g // 2, (g % 2) * 64
            dma("sp", wfst[o_:o_ + 64, c_, o_:o_ + 64], wf_d[l, g], "c2", group="wf%d" % l)
        cp("dve", wfbd[:], wfst[:])
        dma("pool", wpw2[:], wco_d[l].rearrange("(c p) d -> p c d", p=128), "wpw")
        for zz, W_ in ((zl, N), (zc_, NC)):
            mset("pool", zz[:, :, 0:15], 0.0)
            mset("pool", zz[:, :, W_ + 15:W_ + 30], 0.0)
        sq = AV(MIX_O + 16384, (KD, 512), BF16)
        hTs = [AV(MIX_O + 24576, (KD, 512), BF16), AV(Z_O + 9472, (KD, 512), BF16)]
        rstd = AV(MIX_O + 32768, (512,), F32); sqrtt = AV(MIX_O + 34816, (512,), F32)
        TO = Z_O + 9472 + 8192
        tmpsA = _Rot([AV(TO, (512,), F32), AV(TO + 2048, (512,), F32)])
        qraws = _Rot([AV(TO + 4096, (512,), BF16), AV(TO + 5120, (512,), BF16)])
        t1s = _Rot([AV(TO + 6144, (512,), F32), AV(TO + 8192, (512,), F32)])
        t2s = _Rot([AV(TO + 10240, (512,), F32), AV(TO + 12288, (512,), F32)])
        sigs = _Rot([AV(TO + 14336, (512,), F32), AV(TO + 16384, (512,), F32)])
        def p1_load(ti):
            t0, W = all_tiles[ti]
            xin = AV(MIX_O, (KD, W), F32)
            dma("sp", xin, xs_tile(t0, W), "xin")

        def p1_norm(ti):
            t0, W = all_tiles[ti]
            col = 1 if t0 >= N else 0
            xin = AV(MIX_O, (KD, W), F32)
            norm_tile(l, 0, 0, xin, W, col, hTs[ti % 2], sq, rstd, sqrtt, tmpsA)

        def p1_norm_sq(ti):
            t0, W = all_tiles[ti]
            norm_sq(AV(MIX_O, (KD, W), F32), W, sq)

        def p1_norm_rest(ti):
            t0, W = all_tiles[ti]
            col = 1 if t0 >= N else 0
            norm_rest(l, 0, 0, AV(MIX_O, (KD, W), F32), W, col, hTs[ti % 2], sq, rstd, sqrtt, tmpsA)

        def rope_tail(qraw, t1, dst, t0, W):
            t2 = t2s.next()
            pb2 = PS(pbr.next())
            mm(pb2[:, 0:W], perm_bf, qraw[:, 0:W])
            tt("dve", t2[:, 0:W], pb2[:, 0:W], ropes[:, t0:t0 + W], ALU.mult)
            tt("pool", dst, t1[:, 0:W], t2[:, 0:W], ALU.add)

        nt_ = len(all_tiles)
        p1_load(0); p1_norm(0)
        if nt_ > 1:
            p1_load(1)
        for ti, (t0, W) in enumerate(all_tiles):
            is_ctx = t0 >= N
            col = 1 if is_ctx else 0
            hT = hTs[ti % 2]
            pend_rope = None
            ctx_kv_only = is_ctx and last
            for ch in range(8):
                if ctx_kv_only and ch < 4:
                    continue
                pb = PS(pbr.next())
                for k in range(KD):
                    mm(pb[:, 0:W], w_in[:, k, ch * 128:(ch + 1) * 128], hT[:, k, 0:W], start=(k == 0), stop=(k == KD - 1))
                dst = (QT if ch < 4 else KT)[:, ch % 4, t0:t0 + W]
                if ch == 3 and ti + 1 < nt_:
                    p1_norm_sq(ti + 1)
                if is_ctx:
                    cp(cpr.next(), dst, pb[:, 0:W])
                else:
                    qraw = qraws.next(); t1 = t1s.next()
                    cp("act", qraw[:, 0:W], pb[:, 0:W])
                    tt("dve", t1[:, 0:W], pb[:, 0:W], ropec[:, t0:t0 + W], ALU.mult)
                    if pend_rope is not None:
                        rope_tail(*pend_rope)
                    pend_rope = (qraw, t1, dst, t0, W)
            bg_mods(l, pbr.next())
            if ti + 1 < nt_:
                p1_norm_rest(ti + 1)
                if ti + 2 < nt_:
                    p1_load(ti + 2)
            for s_ in range(W // 128):
                pb = PS(pbr.next())
                for k in range(KD):
                    mm(pb[:, :], hT[:, k, s_ * 128:(s_ + 1) * 128], w_in[:, k, 1024:1536], start=(k == 0), stop=(k == KD - 1))
                tb = (t0 // 128) + s_
                cp(cpr.next(), V[:, tb, :, 0:128], pb[:, :].rearrange("p (h d) -> p h d", d=128))
                if s_ == 0 and pend_rope is not None:
                    rope_tail(*pend_rope)
                    pend_rope = None
            if ctx_kv_only:
                bg_mods(l, pbr.next())
                continue
            for c_ in range(2):
                pb = PS(pbr.next())
                for k in range(KD):
                    mm(pb[:, 0:W], w_in[:, k, 1536 + c_ * 128:1536 + (c_ + 1) * 128], hT[:, k, 0:W], start=(k == 0), stop=(k == KD - 1))
                cp(cpr.next(), ufT[:, c_, t0:t0 + W], pb[:, 0:W])
            for c_ in range(2):
                pa = PS(pbr.next()); pg = PS(pbr.next())
                for k in range(KD):
                    mm(pa[:, 0:W], w_in[:, k, 1792 + c_ * 128:1792 + (c_ + 1) * 128], hT[:, k, 0:W], start=(k == 0), stop=(k == KD - 1))
                for k in range(KD):
                    mm(pg[:, 0:W], w_in[:, k, 2048 + c_ * 128:2048 + (c_ + 1) * 128], hT[:, k, 0:W], start=(k == 0), stop=(k == KD - 1))
                sg = sigs.next()
                act(sg[:, 0:W], pg[:, 0:W], AF.Sigmoid)
                zdst = zc_[:, c_, 15:15 + W] if is_ctx else zl[:, c_, 15 + t0:15 + t0 + W]
                tt("dve", zdst, pa[:, 0:W], sg[:, 0:W], ALU.mult)
            bg_mods(l, pbr.next())

        WOUT_O = 159744
        w_out = AV(WOUT_O, (KD, D), BF16)
        for k in range(KD):
            dma("pool", w_out[:, k, :], w_out_d[l, k * 128:(k + 1) * 128, :], "wout%d" % (k % 2), group="l%d" % l)

        E_O = WIN_O
        QT_W = 256
        Ebufs = _Rot([AV(E_O + 24576 + i * 1024, (512,), BF16) for i in range(4)])
        rrs = _Rot([AV(E_O + 3072 + i * 2048, (512,), F32) for i in range(2)])
        t12s = _Rot([AV(E_O + 7168 + i * 2048, (512,), F32) for i in range(2)])
        ofs = _Rot([AV(E_O + 11264 + i * 1024, (256,), F32) for i in range(3)])
        sqb = _Rot([AV(E_O + 14336 + i * 512, (256,), BF16) for i in range(3)])
        lnb = _Rot([AV(E_O + 15872 + i * 1024, (256,), F32) for i in range(2)])
        rsb = _Rot([AV(E_O + 17920 + i * 1024, (256,), F32) for i in range(2)])
        sbr = _Rot([0, 1, 6])
        ssr = _Rot([7])
        qsets = []
        for q0 in range(0, N, QT_W):
            qsets.append((q0, QT_W, list(range(TB))))
        if not last:
            for q0 in range(0, NC, QT_W):
                qsets.append((N + q0, min(QT_W, NC - q0), list(range(NB, TB))))

        Qz = [[AV(E_O + 20480 + (hpar * 2 + tp_) * 1024, (2, 256), BF16) for tp_ in range(2)] for hpar in range(2)]
        for hpar in range(2):
            for tp_ in range(2):
                mset("pool", Qz[hpar][tp_][:, :, :], 0.0)

        def att_Q(h, q0, QW, tpar):
            hp = slice((h % 2) * 64, (h % 2) * 64 + 64)
            c1, c2 = h // 2, 2 + h // 2
            qz = Qz[h % 2][tpar]
            cp("pool", qz[hp, 0, 0:QW], QT[hp, c1, q0:q0 + QW])
            cp("pool", qz[hp, 1, 0:QW], QT[hp, c2, q0:q0 + QW])

        def att_S(h, q0, QW, kb, tpar):
            c1, c2 = h // 2, 2 + h // 2
            qz = Qz[h % 2][tpar]
            sb = PS(sbr.next())
            mm(sb[:, 0:QW], KT[:, c1, kb * 128:(kb + 1) * 128], qz[:, 0, 0:QW])
            mm(sb[:, 256:256 + QW], KT[:, c2, kb * 128:(kb + 1) * 128], qz[:, 1, 0:QW])
            E = Ebufs.next()
            if QW == 256:
                act(E[:, 0:512], sb[:, 0:512], AF.Exp, scale=0.125)
            else:
                act(E[:, :].rearrange("p (m q) -> p m q", m=2)[:, :, 0:QW], sb[:, :].rearrange("p (m q) -> p m q", m=2)[:, :, 0:QW], AF.Exp, scale=0.125)
            return E

        def att_PV(h, QW, ki, nk, kb, E, par):
            ob = PS(2 + par); sk = PS(4 + par)
            f, l_ = (ki == 0), (ki == nk - 1)
            mm(ob[:, 0:QW], V[:, kb, h, 0:128], E[:, 0:QW], start=f, stop=False)
            mm(ob[:, 256:256 + QW], V[:, kb, h, 0:128], E[:, 256:256 + QW], start=False, stop=l_)
            mm(sk[:, 0:QW], ones_bf, E[:, 0:QW], start=f, stop=False)
            mm(sk[:, 256:256 + QW], ones_bf, E[:, 256:256 + QW], start=False, stop=l_)

        def att_fin_a(h, q0, QW, par):
            ob = PS(2 + par); sk = PS(4 + par)
            rr = rrs.next(); t12 = t12s.next(); of = ofs.next(); sq_ = sqb.next()
            if QW == 256:
                recip(rr[:, 0:512], sk[:, 0:512])
                tt("dve", t12[:, 0:512], ob[:, 0:512], rr[:, 0:512], ALU.mult)
            else:
                v3 = lambda a_: a_[:, :].rearrange("p (m q) -> p m q", m=2)[:, :, 0:QW]
                recip(v3(rr), v3(sk))
                tt("dve", v3(t12), v3(ob), v3(rr), ALU.mult)
            stt(of[:, 0:QW], t12[:, 256:256 + QW], neglam[:, l:l + 1], t12[:, 0:QW], ALU.mult, ALU.add)
            tt("dve", sq_[:, 0:QW], of[:, 0:QW], of[:, 0:QW], ALU.mult)
            return (h, q0, QW, of, sq_)

        def att_fin_b(h, q0, QW, of, sq_):
            ssb = PS(ssr.next())
            mm(ssb[:, 0:QW], ones_bf, sq_[:, 0:QW])
            ln_ = lnb.next(); rs_ = rsb.next()
            act(ln_[:, 0:QW], ssb[:, 0:QW], AF.Ln, scale=1.0 / 128, bias=epsT[:, 0:1])
            act(rs_[:, 0:QW], ln_[:, 0:QW], AF.Exp, scale=-0.5)
            stt(mixT[:, h, q0:q0 + QW], of[:, 0:QW], gcol[:, l:l + 1], rs_[:, 0:QW], ALU.mult, ALU.mult)

        stages = []
        tiles_ = []
        tno = 0
        for h in range(4):
            for (q0, QW, kbs) in qsets:
                tiles_.append((h, q0, QW, tno))
                for ki, kb in enumerate(kbs):
                    stages.append((h, q0, QW, ki, len(kbs), kb, tno))
                tno += 1
        hcnt = [0, 0]
        tpar_of = {}
        for (h, q0, QW, tn) in tiles_:
            tpar_of[tn] = hcnt[h % 2] % 2
            hcnt[h % 2] += 1
        att_Q(tiles_[0][0], tiles_[0][1], tiles_[0][2], tpar_of[0])
        LA = 2
        fifo = []
        pend = []

        def do_pv(it):
            ph, pq0, pQW, pki, pnk, pkb, ppar, pE = it
            att_PV(ph, pQW, pki, pnk, pkb, pE, ppar)
            if pki == pnk - 1:
                pend.append([10, att_fin_a(ph, pq0, pQW, ppar)])

        for st in stages:
            h, q0, QW, ki, nk, kb, tn = st
            par = tn % 2
            if ki == 0 and tn + 1 < len(tiles_):
                nh, nq0, nQW, ntn = tiles_[tn + 1]
                att_Q(nh, nq0, nQW, tpar_of[ntn])
            if ki == 4:
                bg_mods(l + 1, 7)
            E = att_S(h, q0, QW, kb, tpar_of[tn])
            fifo.append((h, q0, QW, ki, nk, kb, par, E))
            if len(fifo) > LA:
                do_pv(fifo.pop(0))
            for it in pend:
                it[0] -= 1
            while pend and pend[0][0] <= 0:
                att_fin_b(*pend.pop(0)[1])
        while fifo:
            do_pv(fifo.pop(0))

        while bg_mods(l + 1, 7):
            pass

        AB = AV(0, (TB, 512), BF16)
        DFB = [(AV(18432 + i * 8192, (NB // 2, 256), BF16), AV(18432 + i * 8192 + 4096, (NB // 2, 256), BF16)) for i in range(2)]
        FTs = _Rot([AV(E_O + i * 1024, (2, 256), BF16) for i in range(2)])
        pbr3 = _Rot(range(8))
        NH = NB // 2
        ABp = AV(34816, (NH, 512), BF16); ABm = AV(34816 + NH * 1024, (NH, 512), BF16)

        def stage_a(tb):
            pb = PS(pbr3.next())
            for c_ in range(2):
                mm(pb[:, c_ * 256:(c_ + 1) * 256], ufT[:, c_, tb * 128:(tb + 1) * 128], cmat[:, 3:5, :].rearrange("p a b -> p (a b)"))
            cp(cpr.next(), AB[:, tb, :], pb[:, :])

        for i in range(NH):
            stage_a(i); stage_a(i + NH)
            tt("dve", ABp[:, i, :], AB[:, i, :], AB[:, i + NH, :], ALU.add)
            tt("pool", ABm[:, i, :], AB[:, i, :], AB[:, i + NH, :], ALU.subtract)
        if not last:
            for tb in range(NB, TB):
                stage_a(tb)
        while pend:
            att_fin_b(*pend.pop(0)[1])

        pend_f = None

        def fourier_out1(pb, Wk, tok0):
            FT = FTs.next()
            cp("act", FT[:, :, 0:Wk], pb[:, :].rearrange("p (c k) -> p c k", c=2)[:, :, 0:Wk])
            return (FT, Wk, tok0)

        def fourier_out(pb, Wk, tok0):
            fourier_out2(*fourier_out1(pb, Wk, tok0))

        def fourier_out2(FT, Wk, tok0):
            pb2 = PS(pbr3.next())
            for c_ in range(2):
                mm(pb2[:, c_ * 256:c_ * 256 + Wk], wfbd[:, c_, :], FT[:, c_, 0:Wk])
            dst = tok0 if not isinstance(tok0, int) else mixT[:, 4:6, tok0:tok0 + Wk]
            cp("dve", dst, pb2[:, :].rearrange("p (c k) -> p c k", c=2)[:, :, 0:Wk])

        step = 0
        for par, ABx in ((0, ABp), (1, ABm)):
            for kc in range(NKC // 2):
                Cb, Sb = DFB[step % 2]
                dma("sp", Cb, dftc_d[par, kc], "dfc%d" % (step % 2)); dma("sp", Sb, dfts_d[par, kc], "dfs%d" % (step % 2))
                step += 1
                pb = PS(pbr3.next())
                for c_ in range(2):
                    for nb in range(NH):
                        mm(pb[:, c_ * 256:(c_ + 1) * 256], ABx[:, nb, c_ * 256:c_ * 256 + 128], Cb[:, nb, :], start=(nb == 0), stop=False)
                        mm(pb[:, c_ * 256:(c_ + 1) * 256], ABx[:, nb, c_ * 256 + 128:c_ * 256 + 256], Sb[:, nb, :], start=False, stop=(nb == NH - 1))
                if pend_f is not None:
                    fourier_out2(*pend_f)
                dst = mixT[:, 4:6, kc * 512:(kc + 1) * 512].rearrange("p c (j two) -> p c j two", two=2)[:, :, :, par]
                pend_f = fourier_out1(pb, 256, dst)
        if pend_f is not None:
            fourier_out2(*pend_f)
            pend_f = None
        if not last:
            for k0 in range(0, NC, 256):
                Wk = min(256, NC - k0)
                pb = PS(pbr3.next())
                for c_ in range(2):
                    for nb in range(NCB):
                        mm(pb[:, c_ * 256:c_ * 256 + Wk], AB[:, NB + nb, c_ * 256:c_ * 256 + 128], dcc[:, nb, k0:k0 + Wk], start=(nb == 0), stop=False)
                        mm(pb[:, c_ * 256:c_ * 256 + Wk], AB[:, NB + nb, c_ * 256 + 128:c_ * 256 + 256], dcs[:, nb, k0:k0 + Wk], start=False, stop=(nb == NCB - 1))
                fourier_out(pb, Wk, N + k0)

        DG_O = WIN_O + 16384
        diag = AV(DG_O, (CONV_K, 2, 128), BF16)
        for j in range(CONV_K):
            for c_ in range(2):
                r = vb + R_CW + j * 2 + c_
                ts("dve", diag[:, j, c_, :], ident_bf, vT[:, r:r + 1], ALU.mult)
        C_O = 51200
        zcs = _Rot([AV(C_O + i * 2048, (512,), F32) for i in range(3)])
        cens = _Rot([AV(C_O + 6144 + i * 2048, (512,), F32) for i in range(3)])
        sqs = _Rot([AV(C_O + 12288 + i * 2048, (512,), F32) for i in range(3)])
        rsd = _Rot([AV(C_O + 18432 + i * 2048, (512,), F32) for i in range(2)])
        sils = _Rot([AV(E_O + 4096 + i * 2048, (2, 512), BF16) for i in range(2)])
        segs = [(zl, t0, W, t0) for (t0, W) in lat_tiles]
        if not last:
            segs.append((zc_, 0, NC, N))
        items = []
        for si, (zz, s0, W, tok0) in enumerate(segs):
            sil = sils.next()
            for c_ in range(2):
                items.append(dict(zz=zz, s0=s0, W=W, tok0=tok0, c=c_, sil=sil))

        def cv_A(it):
            W = it["W"]; c_ = it["c"]
            pb = PS(pbr3.next())
            for j in range(CONV_K):
                mm(pb[:, 0:W], diag[:, j, c_, :], it["zz"][:, c_, it["s0"] + j:it["s0"] + j + W], start=(j == 0), stop=(j == CONV_K - 1))
            it["zcv"] = zcs.next()
            act(it["zcv"][:, 0:W], pb[:, 0:W], AF.Identity, bias=vT[:, vb + R_CB + c_: vb + R_CB + c_ + 1])

        def cv_B(it):
            W = it["W"]
            pm = PS(pbr3.next())
            mm(pm[:, 0:W], gln_f, it["zcv"][:, 0:W])
            it["cen"] = cens.next(); it["sq"] = sqs.next()
            tt("dve", it["cen"][:, 0:W], it["zcv"][:, 0:W], pm[:, 0:W], ALU.subtract)
            act(it["sq"][:, 0:W], it["cen"][:, 0:W], AF.Square)

        def cv_C(it):
            W = it["W"]; c_ = it["c"]
            pv_ = PS(pbr3.next())
            mm(pv_[:, 0:W], gln_f, it["sq"][:, 0:W])
            rs_ = rsd.next()
            act(rs_[:, 0:W], pv_[:, 0:W], AF.Ln, bias=epsT[:, 0:1])
            act(rs_[:, 0:W], rs_[:, 0:W], AF.Exp, scale=-0.5)
            zn = it["sq"]
            tt("dve", zn[:, 0:W], it["cen"][:, 0:W], rs_[:, 0:W], ALU.mult)
            act(it["sil"][:, c_, 0:W], zn[:, 0:W], AF.Silu, scale=vT[:, vb + R_LG + c_: vb + R_LG + c_ + 1], bias=vT[:, vb + R_LB + c_: vb + R_LB + c_ + 1])

        def cv_D(it):
            W = it["W"]
            for dc in range(2):
                pb = PS(pbr3.next())
                for c_ in range(2):
                    mm(pb[:, 0:W], wpw2[:, c_, dc * 128:(dc + 1) * 128], it["sil"][:, c_, 0:W], start=(c_ == 0), stop=(c_ == 1))
                cp(cpr.next(), mixT[:, 6 + dc, it["tok0"]:it["tok0"] + W], pb[:, 0:W])

        ni = len(items)
        for i in range(ni + 4):
            if i < ni:
                cv_A(items[i])
            if 0 <= i - 1 < ni:
                cv_B(items[i - 1])
            if 0 <= i - 2 < ni:
                cv_C(items[i - 2])
            if 0 <= i - 3 < ni and items[i - 3]["c"] == 1:
                cv_D(items[i - 3])

        FW_O = WIN_O
        XIN5 = FW_O + 24576
        SQ5 = XIN5 + 16384
        T5 = 176128
        rstd5 = AV(T5, (512,), F32); sqrtt5 = AV(T5 + 2048, (512,), F32)
        tmps5 = _Rot([AV(T5 + 4096, (512,), F32), AV(T5 + 6144, (512,), F32)])
        sq5 = AV(SQ5, (KD, 512), BF16)
        tiles5 = all_tiles if not last else lat_tiles
        pend5 = None
        for (t0, W) in tiles5:
            col = 1 if t0 >= N else 0
            xin = AV(XIN5, (KD, W), F32)
            for m_ in range(KD):
                dma("sp", xin[:, m_, :], xs_tile(t0, W)[:, m_, :], "xin%d" % (m_ % 2))
            if pend5 is not None:
                norm_sq(XRES[:, :, pend5[0]:pend5[0] + pend5[1]], pend5[1], sq5)
            for m_ in range(KD):
                pb = PS(pbr3.next())
                for k in range(KD):
                    mm(pb[:, 0:W], w_out[:, k, m_ * 128:(m_ + 1) * 128], mixT[:, k, t0:t0 + W], start=(k == 0), stop=(k == KD - 1))
                stt(XRES[:, m_, t0:t0 + W], pb[:, 0:W], modT[:, l, 2 * 8 + m_, col:col + 1], xin[:, m_, :], ALU.mult, ALU.add)
                if m_ == 3 and pend5 is not None:
                    pt0, pW, pcol = pend5
                    norm_rest(l, 1, 3, XRES[:, :, pt0:pt0 + pW], pW, pcol, mixT[:, :, pt0:pt0 + pW], sq5, rstd5, sqrtt5, tmps5, bank=pbr3.next())
                    pend5 = None
            pend5 = (t0, W, col)
        pt0, pW, pcol = pend5
        norm_sq(XRES[:, :, pt0:pt0 + pW], pW, sq5)
        norm_rest(l, 1, 3, XRES[:, :, pt0:pt0 + pW], pW, pcol, mixT[:, :, pt0:pt0 + pW], sq5, rstd5, sqrtt5, tmps5, bank=pbr3.next())
        h2T = mixT

        G = 4
        groups = [(g0, min(G, NFF - g0)) for g0 in range(0, NFF, G)]
        if groups[-1][1] < G:
            groups = [groups[-1]] + groups[:-1]
        fwb = [(AV(FW_O + i * 24576, (KD, 512), BF16), AV(FW_O + i * 24576 + 8192, (KD, 512), BF16), AV(FW_O + i * 24576 + 16384, (G, D), BF16)) for i in range(2)]
        ACT_O = 159744
        actb = _Rot([AV(ACT_O + i * 4096, (G, 512), BF16) for i in range(2)])
        sgs = _Rot([AV(ACT_O + 8192 + i * 2048, (512,), F32) for i in range(2)])
        upb = _Rot([0, 1, 2, 3]); dnb = _Rot([4, 5, 6, 7])
        pending = None

        def down(g0, gn, t0, W, ab, w2b, col, is_last_group=False):
            for m_ in range(KD):
                pb = PS(dnb.next())
                for jj in range(gn):
                    mm(pb[:, 0:W], w2b[:, jj, m_ * 128:(m_ + 1) * 128], ab[:, jj, 0:W], start=(jj == 0), stop=(jj == gn - 1))
                stt(XRES[:, m_, t0:t0 + W], pb[:, 0:W], modT[:, l, 5 * 8 + m_, col:col + 1], XRES[:, m_, t0:t0 + W], ALU.mult, ALU.add)

        for gi, (g0, gn) in enumerate(groups):
            w1b, w3b, w2b = fwb[gi % 2]
            pass
            cw = gn * 128
            dma("pool", w1b[:, :, 0:cw], w1_d[l, :, g0 * 128:g0 * 128 + cw].rearrange("(k p) n -> p k n", p=128), "fw1_%d" % (gi % 2))
            dma("pool", w3b[:, :, 0:cw], w3_d[l, :, g0 * 128:g0 * 128 + cw].rearrange("(k p) n -> p k n", p=128), "fw3_%d" % (gi % 2))
            dma("pool", w2b[:, 0:gn, :], w2_d[l, g0 * 128:g0 * 128 + cw, :].rearrange("(j p) n -> p j n", p=128), "fw2_%d" % (gi % 2))
            for (t0, W) in tiles5:
                col = 1 if t0 >= N else 0
                ab = actb.next()
                for jj in range(gn):
                    pa = PS(upb.next()); pg = PS(upb.next())
                    for k in range(KD):
                        mm(pa[:, 0:W], w1b[:, k, jj * 128:(jj + 1) * 128], h2T[:, k, t0:t0 + W], start=(k == 0), stop=(k == KD - 1))
                    for k in range(KD):
                        mm(pg[:, 0:W], w3b[:, k, jj * 128:(jj + 1) * 128], h2T[:, k, t0:t0 + W], start=(k == 0), stop=(k == KD - 1))
                    sg = sgs.next()
                    act(sg[:, 0:W], pa[:, 0:W], AF.Silu)
                    tt("dve", ab[:, jj, 0:W], pg[:, 0:W], sg[:, 0:W], ALU.mult)
                if pending is not None:
                    down(*pending)
                    if pending[-1] and not last:
                        pt0, pW = pending[2], pending[3]
                        dma("sp", xs_tile(pt0, pW), XRES[:, :, pt0:pt0 + pW], "xst%d" % ((pt0 // 512) % 2))
                pending = (g0, gn, t0, W, ab, w2b, col, gi == len(groups) - 1)
        down(*pending)
        if not last:
            pt0, pW = pending[2], pending[3]
            dma("sp", xs_tile(pt0, pW), XRES[:, :, pt0:pt0 + pW], "xst%d" % ((pt0 // 512) % 2))
        pending = None

    FB = 73728
    fgr = DEPTH * VROWS
    gbc = AV(FB, (D,), F32)
    junkF = AV(FB + 4096, (512,), F32)
    otl = [AV(FB + 6144 + i * 4096, (D,), F32) for i in range(3)]
    ssqF = ST("ssqF", [128, NB, 4], F32)
    dma("sp", gbc, vecs_d[fgr:fgr + 8, :].rearrange("a b -> (a b)").partition_broadcast(128), "c0")
    pbrF = _Rot(range(8))

    def fin_T(blk):
        banks = []
        for kq in range(2):
            pb = PS(pbrF.next())
            for kk in range(4):
                tr(pb[:, kk * 128:(kk + 1) * 128], XRES[:, kq * 4 + kk, blk * 128:(blk + 1) * 128], ident_f)
            act(junkF, pb[:, :], AF.Square, accum=ssqF[:, blk, kq:kq + 1])
            banks.append(pb)
        return banks

    def fin_E(blk, banks):
        tt("dve", ssqF[:, blk, 2:3], ssqF[:, blk, 0:1], ssqF[:, blk, 1:2], ALU.add)
        ts("dve", ssqF[:, blk, 2:3], ssqF[:, blk, 2:3], 1.0 / D, ALU.mult, EPS, ALU.add)
        tt("pool", ssqF[:, blk, 3:4], ssqF[:, blk, 2:3], small[:, 0:1], ALU.pow)
        ot = otl[blk % 3]
        for kq in range(2):
            stt(ot[:, kq * 512:(kq + 1) * 512], banks[kq][:, :], ssqF[:, blk, 3:4], gbc[:, kq * 512:(kq + 1) * 512], ALU.mult, ALU.mult)
        dma("sp", out_d[blk * 128:(blk + 1) * 128, :], ot, "ost%d" % (blk % 3))

    prevF = None
    for blk in range(NB):
        banks = fin_T(blk)
        if prevF is not None:
            fin_E(*prevF)
        prevF = (blk, banks)
    fin_E(*prevF)

    P.finalize(None)
    streams = []
    for o in P.ops:
        s = P._stream(o)
        if s not in streams:
            streams.append(s)
    sems = {s: es.enter_context(nc.semaphore("sem_%s_%s" % s)) for s in streams}
    out_slots = [s for s in streams if s[0] == "slot" and s[1].startswith("ost")]
    block = es.enter_context(nc.Block())

    @block.tensor
    def _(e):
        P.emit_engine("pe", e, sems)

    @block.scalar
    def _(e):
        P.emit_engine("act", e, sems)

    @block.vector
    def _(e):
        P.emit_engine("dve", e, sems)

    @block.gpsimd
    def _(e):
        P.emit_engine("pool", e, sems)

    @block.sync
    def _(e):
        P.emit_engine("sp", e, sems)
        for s in out_slots:
            e.wait_ge(sems[s], P.max_vals[s])

    es.close()
    return nc, P


def _const_tables(N, NC):
    bf = ml_dtypes.bfloat16
    rows = N // GRID_W
    t = np.arange(N)
    row = (t // GRID_W).astype(np.float64)
    colp = (t % GRID_W).astype(np.float64)
    inv_freq = ROPE_BASE ** (-np.arange(16, dtype=np.float64) / 16)
    ropec = np.zeros((128, N)); ropes = np.zeros((128, N))
    perm = np.zeros((128, 128))
    for p in range(128):
        d = p % 64
        axis, half, f = d // 32, (d % 32) // 16, d % 16
        ang = (row if axis == 0 else colp) * inv_freq[f]
        ropec[p] = np.cos(ang)
        ropes[p] = -np.sin(ang) if half == 0 else np.sin(ang)
        partner = p + 16 if half == 0 else p - 16
        perm[partner, p] = 1.0
    ident = np.eye(128)
    ones = np.ones((128, 128))
    cc = np.arange(64)
    a64 = 2 * np.pi * np.outer(cc, cc) / 64
    c64 = np.cos(a64) / 8.0; s64 = np.sin(a64) / 8.0
    c64bd = np.zeros((128, 128)); s64bd = np.zeros((128, 128))
    gln = np.zeros((128, 128))
    for g in range(2):
        c64bd[g * 64:(g + 1) * 64, g * 64:(g + 1) * 64] = c64
        s64bd[g * 64:(g + 1) * 64, g * 64:(g + 1) * 64] = s64
        gln[g * 64:(g + 1) * 64, g * 64:(g + 1) * 64] = 1.0 / 64
    cmat = np.stack([ident, ones, perm, c64bd, s64bd], axis=1).astype(bf)
    cmatf = np.stack([ident, gln], axis=1).astype(np.float32)

    NB = N // 128
    NHt = N // 2
    n_ = np.arange(NHt)
    kp = np.arange(NHt)
    ang_e = 2 * np.pi * (np.outer(n_, kp) % NHt) / NHt
    ang_o = 2 * np.pi * (np.outer(n_, 2 * kp + 1) % N) / N
    sc_ = 1.0 / np.sqrt(N)

    def lay(Mx):
        return np.ascontiguousarray(Mx.reshape(NB // 2, 128, NHt // 256, 256).transpose(2, 1, 0, 3))

    dftc_h = np.stack([lay(np.cos(ang_e) * sc_), lay(np.cos(ang_o) * sc_)], axis=0).astype(bf)
    dfts_h = np.stack([lay(-np.sin(ang_e) * sc_), lay(-np.sin(ang_o) * sc_)], axis=0).astype(bf)

    def dft(M):
        n = np.arange(M)
        ang = 2 * np.pi * (np.outer(n, n) % M) / M
        return np.cos(ang) / np.sqrt(M), -np.sin(ang) / np.sqrt(M)

    Cc, Sc = dft(NC)
    NCB = NC // 128

    def layc(Mx):
        return np.ascontiguousarray(Mx.reshape(NCB, 128, NC).transpose(1, 0, 2)).astype(bf)

    return dict(ropec=ropec.astype(bf), ropes=ropes.astype(bf), cmat=cmat, cmatf=cmatf,
                dftc=dftc_h, dfts=dfts_h, dcc=layc(Cc), dcs=layc(Sc))


def _pack_vecs(inp, DEPTH):
    vecs = np.zeros((384, 128), np.float32)
    for l in range(DEPTH):
        b = l * VROWS
        vecs[b + R_BADA:b + R_BADA + 48] = np.asarray(inp["b_ada"][l], np.float32).reshape(48, 128)
        vecs[b + R_N1:b + R_N1 + 8] = np.asarray(inp["norm1_g"][l], np.float32).reshape(8, 128)
        vecs[b + R_N2:b + R_N2 + 8] = np.asarray(inp["norm2_g"][l], np.float32).reshape(8, 128)
        vecs[b + R_CB:b + R_CB + 2] = np.asarray(inp["conv_b"][l], np.float32).reshape(2, 128)
        vecs[b + R_LG:b + R_LG + 2] = np.asarray(inp["conv_ln_g"][l], np.float32).reshape(2, 128)
        vecs[b + R_LB:b + R_LB + 2] = np.asarray(inp["conv_ln_b"][l], np.float32).reshape(2, 128)
        vecs[b + R_CW:b + R_CW + 62] = np.asarray(inp["conv_w"][l], np.float32).reshape(62, 128)
    vecs[DEPTH * VROWS:DEPTH * VROWS + 8] = np.asarray(inp["final_g"], np.float32).reshape(8, 128)
    return vecs


def make_in_maps(inp, N, NC, DEPTH, B):
    f = lambda a: np.ascontiguousarray(np.asarray(a, np.float32))
    consts = _const_tables(N, NC)
    shared = dict(consts)
    shared["vecs"] = _pack_vecs(inp, DEPTH)
    for k in ("w_ada", "w_in", "subln_g", "w_fourier", "w_conv_out", "w_out", "w_ffn1", "w_ffn3", "w_ffn2"):
        shared[k] = f(inp[k])
    shared["lamv"] = np.ascontiguousarray(np.stack([f(inp["lam_q1"]), f(inp["lam_k1"]), f(inp["lam_q2"]), f(inp["lam_k2"])], axis=1))
    x = f(inp["x"]); ctx = f(inp["ctx"]); c = f(inp["c"]); c_ctx = f(inp["c_ctx"])
    maps = []
    for b in range(B):
        m = dict(shared)
        m["x"] = x[b]; m["ctx"] = ctx[b]
        m["cc"] = np.ascontiguousarray(np.stack([c[b], c_ctx], axis=1))
        maps.append(m)
    return maps


_CACHE = {}


def kernel(**inputs):
    N, NC, DEPTH, B = 2048, 256, 2, 8
    if "nc" not in _CACHE:
        _CACHE["nc"] = build_program(N, NC, DEPTH)[0]
    nc = _CACHE["nc"]
    maps = make_in_maps(inputs, N, NC, DEPTH, B)
    res = run_bass_kernel_spmd(nc, maps, core_ids=list(range(B)))
    return np.stack([np.asarray(r["out"], np.float32) for r in res.results], axis=0)
```

```python
import numpy as np
import ml_dtypes
import concourse.bass as bass
import concourse.mybir as mybir
from concourse.bass_utils import run_bass_kernel_spmd

F32 = mybir.dt.float32
BF16 = mybir.dt.bfloat16
AF = mybir.ActivationFunctionType
ALU = mybir.AluOpType


class _Op:
    __slots__ = ("eng", "fn", "deps", "sem", "val", "signal", "slot", "group", "idx", "dbg")


def _box(ap):
    t = ap.tensor
    shp = tuple(t.shape)
    pat = ap.ap
    off = int(ap.offset)
    sp = str(ap.space)
    esz = mybir.dt.size(ap.dtype)
    if sp == "DRAM":
        lo = off * esz
        hi = (off + sum((c - 1) * abs(s) for s, c in pat) + 1) * esz - 1
        return (ap.name, 0, 0, lo, hi, False)
    pstride = 1
    for d in shp[1:]:
        pstride *= d
    plo = off // pstride
    flo = off % pstride
    assert pat[0][0] == pstride or pat[0][1] == 1, (pat, pstride)
    phi = plo + pat[0][1] - 1
    fhi = flo + sum((c - 1) * abs(s) for s, c in pat[1:])
    lo = flo * esz
    hi = (fhi + 1) * esz - 1
    if sp == "PSUM":
        lo = (lo // 2048) * 2048
        hi = (hi // 2048) * 2048 + 2047
        return (ap.name + "@" + sp, 0, 127, lo, hi, True)
    return (ap.name + "@" + sp, plo, phi, lo, hi, False)


class Prog:
    ENGS = ("pe", "act", "dve", "pool", "sp")

    def __init__(self, nc):
        self.nc = nc
        self.ops = []
        self.track = {}
        self.slot_groups = {}
        self.slot_cur = {}

    def add(self, eng, fn, reads=(), writes=(), slot=None, group=None):
        op = _Op()
        op.eng = eng
        op.fn = fn
        op.deps = set()
        op.signal = False
        op.slot = slot
        op.idx = len(self.ops)
        op.group = None
        op.dbg = None
        if DEBUG_LINES:
            import sys as _sys
            fr = _sys._getframe(2)
            op.dbg = (fr.f_lineno, fr.f_back.f_lineno if fr.f_back else None)
        if slot is not None:
            groups = self.slot_groups.setdefault(slot, [])
            if groups and self.slot_cur.get(slot) == group and group is not None:
                groups[-1].append(op.idx)
            else:
                if groups:
                    op.deps.add(groups[-1][-1])
                groups.append([op.idx])
                self.slot_cur[slot] = group
            op.group = (slot, len(groups) - 1)
        for ap in reads:
            self._access(op, ap, False)
        for ap in writes:
            self._access(op, ap, True)
        self.ops.append(op)
        return op

    def _same_stream(self, a, b):
        if a.slot is not None or b.slot is not None:
            return a.slot is not None and a.slot == b.slot
        return a.eng == b.eng

    def _access(self, op, ap, is_write):
        key, plo, phi, lo, hi, excl = _box(ap)
        is_write = is_write or excl
        lst = self.track.setdefault(key, [])
        keep = []
        for rec in lst:
            rplo, rphi, rlo, rhi, ridx, rw = rec
            overlap = not (rphi < plo or rplo > phi or rhi < lo or rlo > hi)
            if overlap and (is_write or rw) and ridx != op.idx:
                op.deps.add(ridx)
            contained = rplo >= plo and rphi <= phi and rlo >= lo and rhi <= hi
            if contained and ridx != op.idx:
                if is_write:
                    continue
                if (not rw) and self._same_stream(self.ops[ridx], op):
                    continue
            keep.append(rec)
        keep.append([plo, phi, lo, hi, op.idx, is_write])
        self.track[key] = keep

    def finalize(self, block_sems):
        ops = self.ops
        def stream(o):
            return ("slot", o.slot) if o.slot is not None else ("eng", o.eng)
        pos = {}
        cnt = {}
        for o in ops:
            s = stream(o)
            cnt[s] = cnt.get(s, 0) + 1
            pos[o.idx] = cnt[s]
        def target(j):
            o = ops[j]
            if o.slot is not None:
                g = self.slot_groups[o.slot][o.group[1]]
                return g[-1]
            return j
        waited = {e: {} for e in self.ENGS}
        need = {}
        for o in ops:
            w = waited[o.eng]
            req = {}
            for j in o.deps:
                tj = target(j)
                if tj == o.idx:
                    continue
                if tj > o.idx:
                    assert ops[tj].slot is not None and ops[tj].group == o.group, "dep on future op"
                    continue
                s = stream(ops[tj])
                if s == ("eng", o.eng) and o.slot is None and not SAME_ENGINE_SYNC:
                    continue
                if s == ("eng", o.eng) and o.eng == "pe" and o.slot is None:
                    continue
                if s == ("eng", o.eng) and o.slot is not None and ops[tj].slot is None:
                    pass
                p = pos[tj]
                if w.get(s, 0) >= p:
                    continue
                if req.get(s, (0, None))[0] < p:
                    req[s] = (p, tj)
            lst = []
            for s, (p, tj) in req.items():
                w[s] = p
                lst.append((s, tj))
                ops[tj].signal = True
            need[o.idx] = lst
        for o in ops:
            if o.slot is not None:
                o.signal = True
        val = {}
        c = {}
        for o in ops:
            s = stream(o)
            if o.signal:
                c[s] = c.get(s, 0) + (16 if o.slot is not None else 1)
            val[o.idx] = c.get(s, 0)
        self._need, self._val, self._stream = need, val, stream
        self.max_vals = dict(c)
        return need, val

    def emit(self, engines, sems):
        need, val, stream = self._need, self._val, self._stream
        for o in self.ops:
            e = engines[o.eng]
            for s, tj in need[o.idx]:
                e.wait_ge(sems[s], val[tj])
            ins = o.fn(e)
            if o.signal:
                ins.then_inc(sems[stream(o)], 16 if o.slot is not None else 1)

    def emit_engine(self, eng_name, e, sems):
        need, val, stream = self._need, self._val, self._stream
        for o in self.ops:
            if o.eng != eng_name:
                continue
            for s, tj in need[o.idx]:
                e.wait_ge(sems[s], val[tj])
            ins = o.fn(e)
            if DEBUG_LINES:
                NAMES[getattr(ins.ins, "name", None)] = (o.idx, o.dbg)
            if o.signal:
                ins.then_inc(sems[stream(o)], 16 if o.slot is not None else 1)


SAME_ENGINE_SYNC = True
DEBUG_LINES = False
NAMES = {}


D = 1024
KD = D // 128
GRID_W = 64
INW = 2304
DFF = 2816
NFF = DFF // 128
CONV_K = 31
EPS = 1e-6
ROPE_BASE = 10000.0
VROWS = 132
R_BADA, R_N1, R_N2, R_CB, R_LG, R_LB, R_CW = 0, 48, 56, 64, 66, 68, 70


class _Rot:
    def __init__(self, items):
        self.items = list(items)
        self.i = 0

    def next(self):
        v = self.items[self.i % len(self.items)]
        self.i += 1
        return v


def build_program(N, NC, DEPTH):
    T = N + NC
    NB = N // 128
    NCB = NC // 128
    TB = NB + NCB
    lat_tiles = [(t0, 512) for t0 in range(0, N, 512)]
    all_tiles = lat_tiles + [(N, NC)]
    nc = bass.Bass("TRN2", target_bir_lowering=False)
    P = Prog(nc)

    def din(name, shape, dt=F32):
        return nc.dram_tensor(name, list(shape), dt, kind="ExternalInput").ap()

    x_d = din("x", [N, D]); ctx_d = din("ctx", [NC, D]); cc_d = din("cc", [D, 2])
    w_ada_d = din("w_ada", [DEPTH, D, 6 * D]); vecs_d = din("vecs", [384, 128])
    w_in_d = din("w_in", [DEPTH, D, INW]); lamv_d = din("lamv", [DEPTH, 4, 64])
    subln_d = din("subln_g", [DEPTH, 128]); wf_d = din("w_fourier", [DEPTH, 4, 64, 64])
    wco_d = din("w_conv_out", [DEPTH, 256, 256]); w_out_d = din("w_out", [DEPTH, D, D])
    w1_d = din("w_ffn1", [DEPTH, D, DFF]); w3_d = din("w_ffn3", [DEPTH, D, DFF]); w2_d = din("w_ffn2", [DEPTH, DFF, D])
    ropec_d = din("ropec", [128, N], BF16); ropes_d = din("ropes", [128, N], BF16)
    NKC = N // 256
    dftc_d = din("dftc", [2, NKC // 2, 128, NB // 2, 256], BF16); dfts_d = din("dfts", [2, NKC // 2, 128, NB // 2, 256], BF16)
    dcc_d = din("dcc", [128, NCB, NC], BF16); dcs_d = din("dcs", [128, NCB, NC], BF16)
    cmat_d = din("cmat", [128, 5, 128], BF16)
    cmatf_d = din("cmatf", [128, 2, 128], F32)
    out_d = nc.dram_tensor("out", [N, D], F32, kind="ExternalOutput").ap()
    NT_ALL = len(all_tiles)
    xs_d = nc.dram_tensor("xs", [NT_ALL, 128, KD, 512], F32).ap()

    def xs_tile(t0, W):
        ti_ = (t0 // 512) if t0 < N else (NT_ALL - 1)
        return xs_d[ti_, :, :, 0:W]
    modrow_d = nc.dram_tensor("modrow", [DEPTH, 2, 6 * D], F32).ap()

    from contextlib import ExitStack
    es = ExitStack()
    ARENA = 184320
    arena = es.enter_context(nc.sbuf_tensor("arena", [128, ARENA // 4], F32))
    psum = es.enter_context(nc.psum_tensor("ps", [128, 8, 512], F32))

    def AV(off, shape, dt, parts=128):
        esz = mybir.dt.size(dt)
        n = 1
        for d_ in shape:
            n *= d_
        assert off % 4 == 0 and off + n * esz <= ARENA, (off, shape)
        a = arena[:, off // 4:(off + n * esz + 3) // 4]
        if dt != F32:
            a = a.bitcast(dt)
        a = a[0:parts, 0:n]
        if len(shape) == 2:
            a = a.rearrange("p (a b) -> p a b", b=shape[1])
        elif len(shape) == 3:
            a = a.rearrange("p (a b c) -> p a b c", b=shape[1], c=shape[2])
        elif len(shape) == 4:
            a = a.rearrange("p (a b c d) -> p a b c d", b=shape[1], c=shape[2], d=shape[3])
        return a

    def PS(bank, dt=F32):
        a = psum[:, bank, :]
        return a if dt == F32 else a.bitcast(dt)

    def ST(name, shape, dt):
        return es.enter_context(nc.sbuf_tensor("s_" + name, list(shape), dt))

    cmat = ST("cmat", [128, 5, 128], BF16); cmatf = ST("cmatf", [128, 2, 128], F32)
    ident_bf, ones_bf, perm_bf = cmat[:, 0, :], cmat[:, 1, :], cmat[:, 2, :]
    ident_f, gln_f = cmatf[:, 0, :], cmatf[:, 1, :]
    ropec = ST("ropec", [128, N], BF16); ropes = ST("ropes", [128, N], BF16)
    vT = ST("vT", [128, 384], F32)
    modT = ST("modT", [128, DEPTH, 48, 2], F32)
    gm = ST("gm", [128, DEPTH, 2, KD, 2], F32)
    dcc = ST("dcc", [128, NCB, NC], BF16); dcs = ST("dcs", [128, NCB, NC], BF16)
    gcol = ST("gcol", [128, DEPTH], F32)
    lamb = ST("lamb", [128, DEPTH, 4, 64], F32); lamt = ST("lamt", [128, 2, 64], F32)
    lams = ST("lams", [128, 8], F32); neglam = ST("neglam", [128, DEPTH], F32)
    sc_f = ST("sc_f", [128, KD, 2], F32); sc_bf = ST("sc_bf", [128, KD, 2], BF16)
    small = ST("small", [128, 64], F32)
    wfbd = ST("wfbd", [128, 2, 128], BF16); wfst = ST("wfst", [128, 2, 128], F32)
    wpw2 = ST("wpw2", [128, 2, 256], BF16)
    epsT = ST("epsT", [128, 1], F32)

    def mm(out, lhsT, rhs, start=True, stop=True):
        P.add("pe", lambda e: e.matmul(out, lhsT=lhsT, rhs=rhs, start=start, stop=stop), [lhsT, rhs], [out])

    def tr(out, in_, ident):
        P.add("pe", lambda e: e.transpose(out, in_, ident), [in_, ident], [out])

    def act(out, in_, func, scale=1.0, bias=None, accum=None, eng="act"):
        rd = [in_] + [a for a in (scale, bias) if not isinstance(a, (int, float, type(None)))]
        wr = [out] + ([accum] if accum is not None else [])
        kw = {}
        if bias is not None:
            kw["bias"] = bias
        if accum is not None:
            kw["accum_out"] = accum
        P.add("act", lambda e: e.activation(out=out, in_=in_, func=func, scale=scale, **kw), rd, wr)

    def tt(eng, out, a, b, op):
        P.add(eng, lambda e: e.tensor_tensor(out=out, in0=a, in1=b, op=op), [a, b], [out])

    def ts(eng, out, a, s1, op0, s2=None, op1=None):
        rd = [a] + [s for s in (s1, s2) if not isinstance(s, (int, float, type(None)))]
        if op1 is None:
            P.add(eng, lambda e: e.tensor_scalar(out=out, in0=a, scalar1=s1, scalar2=None, op0=op0), rd, [out])
        else:
            P.add(eng, lambda e: e.tensor_scalar(out=out, in0=a, scalar1=s1, scalar2=s2, op0=op0, op1=op1), rd, [out])

    def stt(out, a, s, b, op0, op1):
        rd = [a, b] + ([s] if not isinstance(s, (int, float)) else [])
        P.add("dve", lambda e: e.scalar_tensor_tensor(out=out, in0=a, scalar=s, in1=b, op0=op0, op1=op1), rd, [out])

    def cp(eng, out, in_):
        if eng == "act":
            P.add("act", lambda e: e.activation(out=out, in_=in_, func=AF.Copy), [in_], [out])
        else:
            P.add(eng, lambda e: e.tensor_copy(out=out, in_=in_), [in_], [out])

    def recip(out, in_):
        P.add("dve", lambda e: e.reciprocal(out=out, in_=in_), [in_], [out])

    def mset(eng, ap, val):
        P.add(eng, lambda e: e.memset(ap, val), [], [ap])

    def dma(eng, out, in_, slot, group=None):
        P.add(eng, lambda e: e.dma_start(out=out, in_=in_), [in_], [out], slot=slot, group=group)

    cpr = _Rot(["act", "dve"])

    dma("sp", cmat[:], cmat_d, "c0"); dma("sp", cmatf[:], cmatf_d, "c1")
    dma("sp", ropec[:], ropec_d, "c2"); dma("sp", ropes[:], ropes_d, "c3")
    dma("sp", dcc[:], dcc_d, "c0"); dma("sp", dcs[:], dcs_d, "c1")
    mset("dve", epsT[:], EPS)
    mset("dve", small[:, 0:8], -0.5)
    vst = AV(0, (3, 128), F32)
    dma("sp", vst, vecs_d.rearrange("(j p) f -> p j f", p=128), "c2")
    for j in range(3):
        tr(PS(0)[:, j * 128:(j + 1) * 128], vst[:, j, :], ident_f)
    cp("dve", vT[:], PS(0)[:, 0:384])
    for l in range(DEPTH):
        dma("sp", lamb[:, l], lamv_d[l].partition_broadcast(128), "c3")
        dma("sp", gcol[:, l:l + 1], subln_d[l].rearrange("(p o) -> p o", o=1), "c0")
        lam_init = 0.8 - 0.6 * float(np.exp(-0.3 * l))
        tt("dve", lamt[:, 0, :], lamb[:, l, 0, :], lamb[:, l, 1, :], ALU.mult)
        tt("dve", lamt[:, 1, :], lamb[:, l, 2, :], lamb[:, l, 3, :], ALU.mult)
        P.add("dve", lambda e: e.tensor_reduce(out=lams[:, 0:2], in_=lamt[:], axis=mybir.AxisListType.X, op=ALU.add), [lamt[:]], [lams[:, 0:2]])
        act(lams[:, 2:4], lams[:, 0:2], AF.Exp)
        tt("dve", lams[:, 4:5], lams[:, 3:4], lams[:, 2:3], ALU.subtract)
        ts("dve", neglam[:, l:l + 1], lams[:, 4:5], -lam_init, ALU.add)
        ts("dve", gcol[:, l:l + 1], gcol[:, l:l + 1], 1.0 - lam_init, ALU.mult)
    dma("sp", sc_f[:], cc_d.rearrange("(k p) t -> p k t", p=128), "c1")
    act(sc_bf[:], sc_f[:], AF.Silu)
    WA_O = 64800
    wa_bufs = [AV(WA_O + i * 4096, (KD, 256), BF16) for i in range(2)]
    mstage_t = ST("mstage", [2, 256], F32)
    mstage = mstage_t[:, :]
    mrow_t = ST("mrow", [96, 128], F32)
    mrow = mrow_t[:, :]
    pbr = _Rot(range(1, 8))
    all_pieces = [(0, pc) for pc in range(24)]
    for l_ in range(1, DEPTH):
        all_pieces += [(l_, pc) for pc in range(24)]
    mstate = {"dma": 0, "mm": 0}

    def mods_dma():
        i = mstate["dma"]
        if i >= len(all_pieces):
            return
        l_, pc = all_pieces[i]
        dma("pool", wa_bufs[i % 2], w_ada_d[l_, :, pc * 256:(pc + 1) * 256].rearrange("(k p) n -> p k n", p=128), "wa%d" % (i % 2))
        mstate["dma"] = i + 1

    def mods_piece(bank):
        i = mstate["mm"]
        while mstate["dma"] <= min(i + 1, len(all_pieces) - 1):
            mods_dma()
        l_, pc = all_pieces[i]
        wa = wa_bufs[i % 2]
        pb = PS(bank)
        for k in range(KD):
            mm(pb[0:2, 0:256], sc_bf[:, k, :], wa[:, k, :], start=(k == 0), stop=(k == KD - 1))
        cp("dve", mstage, pb[0:2, 0:256])
        dma("sp", modrow_d[l_, :, pc * 256:(pc + 1) * 256], mstage, "mst")
        mstate["mm"] = i + 1

    def mods_finish(l, s0, s1, bank):
        r0, r1 = s0 * 8, s1 * 8
        nr = r1 - r0
        for v in range(2):
            dma("sp", mrow[v * nr:(v + 1) * nr, :], modrow_d[l, v, r0 * 128:r1 * 128].rearrange("(r p) -> r p", p=128), "mld", group="mf%d_%d" % (l, s0))
        pb = PS(bank)
        tr(pb[:, 0:2 * nr], mrow[0:2 * nr, :], ident_f[0:2 * nr, 0:2 * nr])
        for v in range(2):
            tt("dve", modT[:, l, r0:r1, v], pb[:, v * nr:(v + 1) * nr], vT[:, l * VROWS + R_BADA + r0: l * VROWS + R_BADA + r1], ALU.add)
        for n_i, (sec_sc, rg) in enumerate(((1, R_N1), (4, R_N2))):
            if not (s0 <= sec_sc < s1):
                continue
            for k in range(KD):
                ts("dve", gm[:, l, n_i, k, :], modT[:, l, sec_sc * 8 + k, :], 1.0, ALU.add)
                ts("dve", gm[:, l, n_i, k, :], gm[:, l, n_i, k, :], vT[:, l * VROWS + rg + k: l * VROWS + rg + k + 1], ALU.mult)

    def load_w_in(l_):
        w_in_ = AV(110592, (KD, INW), BF16)
        if l_ == 0:
            order = [(hf, k) for hf in range(2) for k in range(KD)]
        else:
            order = [(hf, k) for hf in range(2) for k in range(5)] + [(hf, k) for hf in range(2) for k in range(5, KD)]
        for (hf, k) in order:
            dma("pool", w_in_[:, k, hf * 1152:(hf + 1) * 1152], w_in_d[l_, k * 128:(k + 1) * 128, hf * 1152:(hf + 1) * 1152],
                "win%d" % ((hf if l_ == 0 else (0 if k < 5 else 1))), group="l%d" % l_)

    load_w_in(0)

    def bg_mods(max_layer, bank):
        i = mstate["mm"]
        if i < len(all_pieces) and all_pieces[i][0] <= max_layer:
            mods_piece(bank)
            if i == 23:
                mods_finish(0, 2, 6, bank)
            elif i > 23 and (i + 1) % 24 == 0:
                mods_finish(all_pieces[i][0], 0, 6, bank)
            return True
        return False

    XIN = 90112
    xtoks = [AV(16384, (4, D), F32), AV(73728, (4, D), F32)]
    xins = [AV(XIN, (KD, 512), F32), AV(32768, (KD, 512), F32)]

    def x_load(ti):
        t0, W = all_tiles[ti]
        src = x_d[t0:t0 + W, :] if t0 < N else ctx_d[:, :]
        dma("sp", xtoks[ti % 2][:, 0:W // 128, :], src.rearrange("(s p) d -> p s d", p=128), "xl%d" % (ti % 2))

    x_load(0)
    mods_done = 0
    for ti, (t0, W) in enumerate(all_tiles):
        nsub = W // 128
        if ti + 1 < len(all_tiles):
            x_load(ti + 1)
        xtok = xtoks[ti % 2]
        xin = xins[ti % 2][:, :, 0:W]
        for k in range(KD):
            pb = PS(pbr.next())
            for s_ in range(nsub):
                tr(pb[:, s_ * 128:(s_ + 1) * 128], xtok[:, s_, k * 128:(k + 1) * 128], ident_f)
            cp(cpr.next(), xin[:, k, :], pb[:, 0:W])
        dma("sp", xs_tile(t0, W), xin, "xst%d" % ((t0 // 512) % 2))
        for _ in range(2):
            if mods_done < 8:
                mods_piece(pbr.next()); mods_done += 1
    while mods_done < 8:
        mods_piece(pbr.next()); mods_done += 1
    mods_finish(0, 0, 2, pbr.next())

    QT_O, KT_O, V_O, UF_O = 0, 18432, 36864, 55584
    MIX_O = 73728
    WIN_O = 110592
    Z_O = 147456
    QT = AV(QT_O, (4, T), BF16); KT = AV(KT_O, (4, T), BF16)
    V = AV(V_O, (TB, 4, 130), BF16)
    ufT = AV(UF_O, (2, T), BF16)
    mixT = AV(MIX_O, (KD, T), BF16)
    zl = AV(Z_O, (2, N + 30), BF16)
    zc_ = AV(Z_O + 2 * (N + 30) * 2, (2, NC + 30), BF16)
    XRES = AV(0, (KD, T), F32)

    def norm_sq(xin, W, sq):
        act(sq[:, :, 0:W], xin, AF.Square)

    def norm_rest(l, n_i, sec_sh, xin, W, col, hT, sq, rstd, sqrtt, tmps, bank=None):
        pb = PS(pbr.next() if bank is None else bank)
        for k in range(KD):
            mm(pb[:, 0:W], ones_bf, sq[:, k, 0:W], start=(k == 0), stop=(k == KD - 1))
        act(sqrtt[:, 0:W], pb[:, 0:W], AF.Ln, scale=1.0 / D, bias=epsT[:, 0:1])
        act(rstd[:, 0:W], sqrtt[:, 0:W], AF.Exp, scale=-0.5)
        for k in range(KD):
            tmp = tmps.next()
            stt(tmp[:, 0:W], xin[:, k, :], gm[:, l, n_i, k, col:col + 1], rstd[:, 0:W], ALU.mult, ALU.mult)
            act(hT[:, k, 0:W], tmp[:, 0:W], AF.Identity, bias=modT[:, l, sec_sh * 8 + k, col:col + 1])

    def norm_tile(l, n_i, sec_sh, xin, W, col, hT, sq, rstd, sqrtt, tmps):
        norm_sq(xin, W, sq)
        norm_rest(l, n_i, sec_sh, xin, W, col, hT, sq, rstd, sqrtt, tmps)

    for l in range(DEPTH):
        last = (l == DEPTH - 1)
        vb = l * VROWS
        w_in = AV(WIN_O, (KD, INW), BF16)
        if l > 0:
            load_w_in(l)
        mset("pool", wfst[:], 0.0)
        for g in range(4):
            c_, o_ = g // 2, (g % 2) * 64
            dma("sp", wfst[o_:o_ + 64, c_, o_:o_ + 64], wf_d[l, g], "c2", group="wf%d" % l)
        cp("dve", wfbd[:], wfst[:])
        dma("pool", wpw2[:], wco_d[l].rearrange("(c p) d -> p c d", p=128), "wpw")
        for zz, W_ in ((zl, N), (zc_, NC)):
            mset("pool", zz[:, :, 0:15], 0.0)
            mset("pool", zz[:, :, W_ + 15:W_ + 30], 0.0)
        sq = AV(MIX_O + 16384, (KD, 512), BF16)
        hTs = [AV(MIX_O + 24576, (KD, 512), BF16), AV(Z_O + 9472, (KD, 512), BF16)]
        rstd = AV(MIX_O + 32768, (512,), F32); sqrtt = AV(MIX_O + 34816, (512,), F32)
        TO = Z_O + 9472 + 8192
        tmpsA = _Rot([AV(TO, (512,), F32), AV(TO + 2048, (512,), F32)])
        qraws = _Rot([AV(TO + 4096, (512,), BF16), AV(TO + 5120, (512,), BF16)])
        t1s = _Rot([AV(TO + 6144, (512,), F32), AV(TO + 8192, (512,), F32)])
        t2s = _Rot([AV(TO + 10240, (512,), F32), AV(TO + 12288, (512,), F32)])
        sigs = _Rot([AV(TO + 14336, (512,), F32), AV(TO + 16384, (512,), F32)])
        def p1_load(ti):
            t0, W = all_tiles[ti]
            xin = AV(MIX_O, (KD, W), F32)
            dma("sp", xin, xs_tile(t0, W), "xin")

        def p1_norm(ti):
            t0, W = all_tiles[ti]
            col = 1 if t0 >= N else 0
            xin = AV(MIX_O, (KD, W), F32)
            norm_tile(l, 0, 0, xin, W, col, hTs[ti % 2], sq, rstd, sqrtt, tmpsA)

        def p1_norm_sq(ti):
            t0, W = all_tiles[ti]
            norm_sq(AV(MIX_O, (KD, W), F32), W, sq)

        def p1_norm_rest(ti):
            t0, W = all_tiles[ti]
            col = 1 if t0 >= N else 0
            norm_rest(l, 0, 0, AV(MIX_O, (KD, W), F32), W, col, hTs[ti % 2], sq, rstd, sqrtt, tmpsA)

        def rope_tail(qraw, t1, dst, t0, W):
            t2 = t2s.next()
            pb2 = PS(pbr.next())
            mm(pb2[:, 0:W], perm_bf, qraw[:, 0:W])
            tt("dve", t2[:, 0:W], pb2[:, 0:W], ropes[:, t0:t0 + W], ALU.mult)
            tt("pool", dst, t1[:, 0:W], t2[:, 0:W], ALU.add)

        nt_ = len(all_tiles)
        if l == 0:
            p1_load(0); p1_norm(0)
        else:
            t0_, W_ = all_tiles[0]
            norm_tile(l, 0, 0, XRES[:, :, t0_:t0_ + W_], W_, 0, hTs[0], sq, rstd, sqrtt, tmpsA)
        if nt_ > 1:
            p1_load(1)
        for ti, (t0, W) in enumerate(all_tiles):
            is_ctx = t0 >= N
            col = 1 if is_ctx else 0
            hT = hTs[ti % 2]
            pend_rope = None
            ctx_kv_only = is_ctx and last
            for ch in range(8):
                if ctx_kv_only and ch < 4:
                    continue
                pb = PS(pbr.next())
                for k in range(KD):
                    mm(pb[:, 0:W], w_in[:, k, ch * 128:(ch + 1) * 128], hT[:, k, 0:W], start=(k == 0), stop=(k == KD - 1))
                dst = (QT if ch < 4 else KT)[:, ch % 4, t0:t0 + W]
                if ch == 3 and ti + 1 < nt_:
                    p1_norm_sq(ti + 1)
                if is_ctx:
                    cp(cpr.next(), dst, pb[:, 0:W])
                else:
                    qraw = qraws.next(); t1 = t1s.next()
                    cp("act", qraw[:, 0:W], pb[:, 0:W])
                    tt("dve", t1[:, 0:W], pb[:, 0:W], ropec[:, t0:t0 + W], ALU.mult)
                    if pend_rope is not None:
                        rope_tail(*pend_rope)
                    pend_rope = (qraw, t1, dst, t0, W)
            bg_mods(l, pbr.next())
            if ti + 1 < nt_:
                p1_norm_rest(ti + 1)
                if ti + 2 < nt_:
                    p1_load(ti + 2)
            for s_ in range(W // 128):
                pb = PS(pbr.next())
                for k in range(KD):
                    mm(pb[:, :], hT[:, k, s_ * 128:(s_ + 1) * 128], w_in[:, k, 1024:1536], start=(k == 0), stop=(k == KD - 1))
                tb = (t0 // 128) + s_
                cp(cpr.next(), V[:, tb, :, 0:128], pb[:, :].rearrange("p (h d) -> p h d", d=128))
                if s_ == 0 and pend_rope is not None:
                    rope_tail(*pend_rope)
                    pend_rope = None
            if ctx_kv_only:
                bg_mods(l, pbr.next())
                continue
            for c_ in range(2):
                pb = PS(pbr.next())
                for k in range(KD):
                    mm(pb[:, 0:W], w_in[:, k, 1536 + c_ * 128:1536 + (c_ + 1) * 128], hT[:, k, 0:W], start=(k == 0), stop=(k == KD - 1))
                cp(cpr.next(), ufT[:, c_, t0:t0 + W], pb[:, 0:W])
            for c_ in range(2):
                pa = PS(pbr.next()); pg = PS(pbr.next())
                for k in range(KD):
                    mm(pa[:, 0:W], w_in[:, k, 1792 + c_ * 128:1792 + (c_ + 1) * 128], hT[:, k, 0:W], start=(k == 0), stop=(k == KD - 1))
                for k in range(KD):
                    mm(pg[:, 0:W], w_in[:, k, 2048 + c_ * 128:2048 + (c_ + 1) * 128], hT[:, k, 0:W], start=(k == 0), stop=(k == KD - 1))
                sg = sigs.next()
                act(sg[:, 0:W], pg[:, 0:W], AF.Sigmoid)
                zdst = zc_[:, c_, 15:15 + W] if is_ctx else zl[:, c_, 15 + t0:15 + t0 + W]
                tt("dve", zdst, pa[:, 0:W], sg[:, 0:W], ALU.mult)
            bg_mods(l, pbr.next())

        WOUT_O = 159744
        w_out = AV(WOUT_O, (KD, D), BF16)
        for k in range(KD):
            dma("pool", w_out[:, k, :], w_out_d[l, k * 128:(k + 1) * 128, :], "wout%d" % (k % 2), group="l%d" % l)

        E_O = WIN_O
        QT_W = 256
        Ebufs = _Rot([AV(E_O + 24576 + i * 1024, (512,), BF16) for i in range(4)])
        rrs = _Rot([AV(E_O + 3072 + i * 2048, (512,), F32) for i in range(2)])
        t12s = _Rot([AV(E_O + 7168 + i * 2048, (512,), F32) for i in range(2)])
        ofs = _Rot([AV(E_O + 11264 + i * 1024, (256,), F32) for i in range(3)])
        sqb = _Rot([AV(E_O + 14336 + i * 512, (256,), BF16) for i in range(3)])
        lnb = _Rot([AV(E_O + 15872 + i * 1024, (256,), F32) for i in range(2)])
        rsb = _Rot([AV(E_O + 17920 + i * 1024, (256,), F32) for i in range(2)])
        sbr = _Rot([0, 1, 6])
        ssr = _Rot([7])
        qsets = []
        for q0 in range(0, N, QT_W):
            qsets.append((q0, QT_W, list(range(TB))))
        if not last:
            for q0 in range(0, NC, QT_W):
                qsets.append((N + q0, min(QT_W, NC - q0), list(range(NB, TB))))

        Qz = [[AV(E_O + 20480 + (hpar * 2 + tp_) * 1024, (2, 256), BF16) for tp_ in range(2)] for hpar in range(2)]
        for hpar in range(2):
            for tp_ in range(2):
                mset("pool", Qz[hpar][tp_][:, :, :], 0.0)

        def att_Q(h, q0, QW, tpar):
            hp = slice((h % 2) * 64, (h % 2) * 64 + 64)
            c1, c2 = h // 2, 2 + h // 2
            qz = Qz[h % 2][tpar]
            cp("pool", qz[hp, 0, 0:QW], QT[hp, c1, q0:q0 + QW])
            cp("pool", qz[hp, 1, 0:QW], QT[hp, c2, q0:q0 + QW])

        def att_S(h, q0, QW, kb, tpar):
            c1, c2 = h // 2, 2 + h // 2
            qz = Qz[h % 2][tpar]
            sb = PS(sbr.next())
            mm(sb[:, 0:QW], KT[:, c1, kb * 128:(kb + 1) * 128], qz[:, 0, 0:QW])
            mm(sb[:, 256:256 + QW], KT[:, c2, kb * 128:(kb + 1) * 128], qz[:, 1, 0:QW])
            E = Ebufs.next()
            if QW == 256:
                act(E[:, 0:512], sb[:, 0:512], AF.Exp, scale=0.125)
            else:
                act(E[:, :].rearrange("p (m q) -> p m q", m=2)[:, :, 0:QW], sb[:, :].rearrange("p (m q) -> p m q", m=2)[:, :, 0:QW], AF.Exp, scale=0.125)
            return E

        def att_PV(h, QW, ki, nk, kb, E, par):
            ob = PS(2 + par); sk = PS(4 + par)
            f, l_ = (ki == 0), (ki == nk - 1)
            mm(ob[:, 0:QW], V[:, kb, h, 0:128], E[:, 0:QW], start=f, stop=False)
            mm(ob[:, 256:256 + QW], V[:, kb, h, 0:128], E[:, 256:256 + QW], start=False, stop=l_)
            mm(sk[:, 0:QW], ones_bf, E[:, 0:QW], start=f, stop=False)
            mm(sk[:, 256:256 + QW], ones_bf, E[:, 256:256 + QW], start=False, stop=l_)

        def att_fin_a(h, q0, QW, par):
            ob = PS(2 + par); sk = PS(4 + par)
            rr = rrs.next(); t12 = t12s.next(); of = ofs.next(); sq_ = sqb.next()
            if QW == 256:
                recip(rr[:, 0:512], sk[:, 0:512])
                tt("dve", t12[:, 0:512], ob[:, 0:512], rr[:, 0:512], ALU.mult)
            else:
                v3 = lambda a_: a_[:, :].rearrange("p (m q) -> p m q", m=2)[:, :, 0:QW]
                recip(v3(rr), v3(sk))
                tt("dve", v3(t12), v3(ob), v3(rr), ALU.mult)
            stt(of[:, 0:QW], t12[:, 256:256 + QW], neglam[:, l:l + 1], t12[:, 0:QW], ALU.mult, ALU.add)
            tt("dve", sq_[:, 0:QW], of[:, 0:QW], of[:, 0:QW], ALU.mult)
            return (h, q0, QW, of, sq_)

        def att_fin_b(h, q0, QW, of, sq_):
            ssb = PS(ssr.next())
            mm(ssb[:, 0:QW], ones_bf, sq_[:, 0:QW])
            ln_ = lnb.next(); rs_ = rsb.next()
            act(ln_[:, 0:QW], ssb[:, 0:QW], AF.Ln, scale=1.0 / 128, bias=epsT[:, 0:1])
            act(rs_[:, 0:QW], ln_[:, 0:QW], AF.Exp, scale=-0.5)
            stt(mixT[:, h, q0:q0 + QW], of[:, 0:QW], gcol[:, l:l + 1], rs_[:, 0:QW], ALU.mult, ALU.mult)

        stages = []
        tiles_ = []
        tno = 0
        for h in range(4):
            for (q0, QW, kbs) in qsets:
                tiles_.append((h, q0, QW, tno))
                for ki, kb in enumerate(kbs):
                    stages.append((h, q0, QW, ki, len(kbs), kb, tno))
                tno += 1
        hcnt = [0, 0]
        tpar_of = {}
        for (h, q0, QW, tn) in tiles_:
            tpar_of[tn] = hcnt[h % 2] % 2
            hcnt[h % 2] += 1
        att_Q(tiles_[0][0], tiles_[0][1], tiles_[0][2], tpar_of[0])
        LA = 2
        fifo = []
        pend = []

        def do_pv(it):
            ph, pq0, pQW, pki, pnk, pkb, ppar, pE = it
            att_PV(ph, pQW, pki, pnk, pkb, pE, ppar)
            if pki == pnk - 1:
                pend.append([10, att_fin_a(ph, pq0, pQW, ppar)])

        for st in stages:
            h, q0, QW, ki, nk, kb, tn = st
            par = tn % 2
            if ki == 0 and tn + 1 < len(tiles_):
                nh, nq0, nQW, ntn = tiles_[tn + 1]
                att_Q(nh, nq0, nQW, tpar_of[ntn])
            if ki == 4:
                bg_mods(l + 1, 7)
            E = att_S(h, q0, QW, kb, tpar_of[tn])
            fifo.append((h, q0, QW, ki, nk, kb, par, E))
            if len(fifo) > LA:
                do_pv(fifo.pop(0))
            for it in pend:
                it[0] -= 1
            while pend and pend[0][0] <= 0:
                att_fin_b(*pend.pop(0)[1])
        while fifo:
            do_pv(fifo.pop(0))

        while bg_mods(l + 1, 7):
            pass

        AB = AV(0, (TB, 512), BF16)
        DFB = [(AV(18432 + i * 8192, (NB // 2, 256), BF16), AV(18432 + i * 8192 + 4096, (NB // 2, 256), BF16)) for i in range(2)]
        FTs = _Rot([AV(E_O + i * 1024, (2, 256), BF16) for i in range(2)])
        pbr3 = _Rot(range(8))
        NH = NB // 2
        ABp = AV(34816, (NH, 512), BF16); ABm = AV(34816 + NH * 1024, (NH, 512), BF16)

        def stage_a(tb):
            pb = PS(pbr3.next())
            for c_ in range(2):
                mm(pb[:, c_ * 256:(c_ + 1) * 256], ufT[:, c_, tb * 128:(tb + 1) * 128], cmat[:, 3:5, :].rearrange("p a b -> p (a b)"))
            cp(cpr.next(), AB[:, tb, :], pb[:, :])

        for i in range(NH):
            stage_a(i); stage_a(i + NH)
            tt("dve", ABp[:, i, :], AB[:, i, :], AB[:, i + NH, :], ALU.add)
            tt("pool", ABm[:, i, :], AB[:, i, :], AB[:, i + NH, :], ALU.subtract)
        if not last:
            for tb in range(NB, TB):
                stage_a(tb)
        while pend:
            att_fin_b(*pend.pop(0)[1])

        pend_f = None

        def fourier_out1(pb, Wk, tok0):
            FT = FTs.next()
            cp("act", FT[:, :, 0:Wk], pb[:, :].rearrange("p (c k) -> p c k", c=2)[:, :, 0:Wk])
            return (FT, Wk, tok0)

        def fourier_out(pb, Wk, tok0):
            fourier_out2(*fourier_out1(pb, Wk, tok0))

        def fourier_out2(FT, Wk, tok0):
            pb2 = PS(pbr3.next())
            for c_ in range(2):
                mm(pb2[:, c_ * 256:c_ * 256 + Wk], wfbd[:, c_, :], FT[:, c_, 0:Wk])
            dst = tok0 if not isinstance(tok0, int) else mixT[:, 4:6, tok0:tok0 + Wk]
            cp("dve", dst, pb2[:, :].rearrange("p (c k) -> p c k", c=2)[:, :, 0:Wk])

        step = 0
        for par, ABx in ((0, ABp), (1, ABm)):
            for kc in range(NKC // 2):
                Cb, Sb = DFB[step % 2]
                dma("sp", Cb, dftc_d[par, kc], "dfc%d" % (step % 2)); dma("sp", Sb, dfts_d[par, kc], "dfs%d" % (step % 2))
                step += 1
                pb = PS(pbr3.next())
                for c_ in range(2):
                    for nb in range(NH):
                        mm(pb[:, c_ * 256:(c_ + 1) * 256], ABx[:, nb, c_ * 256:c_ * 256 + 128], Cb[:, nb, :], start=(nb == 0), stop=False)
                        mm(pb[:, c_ * 256:(c_ + 1) * 256], ABx[:, nb, c_ * 256 + 128:c_ * 256 + 256], Sb[:, nb, :], start=False, stop=(nb == NH - 1))
                if pend_f is not None:
                    fourier_out2(*pend_f)
                dst = mixT[:, 4:6, kc * 512:(kc + 1) * 512].rearrange("p c (j two) -> p c j two", two=2)[:, :, :, par]
                pend_f = fourier_out1(pb, 256, dst)
        if pend_f is not None:
            fourier_out2(*pend_f)
            pend_f = None
        if not last:
            for k0 in range(0, NC, 256):
                Wk = min(256, NC - k0)
                pb = PS(pbr3.next())
                for c_ in range(2):
                    for nb in range(NCB):
                        mm(pb[:, c_ * 256:c_ * 256 + Wk], AB[:, NB + nb, c_ * 256:c_ * 256 + 128], dcc[:, nb, k0:k0 + Wk], start=(nb == 0), stop=False)
                        mm(pb[:, c_ * 256:c_ * 256 + Wk], AB[:, NB + nb, c_ * 256 + 128:c_ * 256 + 256], dcs[:, nb, k0:k0 + Wk], start=False, stop=(nb == NCB - 1))
                fourier_out(pb, Wk, N + k0)

        DG_O = WIN_O + 16384
        diag = AV(DG_O, (CONV_K, 2, 128), BF16)
        for j in range(CONV_K):
            for c_ in range(2):
                r = vb + R_CW + j * 2 + c_
                ts("dve", diag[:, j, c_, :], ident_bf, vT[:, r:r + 1], ALU.mult)
        C_O = 51200
        zcs = _Rot([AV(C_O + i * 2048, (512,), F32) for i in range(3)])
        cens = _Rot([AV(C_O + 6144 + i * 2048, (512,), F32) for i in range(3)])
        sqs = _Rot([AV(C_O + 12288 + i * 2048, (512,), F32) for i in range(3)])
        rsd = _Rot([AV(C_O + 18432 + i * 2048, (512,), F32) for i in range(2)])
        sils = _Rot([AV(E_O + 4096 + i * 2048, (2, 512), BF16) for i in range(2)])
        segs = [(zl, t0, W, t0) for (t0, W) in lat_tiles]
        if not last:
            segs.append((zc_, 0, NC, N))
        items = []
        for si, (zz, s0, W, tok0) in enumerate(segs):
            sil = sils.next()
            for c_ in range(2):
                items.append(dict(zz=zz, s0=s0, W=W, tok0=tok0, c=c_, sil=sil))

        def cv_A(it):
            W = it["W"]; c_ = it["c"]
            pb = PS(pbr3.next())
            for j in range(CONV_K):
                mm(pb[:, 0:W], diag[:, j, c_, :], it["zz"][:, c_, it["s0"] + j:it["s0"] + j + W], start=(j == 0), stop=(j == CONV_K - 1))
            it["zcv"] = zcs.next()
            act(it["zcv"][:, 0:W], pb[:, 0:W], AF.Identity, bias=vT[:, vb + R_CB + c_: vb + R_CB + c_ + 1])

        def cv_B(it):
            W = it["W"]
            pm = PS(pbr3.next())
            mm(pm[:, 0:W], gln_f, it["zcv"][:, 0:W])
            it["cen"] = cens.next(); it["sq"] = sqs.next()
            tt("dve", it["cen"][:, 0:W], it["zcv"][:, 0:W], pm[:, 0:W], ALU.subtract)
            act(it["sq"][:, 0:W], it["cen"][:, 0:W], AF.Square)

        def cv_C(it):
            W = it["W"]; c_ = it["c"]
            pv_ = PS(pbr3.next())
            mm(pv_[:, 0:W], gln_f, it["sq"][:, 0:W])
            rs_ = rsd.next()
            act(rs_[:, 0:W], pv_[:, 0:W], AF.Ln, bias=epsT[:, 0:1])
            act(rs_[:, 0:W], rs_[:, 0:W], AF.Exp, scale=-0.5)
            zn = it["sq"]
            tt("dve", zn[:, 0:W], it["cen"][:, 0:W], rs_[:, 0:W], ALU.mult)
            act(it["sil"][:, c_, 0:W], zn[:, 0:W], AF.Silu, scale=vT[:, vb + R_LG + c_: vb + R_LG + c_ + 1], bias=vT[:, vb + R_LB + c_: vb + R_LB + c_ + 1])

        def cv_D(it):
            W = it["W"]
            for dc in range(2):
                pb = PS(pbr3.next())
                for c_ in range(2):
                    mm(pb[:, 0:W], wpw2[:, c_, dc * 128:(dc + 1) * 128], it["sil"][:, c_, 0:W], start=(c_ == 0), stop=(c_ == 1))
                cp(cpr.next(), mixT[:, 6 + dc, it["tok0"]:it["tok0"] + W], pb[:, 0:W])

        ni = len(items)
        for i in range(ni + 4):
            if i < ni:
                cv_A(items[i])
            if 0 <= i - 1 < ni:
                cv_B(items[i - 1])
            if 0 <= i - 2 < ni:
                cv_C(items[i - 2])
            if 0 <= i - 3 < ni and items[i - 3]["c"] == 1:
                cv_D(items[i - 3])

        FW_O = WIN_O
        XIN5 = FW_O + 24576
        SQ5 = XIN5 + 16384
        T5 = 176128
        rstd5 = AV(T5, (512,), F32); sqrtt5 = AV(T5 + 2048, (512,), F32)
        tmps5 = _Rot([AV(T5 + 4096, (512,), F32), AV(T5 + 6144, (512,), F32)])
        sq5 = AV(SQ5, (KD, 512), BF16)
        tiles5 = all_tiles if not last else lat_tiles
        pend5 = None
        for (t0, W) in tiles5:
            col = 1 if t0 >= N else 0
            xin = AV(XIN5, (KD, W), F32)
            for m_ in range(KD):
                dma("sp", xin[:, m_, :], xs_tile(t0, W)[:, m_, :], "xin%d" % (m_ % 2))
            if pend5 is not None:
                norm_sq(XRES[:, :, pend5[0]:pend5[0] + pend5[1]], pend5[1], sq5)
            for m_ in range(KD):
                pb = PS(pbr3.next())
                for k in range(KD):
                    mm(pb[:, 0:W], w_out[:, k, m_ * 128:(m_ + 1) * 128], mixT[:, k, t0:t0 + W], start=(k == 0), stop=(k == KD - 1))
                stt(XRES[:, m_, t0:t0 + W], pb[:, 0:W], modT[:, l, 2 * 8 + m_, col:col + 1], xin[:, m_, :], ALU.mult, ALU.add)
                if m_ == 3 and pend5 is not None:
                    pt0, pW, pcol = pend5
                    norm_rest(l, 1, 3, XRES[:, :, pt0:pt0 + pW], pW, pcol, mixT[:, :, pt0:pt0 + pW], sq5, rstd5, sqrtt5, tmps5, bank=pbr3.next())
                    pend5 = None
            pend5 = (t0, W, col)
        pt0, pW, pcol = pend5
        norm_sq(XRES[:, :, pt0:pt0 + pW], pW, sq5)
        norm_rest(l, 1, 3, XRES[:, :, pt0:pt0 + pW], pW, pcol, mixT[:, :, pt0:pt0 + pW], sq5, rstd5, sqrtt5, tmps5, bank=pbr3.next())
        h2T = mixT

        G = 4
        groups = [(g0, min(G, NFF - g0)) for g0 in range(0, NFF, G)]
        if groups[-1][1] < G:
            groups = [groups[-1]] + groups[:-1]
        fwb = [(AV(FW_O + i * 24576, (KD, 512), BF16), AV(FW_O + i * 24576 + 8192, (KD, 512), BF16), AV(FW_O + i * 24576 + 16384, (G, D), BF16)) for i in range(2)]
        ACT_O = 159744
        actb = _Rot([AV(ACT_O + i * 4096, (G, 512), BF16) for i in range(2)])
        sgs = _Rot([AV(ACT_O + 8192 + i * 2048, (512,), F32) for i in range(2)])
        upb = _Rot([0, 1, 2, 3]); dnb = _Rot([4, 5, 6, 7])
        pending = None

        def down(g0, gn, t0, W, ab, w2b, col, is_last_group=False):
            for m_ in range(KD):
                pb = PS(dnb.next())
                for jj in range(gn):
                    mm(pb[:, 0:W], w2b[:, jj, m_ * 128:(m_ + 1) * 128], ab[:, jj, 0:W], start=(jj == 0), stop=(jj == gn - 1))
                stt(XRES[:, m_, t0:t0 + W], pb[:, 0:W], modT[:, l, 5 * 8 + m_, col:col + 1], XRES[:, m_, t0:t0 + W], ALU.mult, ALU.add)

        for gi, (g0, gn) in enumerate(groups):
            w1b, w3b, w2b = fwb[gi % 2]
            pass
            cw = gn * 128
            dma("pool", w1b[:, :, 0:cw], w1_d[l, :, g0 * 128:g0 * 128 + cw].rearrange("(k p) n -> p k n", p=128), "fw1_%d" % (gi % 2))
            dma("pool", w3b[:, :, 0:cw], w3_d[l, :, g0 * 128:g0 * 128 + cw].rearrange("(k p) n -> p k n", p=128), "fw3_%d" % (gi % 2))
            dma("pool", w2b[:, 0:gn, :], w2_d[l, g0 * 128:g0 * 128 + cw, :].rearrange("(j p) n -> p j n", p=128), "fw2_%d" % (gi % 2))
            for (t0, W) in tiles5:
                col = 1 if t0 >= N else 0
                ab = actb.next()
                for jj in range(gn):
                    pa = PS(upb.next()); pg = PS(upb.next())
                    for k in range(KD):
                        mm(pa[:, 0:W], w1b[:, k, jj * 128:(jj + 1) * 128], h2T[:, k, t0:t0 + W], start=(k == 0), stop=(k == KD - 1))
                    for k in range(KD):
                        mm(pg[:, 0:W], w3b[:, k, jj * 128:(jj + 1) * 128], h2T[:, k, t0:t0 + W], start=(k == 0), stop=(k == KD - 1))
                    sg = sgs.next()
                    act(sg[:, 0:W], pa[:, 0:W], AF.Silu)
                    tt("dve", ab[:, jj, 0:W], pg[:, 0:W], sg[:, 0:W], ALU.mult)
                if pending is not None:
                    down(*pending)
                    if pending[-1] and not last:
                        pt0, pW = pending[2], pending[3]
                        dma("sp", xs_tile(pt0, pW), XRES[:, :, pt0:pt0 + pW], "xst%d" % ((pt0 // 512) % 2))
                pending = (g0, gn, t0, W, ab, w2b, col, gi == len(groups) - 1)
        down(*pending)
        if not last:
            pt0, pW = pending[2], pending[3]
            dma("sp", xs_tile(pt0, pW), XRES[:, :, pt0:pt0 + pW], "xst%d" % ((pt0 // 512) % 2))
        pending = None

    FB = 73728
    fgr = DEPTH * VROWS
    gbc = AV(FB, (D,), F32)
    junkF = AV(FB + 4096, (512,), F32)
    otl = [AV(FB + 6144 + i * 4096, (D,), F32) for i in range(3)]
    ssqF = ST("ssqF", [128, NB, 4], F32)
    dma("sp", gbc, vecs_d[fgr:fgr + 8, :].rearrange("a b -> (a b)").partition_broadcast(128), "c0")
    pbrF = _Rot(range(8))

    def fin_T(blk):
        banks = []
        for kq in range(2):
            pb = PS(pbrF.next())
            for kk in range(4):
                tr(pb[:, kk * 128:(kk + 1) * 128], XRES[:, kq * 4 + kk, blk * 128:(blk + 1) * 128], ident_f)
            act(junkF, pb[:, :], AF.Square, accum=ssqF[:, blk, kq:kq + 1])
            banks.append(pb)
        return banks

    def fin_E(blk, banks):
        tt("dve", ssqF[:, blk, 2:3], ssqF[:, blk, 0:1], ssqF[:, blk, 1:2], ALU.add)
        ts("dve", ssqF[:, blk, 2:3], ssqF[:, blk, 2:3], 1.0 / D, ALU.mult, EPS, ALU.add)
        tt("pool", ssqF[:, blk, 3:4], ssqF[:, blk, 2:3], small[:, 0:1], ALU.pow)
        ot = otl[blk % 3]
        for kq in range(2):
            stt(ot[:, kq * 512:(kq + 1) * 512], banks[kq][:, :], ssqF[:, blk, 3:4], gbc[:, kq * 512:(kq + 1) * 512], ALU.mult, ALU.mult)
        dma("sp", out_d[blk * 128:(blk + 1) * 128, :], ot, "ost%d" % (blk % 3))

    prevF = None
    for blk in range(NB):
        banks = fin_T(blk)
        if prevF is not None:
            fin_E(*prevF)
        prevF = (blk, banks)
    fin_E(*prevF)

    P.finalize(None)
    streams = []
    for o in P.ops:
        s = P._stream(o)
        if s not in streams:
            streams.append(s)
    sems = {s: es.enter_context(nc.semaphore("sem_%s_%s" % s)) for s in streams}
    out_slots = [s for s in streams if s[0] == "slot" and s[1].startswith("ost")]
    block = es.enter_context(nc.Block())

    @block.tensor
    def _(e):
        P.emit_engine("pe", e, sems)

    @block.scalar
    def _(e):
        P.emit_engine("act", e, sems)

    @block.vector
    def _(e):
        P.emit_engine("dve", e, sems)

    @block.gpsimd
    def _(e):
        P.emit_engine("pool", e, sems)

    @block.sync
    def _(e):
        P.emit_engine("sp", e, sems)
        for s in out_slots:
            e.wait_ge(sems[s], P.max_vals[s])

    es.close()
    return nc, P


def _const_tables(N, NC):
    bf = ml_dtypes.bfloat16
    rows = N // GRID_W
    t = np.arange(N)
    row = (t // GRID_W).astype(np.float64)
    colp = (t % GRID_W).astype(np.float64)
    inv_freq = ROPE_BASE ** (-np.arange(16, dtype=np.float64) / 16)
    ropec = np.zeros((128, N)); ropes = np.zeros((128, N))
    perm = np.zeros((128, 128))
    for p in range(128):
        d = p % 64
        axis, half, f = d // 32, (d % 32) // 16, d % 16
        ang = (row if axis == 0 else colp) * inv_freq[f]
        ropec[p] = np.cos(ang)
        ropes[p] = -np.sin(ang) if half == 0 else np.sin(ang)
        partner = p + 16 if half == 0 else p - 16
        perm[partner, p] = 1.0
    ident = np.eye(128)
    ones = np.ones((128, 128))
    cc = np.arange(64)
    a64 = 2 * np.pi * np.outer(cc, cc) / 64
    c64 = np.cos(a64) / 8.0; s64 = np.sin(a64) / 8.0
    c64bd = np.zeros((128, 128)); s64bd = np.zeros((128, 128))
    gln = np.zeros((128, 128))
    for g in range(2):
        c64bd[g * 64:(g + 1) * 64, g * 64:(g + 1) * 64] = c64
        s64bd[g * 64:(g + 1) * 64, g * 64:(g + 1) * 64] = s64
        gln[g * 64:(g + 1) * 64, g * 64:(g + 1) * 64] = 1.0 / 64
    cmat = np.stack([ident, ones, perm, c64bd, s64bd], axis=1).astype(bf)
    cmatf = np.stack([ident, gln], axis=1).astype(np.float32)

    NB = N // 128
    NHt = N // 2
    n_ = np.arange(NHt)
    kp = np.arange(NHt)
    ang_e = 2 * np.pi * (np.outer(n_, kp) % NHt) / NHt
    ang_o = 2 * np.pi * (np.outer(n_, 2 * kp + 1) % N) / N
    sc_ = 1.0 / np.sqrt(N)

    def lay(Mx):
        return np.ascontiguousarray(Mx.reshape(NB // 2, 128, NHt // 256, 256).transpose(2, 1, 0, 3))

    dftc_h = np.stack([lay(np.cos(ang_e) * sc_), lay(np.cos(ang_o) * sc_)], axis=0).astype(bf)
    dfts_h = np.stack([lay(-np.sin(ang_e) * sc_), lay(-np.sin(ang_o) * sc_)], axis=0).astype(bf)

    def dft(M):
        n = np.arange(M)
        ang = 2 * np.pi * (np.outer(n, n) % M) / M
        return np.cos(ang) / np.sqrt(M), -np.sin(ang) / np.sqrt(M)

    Cc, Sc = dft(NC)
    NCB = NC // 128

    def layc(Mx):
        return np.ascontiguousarray(Mx.reshape(NCB, 128, NC).transpose(1, 0, 2)).astype(bf)

    return dict(ropec=ropec.astype(bf), ropes=ropes.astype(bf), cmat=cmat, cmatf=cmatf,
                dftc=dftc_h, dfts=dfts_h, dcc=layc(Cc), dcs=layc(Sc))


def _pack_vecs(inp, DEPTH):
    vecs = np.zeros((384, 128), np.float32)
    for l in range(DEPTH):
        b = l * VROWS
        vecs[b + R_BADA:b + R_BADA + 48] = np.asarray(inp["b_ada"][l], np.float32).reshape(48, 128)
        vecs[b + R_N1:b + R_N1 + 8] = np.asarray(inp["norm1_g"][l], np.float32).reshape(8, 128)
        vecs[b + R_N2:b + R_N2 + 8] = np.asarray(inp["norm2_g"][l], np.float32).reshape(8, 128)
        vecs[b + R_CB:b + R_CB + 2] = np.asarray(inp["conv_b"][l], np.float32).reshape(2, 128)
        vecs[b + R_LG:b + R_LG + 2] = np.asarray(inp["conv_ln_g"][l], np.float32).reshape(2, 128)
        vecs[b + R_LB:b + R_LB + 2] = np.asarray(inp["conv_ln_b"][l], np.float32).reshape(2, 128)
        vecs[b + R_CW:b + R_CW + 62] = np.asarray(inp["conv_w"][l], np.float32).reshape(62, 128)
    vecs[DEPTH * VROWS:DEPTH * VROWS + 8] = np.asarray(inp["final_g"], np.float32).reshape(8, 128)
    return vecs


def make_in_maps(inp, N, NC, DEPTH, B):
    f = lambda a: np.ascontiguousarray(np.asarray(a, np.float32))
    consts = _const_tables(N, NC)
    shared = dict(consts)
    shared["vecs"] = _pack_vecs(inp, DEPTH)
    for k in ("w_ada", "w_in", "subln_g", "w_fourier", "w_conv_out", "w_out", "w_ffn1", "w_ffn3", "w_ffn2"):
        shared[k] = f(inp[k])
    shared["lamv"] = np.ascontiguousarray(np.stack([f(inp["lam_q1"]), f(inp["lam_k1"]), f(inp["lam_q2"]), f(inp["lam_k2"])], axis=1))
    x = f(inp["x"]); ctx = f(inp["ctx"]); c = f(inp["c"]); c_ctx = f(inp["c_ctx"])
    maps = []
    for b in range(B):
        m = dict(shared)
        m["x"] = x[b]; m["ctx"] = ctx[b]
        m["cc"] = np.ascontiguousarray(np.stack([c[b], c_ctx], axis=1))
        maps.append(m)
    return maps


_CACHE = {}


def kernel(**inputs):
    N, NC, DEPTH, B = 2048, 256, 2, 8
    if "nc" not in _CACHE:
        _CACHE["nc"] = build_program(N, NC, DEPTH)[0]
    nc = _CACHE["nc"]
    maps = make_in_maps(inputs, N, NC, DEPTH, B)
    res = run_bass_kernel_spmd(nc, maps, core_ids=list(range(B)))
    return np.stack([np.asarray(r["out"], np.float32) for r in res.results], axis=0)
```

```python
import numpy as np
import ml_dtypes
import concourse.bass as bass
import concourse.mybir as mybir
from concourse.bass_utils import run_bass_kernel_spmd

F32 = mybir.dt.float32
BF16 = mybir.dt.bfloat16
AF = mybir.ActivationFunctionType
ALU = mybir.AluOpType


class _Op:
    __slots__ = ("eng", "fn", "deps", "sem", "val", "signal", "slot", "group", "idx", "dbg")


def _box(ap):
    t = ap.tensor
    shp = tuple(t.shape)
    pat = ap.ap
    off = int(ap.offset)
    sp = str(ap.space)
    esz = mybir.dt.size(ap.dtype)
    if sp == "DRAM":
        lo = off * esz
        hi = (off + sum((c - 1) * abs(s) for s, c in pat) + 1) * esz - 1
        return (ap.name, 0, 0, lo, hi, False)
    pstride = 1
    for d in shp[1:]:
        pstride *= d
    plo = off // pstride
    flo = off % pstride
    assert pat[0][0] == pstride or pat[0][1] == 1, (pat, pstride)
    phi = plo + pat[0][1] - 1
    fhi = flo + sum((c - 1) * abs(s) for s, c in pat[1:])
    lo = flo * esz
    hi = (fhi + 1) * esz - 1
    if sp == "PSUM":
        lo = (lo // 2048) * 2048
        hi = (hi // 2048) * 2048 + 2047
        return (ap.name + "@" + sp, 0, 127, lo, hi, True)
    return (ap.name + "@" + sp, plo, phi, lo, hi, False)


class Prog:
    ENGS = ("pe", "act", "dve", "pool", "sp")

    def __init__(self, nc):
        self.nc = nc
        self.ops = []
        self.track = {}
        self.slot_groups = {}
        self.slot_cur = {}

    def add(self, eng, fn, reads=(), writes=(), slot=None, group=None):
        op = _Op()
        op.eng = eng
        op.fn = fn
        op.deps = set()
        op.signal = False
        op.slot = slot
        op.idx = len(self.ops)
        op.group = None
        op.dbg = None
        if DEBUG_LINES:
            import sys as _sys
            fr = _sys._getframe(2)
            op.dbg = (fr.f_lineno, fr.f_back.f_lineno if fr.f_back else None)
        if slot is not None:
            groups = self.slot_groups.setdefault(slot, [])
            if groups and self.slot_cur.get(slot) == group and group is not None:
                groups[-1].append(op.idx)
            else:
                if groups:
                    op.deps.add(groups[-1][-1])
                groups.append([op.idx])
                self.slot_cur[slot] = group
            op.group = (slot, len(groups) - 1)
        for ap in reads:
            self._access(op, ap, False)
        for ap in writes:
            self._access(op, ap, True)
        self.ops.append(op)
        return op

    def _same_stream(self, a, b):
        if a.slot is not None or b.slot is not None:
            return a.slot is not None and a.slot == b.slot
        return a.eng == b.eng

    def _access(self, op, ap, is_write):
        key, plo, phi, lo, hi, excl = _box(ap)
        is_write = is_write or excl
        lst = self.track.setdefault(key, [])
        keep = []
        for rec in lst:
            rplo, rphi, rlo, rhi, ridx, rw = rec
            overlap = not (rphi < plo or rplo > phi or rhi < lo or rlo > hi)
            if overlap and (is_write or rw) and ridx != op.idx:
                op.deps.add(ridx)
            contained = rplo >= plo and rphi <= phi and rlo >= lo and rhi <= hi
            if contained and ridx != op.idx:
                if is_write:
                    continue
                if (not rw) and self._same_stream(self.ops[ridx], op):
                    continue
            keep.append(rec)
        keep.append([plo, phi, lo, hi, op.idx, is_write])
        self.track[key] = keep

    def finalize(self, block_sems):
        ops = self.ops
        def stream(o):
            return ("slot", o.slot) if o.slot is not None else ("eng", o.eng)
        pos = {}
        cnt = {}
        for o in ops:
            s = stream(o)
            cnt[s] = cnt.get(s, 0) + 1
            pos[o.idx] = cnt[s]
        def target(j):
            o = ops[j]
            if o.slot is not None:
                g = self.slot_groups[o.slot][o.group[1]]
                return g[-1]
            return j
        waited = {e: {} for e in self.ENGS}
        need = {}
        for o in ops:
            w = waited[o.eng]
            req = {}
            for j in o.deps:
                tj = target(j)
                if tj == o.idx:
                    continue
                if tj > o.idx:
                    assert ops[tj].slot is not None and ops[tj].group == o.group, "dep on future op"
                    continue
                s = stream(ops[tj])
                if s == ("eng", o.eng) and o.slot is None and not SAME_ENGINE_SYNC:
                    continue
                if s == ("eng", o.eng) and o.eng == "pe" and o.slot is None:
                    continue
                if s == ("eng", o.eng) and o.slot is not None and ops[tj].slot is None:
                    pass
                p = pos[tj]
                if w.get(s, 0) >= p:
                    continue
                if req.get(s, (0, None))[0] < p:
                    req[s] = (p, tj)
            lst = []
            for s, (p, tj) in req.items():
                w[s] = p
                lst.append((s, tj))
                ops[tj].signal = True
            need[o.idx] = lst
        for o in ops:
            if o.slot is not None:
                o.signal = True
        val = {}
        c = {}
        for o in ops:
            s = stream(o)
            if o.signal:
                c[s] = c.get(s, 0) + (16 if o.slot is not None else 1)
            val[o.idx] = c.get(s, 0)
        self._need, self._val, self._stream = need, val, stream
        self.max_vals = dict(c)
        return need, val

    def emit(self, engines, sems):
        need, val, stream = self._need, self._val, self._stream
        for o in self.ops:
            e = engines[o.eng]
            for s, tj in need[o.idx]:
                e.wait_ge(sems[s], val[tj])
            ins = o.fn(e)
            if o.signal:
                ins.then_inc(sems[stream(o)], 16 if o.slot is not None else 1)

    def emit_engine(self, eng_name, e, sems):
        need, val, stream = self._need, self._val, self._stream
        for o in self.ops:
            if o.eng != eng_name:
                continue
            for s, tj in need[o.idx]:
                e.wait_ge(sems[s], val[tj])
            ins = o.fn(e)
            if DEBUG_LINES:
                NAMES[getattr(ins.ins, "name", None)] = (o.idx, o.dbg)
            if o.signal:
                ins.then_inc(sems[stream(o)], 16 if o.slot is not None else 1)


SAME_ENGINE_SYNC = True
DEBUG_LINES = False
NAMES = {}


D = 1024
KD = D // 128
GRID_W = 64
INW = 2304
DFF = 2816
NFF = DFF // 128
CONV_K = 31
EPS = 1e-6
ROPE_BASE = 10000.0
VROWS = 132
R_BADA, R_N1, R_N2, R_CB, R_LG, R_LB, R_CW = 0, 48, 56, 64, 66, 68, 70


class _Rot:
    def __init__(self, items):
        self.items = list(items)
        self.i = 0

    def next(self):
        v = self.items[self.i % len(self.items)]
        self.i += 1
        return v


def build_program(N, NC, DEPTH):
    T = N + NC
    NB = N // 128
    NCB = NC // 128
    TB = NB + NCB
    lat_tiles = [(t0, 512) for t0 in range(0, N, 512)]
    all_tiles = lat_tiles + [(N, NC)]
    nc = bass.Bass("TRN2", target_bir_lowering=False)
    P = Prog(nc)

    def din(name, shape, dt=F32):
        return nc.dram_tensor(name, list(shape), dt, kind="ExternalInput").ap()

    x_d = din("x", [N, D]); ctx_d = din("ctx", [NC, D]); cc_d = din("cc", [D, 2])
    w_ada_d = din("w_ada", [DEPTH, D, 6 * D]); vecs_d = din("vecs", [384, 128])
    w_in_d = din("w_in", [DEPTH, D, INW]); lamv_d = din("lamv", [DEPTH, 4, 64])
    subln_d = din("subln_g", [DEPTH, 128]); wf_d = din("w_fourier", [DEPTH, 4, 64, 64])
    wco_d = din("w_conv_out", [DEPTH, 256, 256]); w_out_d = din("w_out", [DEPTH, D, D])
    w1_d = din("w_ffn1", [DEPTH, D, DFF]); w3_d = din("w_ffn3", [DEPTH, D, DFF]); w2_d = din("w_ffn2", [DEPTH, DFF, D])
    ropec_d = din("ropec", [128, N], BF16); ropes_d = din("ropes", [128, N], BF16)
    NKC = N // 256
    dftc_d = din("dftc", [2, NKC // 2, 128, NB // 2, 256], BF16); dfts_d = din("dfts", [2, NKC // 2, 128, NB // 2, 256], BF16)
    dcc_d = din("dcc", [128, NCB, NC], BF16); dcs_d = din("dcs", [128, NCB, NC], BF16)
    cmat_d = din("cmat", [128, 5, 128], BF16)
    cmatf_d = din("cmatf", [128, 2, 128], F32)
    out_d = nc.dram_tensor("out", [N, D], F32, kind="ExternalOutput").ap()
    NT_ALL = len(all_tiles)
    xs_d = nc.dram_tensor("xs", [NT_ALL, 128, KD, 512], F32).ap()

    def xs_tile(t0, W):
        ti_ = (t0 // 512) if t0 < N else (NT_ALL - 1)
        return xs_d[ti_, :, :, 0:W]
    modrow_d = nc.dram_tensor("modrow", [DEPTH, 2, 6 * D], F32).ap()

    from contextlib import ExitStack
    es = ExitStack()
    ARENA = 184320
    arena = es.enter_context(nc.sbuf_tensor("arena", [128, ARENA // 4], F32))
    psum = es.enter_context(nc.psum_tensor("ps", [128, 8, 512], F32))

    def AV(off, shape, dt, parts=128):
        esz = mybir.dt.size(dt)
        n = 1
        for d_ in shape:
            n *= d_
        assert off % 4 == 0 and off + n * esz <= ARENA, (off, shape)
        a = arena[:, off // 4:(off + n * esz + 3) // 4]
        if dt != F32:
            a = a.bitcast(dt)
        a = a[0:parts, 0:n]
        if len(shape) == 2:
            a = a.rearrange("p (a b) -> p a b", b=shape[1])
        elif len(shape) == 3:
            a = a.rearrange("p (a b c) -> p a b c", b=shape[1], c=shape[2])
        elif len(shape) == 4:
            a = a.rearrange("p (a b c d) -> p a b c d", b=shape[1], c=shape[2], d=shape[3])
        return a

    def PS(bank, dt=F32):
        a = psum[:, bank, :]
        return a if dt == F32 else a.bitcast(dt)

    def ST(name, shape, dt):
        return es.enter_context(nc.sbuf_tensor("s_" + name, list(shape), dt))

    cmat = ST("cmat", [128, 5, 128], BF16); cmatf = ST("cmatf", [128, 2, 128], F32)
    ident_bf, ones_bf, perm_bf = cmat[:, 0, :], cmat[:, 1, :], cmat[:, 2, :]
    ident_f, gln_f = cmatf[:, 0, :], cmatf[:, 1, :]
    ropec = ST("ropec", [128, N], BF16); ropes = ST("ropes", [128, N], BF16)
    vT = ST("vT", [128, 384], F32)
    modT = ST("modT", [128, DEPTH, 48, 2], F32)
    gm = ST("gm", [128, DEPTH, 2, KD, 2], F32)
    dcc = ST("dcc", [128, NCB, NC], BF16); dcs = ST("dcs", [128, NCB, NC], BF16)
    gcol = ST("gcol", [128, DEPTH], F32)
    lamb = ST("lamb", [128, DEPTH, 4, 64], F32); lamt = ST("lamt", [128, 2, 64], F32)
    lams = ST("lams", [128, 8], F32); neglam = ST("neglam", [128, DEPTH], F32)
    sc_f = ST("sc_f", [128, KD, 2], F32); sc_bf = ST("sc_bf", [128, KD, 2], BF16)
    small = ST("small", [128, 64], F32)
    wfbd = ST("wfbd", [128, 2, 128], BF16); wfst = ST("wfst", [128, 2, 128], F32)
    wpw2 = ST("wpw2", [128, 2, 256], BF16)
    epsT = ST("epsT", [128, 1], F32)

    def mm(out, lhsT, rhs, start=True, stop=True):
        P.add("pe", lambda e: e.matmul(out, lhsT=lhsT, rhs=rhs, start=start, stop=stop), [lhsT, rhs], [out])

    def tr(out, in_, ident):
        P.add("pe", lambda e: e.transpose(out, in_, ident), [in_, ident], [out])

    def act(out, in_, func, scale=1.0, bias=None, accum=None, eng="act"):
        rd = [in_] + [a for a in (scale, bias) if not isinstance(a, (int, float, type(None)))]
        wr = [out] + ([accum] if accum is not None else [])
        kw = {}
        if bias is not None:
            kw["bias"] = bias
        if accum is not None:
            kw["accum_out"] = accum
        P.add("act", lambda e: e.activation(out=out, in_=in_, func=func, scale=scale, **kw), rd, wr)

    def tt(eng, out, a, b, op):
        P.add(eng, lambda e: e.tensor_tensor(out=out, in0=a, in1=b, op=op), [a, b], [out])

    def ts(eng, out, a, s1, op0, s2=None, op1=None):
        rd = [a] + [s for s in (s1, s2) if not isinstance(s, (int, float, type(None)))]
        if op1 is None:
            P.add(eng, lambda e: e.tensor_scalar(out=out, in0=a, scalar1=s1, scalar2=None, op0=op0), rd, [out])
        else:
            P.add(eng, lambda e: e.tensor_scalar(out=out, in0=a, scalar1=s1, scalar2=s2, op0=op0, op1=op1), rd, [out])

    def stt(out, a, s, b, op0, op1):
        rd = [a, b] + ([s] if not isinstance(s, (int, float)) else [])
        P.add("dve", lambda e: e.scalar_tensor_tensor(out=out, in0=a, scalar=s, in1=b, op0=op0, op1=op1), rd, [out])

    def cp(eng, out, in_):
        if eng == "act":
            P.add("act", lambda e: e.activation(out=out, in_=in_, func=AF.Copy), [in_], [out])
        else:
            P.add(eng, lambda e: e.tensor_copy(out=out, in_=in_), [in_], [out])

    def recip(out, in_):
        P.add("dve", lambda e: e.reciprocal(out=out, in_=in_), [in_], [out])

    def mset(eng, ap, val):
        P.add(eng, lambda e: e.memset(ap, val), [], [ap])

    def dma(eng, out, in_, slot, group=None):
        P.add(eng, lambda e: e.dma_start(out=out, in_=in_), [in_], [out], slot=slot, group=group)

    cpr = _Rot(["act", "dve"])

    dma("sp", cmat[:], cmat_d, "c0"); dma("sp", cmatf[:], cmatf_d, "c1")
    dma("sp", ropec[:], ropec_d, "c2"); dma("sp", ropes[:], ropes_d, "c3")
    dma("sp", dcc[:], dcc_d, "c0"); dma("sp", dcs[:], dcs_d, "c1")
    mset("dve", epsT[:], EPS)
    mset("dve", small[:, 0:8], -0.5)
    vst = AV(0, (3, 128), F32)
    dma("sp", vst, vecs_d.rearrange("(j p) f -> p j f", p=128), "c2")
    for j in range(3):
        tr(PS(0)[:, j * 128:(j + 1) * 128], vst[:, j, :], ident_f)
    cp("dve", vT[:], PS(0)[:, 0:384])
    for l in range(DEPTH):
        dma("sp", lamb[:, l], lamv_d[l].partition_broadcast(128), "c3")
        dma("sp", gcol[:, l:l + 1], subln_d[l].rearrange("(p o) -> p o", o=1), "c0")
        lam_init = 0.8 - 0.6 * float(np.exp(-0.3 * l))
        tt("dve", lamt[:, 0, :], lamb[:, l, 0, :], lamb[:, l, 1, :], ALU.mult)
        tt("dve", lamt[:, 1, :], lamb[:, l, 2, :], lamb[:, l, 3, :], ALU.mult)
        P.add("dve", lambda e: e.tensor_reduce(out=lams[:, 0:2], in_=lamt[:], axis=mybir.AxisListType.X, op=ALU.add), [lamt[:]], [lams[:, 0:2]])
        act(lams[:, 2:4], lams[:, 0:2], AF.Exp)
        tt("dve", lams[:, 4:5], lams[:, 3:4], lams[:, 2:3], ALU.subtract)
        ts("dve", neglam[:, l:l + 1], lams[:, 4:5], -lam_init, ALU.add)
        ts("dve", gcol[:, l:l + 1], gcol[:, l:l + 1], 1.0 - lam_init, ALU.mult)
    dma("sp", sc_f[:], cc_d.rearrange("(k p) t -> p k t", p=128), "c1")
    act(sc_bf[:], sc_f[:], AF.Silu)
    WA_O = 64800
    wa_bufs = [AV(WA_O + i * 4096, (KD, 256), BF16) for i in range(2)]
    mstage_t = ST("mstage", [2, 256], F32)
    mstage = mstage_t[:, :]
    mrow_t = ST("mrow", [96, 128], F32)
    mrow = mrow_t[:, :]
    pbr = _Rot(range(1, 8))
    all_pieces = [(0, pc) for pc in range(24)]
    for l_ in range(1, DEPTH):
        all_pieces += [(l_, pc) for pc in range(24)]
    mstate = {"dma": 0, "mm": 0}

    def mods_dma():
        i = mstate["dma"]
        if i >= len(all_pieces):
            return
        l_, pc = all_pieces[i]
        dma("pool", wa_bufs[i % 2], w_ada_d[l_, :, pc * 256:(pc + 1) * 256].rearrange("(k p) n -> p k n", p=128), "wa%d" % (i % 2))
        mstate["dma"] = i + 1

    def mods_piece(bank):
        i = mstate["mm"]
        while mstate["dma"] <= min(i + 1, len(all_pieces) - 1):
            mods_dma()
        l_, pc = all_pieces[i]
        wa = wa_bufs[i % 2]
        pb = PS(bank)
        for k in range(KD):
            mm(pb[0:2, 0:256], sc_bf[:, k, :], wa[:, k, :], start=(k == 0), stop=(k == KD - 1))
        cp("dve", mstage, pb[0:2, 0:256])
        dma("sp", modrow_d[l_, :, pc * 256:(pc + 1) * 256], mstage, "mst")
        mstate["mm"] = i + 1

    def mods_finish(l, s0, s1, bank):
        r0, r1 = s0 * 8, s1 * 8
        nr = r1 - r0
        for v in range(2):
            dma("sp", mrow[v * nr:(v + 1) * nr, :], modrow_d[l, v, r0 * 128:r1 * 128].rearrange("(r p) -> r p", p=128), "mld", group="mf%d_%d" % (l, s0))
        pb = PS(bank)
        tr(pb[:, 0:2 * nr], mrow[0:2 * nr, :], ident_f[0:2 * nr, 0:2 * nr])
        for v in range(2):
            tt("dve", modT[:, l, r0:r1, v], pb[:, v * nr:(v + 1) * nr], vT[:, l * VROWS + R_BADA + r0: l * VROWS + R_BADA + r1], ALU.add)
        for n_i, (sec_sc, rg) in enumerate(((1, R_N1), (4, R_N2))):
            if not (s0 <= sec_sc < s1):
                continue
            for k in range(KD):
                ts("dve", gm[:, l, n_i, k, :], modT[:, l, sec_sc * 8 + k, :], 1.0, ALU.add)
                ts("dve", gm[:, l, n_i, k, :], gm[:, l, n_i, k, :], vT[:, l * VROWS + rg + k: l * VROWS + rg + k + 1], ALU.mult)

    def load_w_in(l_):
        w_in_ = AV(110592, (KD, INW), BF16)
        if l_ == 0:
            order = [(hf, k) for hf in range(2) for k in range(KD)]
        else:
            order = [(hf, k) for hf in range(2) for k in range(5)] + [(hf, k) for hf in range(2) for k in range(5, KD)]
        for (hf, k) in order:
            dma("pool", w_in_[:, k, hf * 1152:(hf + 1) * 1152], w_in_d[l_, k * 128:(k + 1) * 128, hf * 1152:(hf + 1) * 1152],
                "win%d" % ((hf if l_ == 0 else (0 if k < 5 else 1))), group="l%d" % l_)

    load_w_in(0)

    def bg_mods(max_layer, bank):
        i = mstate["mm"]
        if i < len(all_pieces) and all_pieces[i][0] <= max_layer:
            mods_piece(bank)
            if i == 23:
                mods_finish(0, 2, 6, bank)
            elif i > 23 and (i + 1) % 24 == 0:
                mods_finish(all_pieces[i][0], 0, 6, bank)
            return True
        return False

    XIN = 90112
    xtoks = [AV(16384, (4, D), F32), AV(0, (4, D), F32)]
    xins = [AV(XIN, (KD, 512), F32), AV(32768, (KD, 512), F32)]
    xin_t0 = AV(73728, (KD, 512), F32)

    def x_load(ti):
        t0, W = all_tiles[ti]
        src = x_d[t0:t0 + W, :] if t0 < N else ctx_d[:, :]
        dma("sp", xtoks[ti % 2][:, 0:W // 128, :], src.rearrange("(s p) d -> p s d", p=128), "xl%d" % (ti % 2))

    x_load(0)
    mods_done = 0
    for ti, (t0, W) in enumerate(all_tiles):
        nsub = W // 128
        if ti + 1 < len(all_tiles):
            x_load(ti + 1)
        xtok = xtoks[ti % 2]
        xin = (xin_t0 if ti == 0 else xins[ti % 2])[:, :, 0:W]
        for k in range(KD):
            pb = PS(pbr.next())
            for s_ in range(nsub):
                tr(pb[:, s_ * 128:(s_ + 1) * 128], xtok[:, s_, k * 128:(k + 1) * 128], ident_f)
            cp(cpr.next(), xin[:, k, :], pb[:, 0:W])
        dma("sp", xs_tile(t0, W), xin, "xst%d" % ((t0 // 512) % 2))
        for _ in range(2):
            if mods_done < 8:
                mods_piece(pbr.next()); mods_done += 1
    while mods_done < 8:
        mods_piece(pbr.next()); mods_done += 1
    mods_finish(0, 0, 2, pbr.next())

    QT_O, KT_O, V_O, UF_O = 0, 18432, 36864, 55584
    MIX_O = 73728
    WIN_O = 110592
    Z_O = 147456
    QT = AV(QT_O, (4, T), BF16); KT = AV(KT_O, (4, T), BF16)
    V = AV(V_O, (TB, 4, 130), BF16)
    ufT = AV(UF_O, (2, T), BF16)
    mixT = AV(MIX_O, (KD, T), BF16)
    zl = AV(Z_O, (2, N + 30), BF16)
    zc_ = AV(Z_O + 2 * (N + 30) * 2, (2, NC + 30), BF16)
    XRES = AV(0, (KD, T), F32)

    def norm_sq(xin, W, sq):
        act(sq[:, :, 0:W], xin, AF.Square)

    def norm_rest(l, n_i, sec_sh, xin, W, col, hT, sq, rstd, sqrtt, tmps, bank=None):
        pb = PS(pbr.next() if bank is None else bank)
        for k in range(KD):
            mm(pb[:, 0:W], ones_bf, sq[:, k, 0:W], start=(k == 0), stop=(k == KD - 1))
        act(sqrtt[:, 0:W], pb[:, 0:W], AF.Ln, scale=1.0 / D, bias=epsT[:, 0:1])
        act(rstd[:, 0:W], sqrtt[:, 0:W], AF.Exp, scale=-0.5)
        for k in range(KD):
            tmp = tmps.next()
            stt(tmp[:, 0:W], xin[:, k, :], gm[:, l, n_i, k, col:col + 1], rstd[:, 0:W], ALU.mult, ALU.mult)
            act(hT[:, k, 0:W], tmp[:, 0:W], AF.Identity, bias=modT[:, l, sec_sh * 8 + k, col:col + 1])

    def norm_tile(l, n_i, sec_sh, xin, W, col, hT, sq, rstd, sqrtt, tmps):
        norm_sq(xin, W, sq)
        norm_rest(l, n_i, sec_sh, xin, W, col, hT, sq, rstd, sqrtt, tmps)

    for l in range(DEPTH):
        last = (l == DEPTH - 1)
        vb = l * VROWS
        w_in = AV(WIN_O, (KD, INW), BF16)
        if l > 0:
            load_w_in(l)
        mset("pool", wfst[:], 0.0)
        for g in range(4):
            c_, o_ = g // 2, (g % 2) * 64
            dma("sp", wfst[o_:o_ + 64, c_, o_:o_ + 64], wf_d[l, g], "c2", group="wf%d" % l)
        cp("dve", wfbd[:], wfst[:])
        dma("pool", wpw2[:], wco_d[l].rearrange("(c p) d -> p c d", p=128), "wpw")
        for zz, W_ in ((zl, N), (zc_, NC)):
            mset("pool", zz[:, :, 0:15], 0.0)
            mset("pool", zz[:, :, W_ + 15:W_ + 30], 0.0)
        sq = AV(MIX_O + 16384, (KD, 512), BF16)
        hTs = [AV(MIX_O + 24576, (KD, 512), BF16), AV(Z_O + 9472, (KD, 512), BF16)]
        rstd = AV(MIX_O + 32768, (512,), F32); sqrtt = AV(MIX_O + 34816, (512,), F32)
        TO = Z_O + 9472 + 8192
        tmpsA = _Rot([AV(TO, (512,), F32), AV(TO + 2048, (512,), F32)])
        qraws = _Rot([AV(TO + 4096, (512,), BF16), AV(TO + 5120, (512,), BF16)])
        t1s = _Rot([AV(TO + 6144, (512,), F32), AV(TO + 8192, (512,), F32)])
        t2s = _Rot([AV(TO + 10240, (512,), F32), AV(TO + 12288, (512,), F32)])
        sigs = _Rot([AV(TO + 14336, (512,), F32), AV(TO + 16384, (512,), F32)])
        def p1_load(ti):
            t0, W = all_tiles[ti]
            xin = AV(MIX_O, (KD, W), F32)
            dma("sp", xin, xs_tile(t0, W), "xin")

        def p1_norm(ti):
            t0, W = all_tiles[ti]
            col = 1 if t0 >= N else 0
            xin = AV(MIX_O, (KD, W), F32)
            norm_tile(l, 0, 0, xin, W, col, hTs[ti % 2], sq, rstd, sqrtt, tmpsA)

        def p1_norm_sq(ti):
            t0, W = all_tiles[ti]
            norm_sq(AV(MIX_O, (KD, W), F32), W, sq)

        def p1_norm_rest(ti):
            t0, W = all_tiles[ti]
            col = 1 if t0 >= N else 0
            norm_rest(l, 0, 0, AV(MIX_O, (KD, W), F32), W, col, hTs[ti % 2], sq, rstd, sqrtt, tmpsA)

        def rope_tail(qraw, t1, dst, t0, W):
            t2 = t2s.next()
            pb2 = PS(pbr.next())
            mm(pb2[:, 0:W], perm_bf, qraw[:, 0:W])
            tt("dve", t2[:, 0:W], pb2[:, 0:W], ropes[:, t0:t0 + W], ALU.mult)
            tt("pool", dst, t1[:, 0:W], t2[:, 0:W], ALU.add)

        nt_ = len(all_tiles)
        if l == 0:
            p1_norm(0)
        else:
            t0_, W_ = all_tiles[0]
            norm_tile(l, 0, 0, XRES[:, :, t0_:t0_ + W_], W_, 0, hTs[0], sq, rstd, sqrtt, tmpsA)
        if nt_ > 1:
            p1_load(1)
        for ti, (t0, W) in enumerate(all_tiles):
            is_ctx = t0 >= N
            col = 1 if is_ctx else 0
            hT = hTs[ti % 2]
            pend_rope = None
            ctx_kv_only = is_ctx and last
            for ch in range(8):
                if ctx_kv_only and ch < 4:
                    continue
                pb = PS(pbr.next())
                for k in range(KD):
                    mm(pb[:, 0:W], w_in[:, k, ch * 128:(ch + 1) * 128], hT[:, k, 0:W], start=(k == 0), stop=(k == KD - 1))
                dst = (QT if ch < 4 else KT)[:, ch % 4, t0:t0 + W]
                if ch == 3 and ti + 1 < nt_:
                    p1_norm_sq(ti + 1)
                if is_ctx:
                    cp(cpr.next(), dst, pb[:, 0:W])
                else:
                    qraw = qraws.next(); t1 = t1s.next()
                    cp("act", qraw[:, 0:W], pb[:, 0:W])
                    tt("dve", t1[:, 0:W], pb[:, 0:W], ropec[:, t0:t0 + W], ALU.mult)
                    if pend_rope is not None:
                        rope_tail(*pend_rope)
                    pend_rope = (qraw, t1, dst, t0, W)
            bg_mods(l, pbr.next())
            if ti + 1 < nt_:
                p1_norm_rest(ti + 1)
                if ti + 2 < nt_:
                    p1_load(ti + 2)
            for s_ in range(W // 128):
                pb = PS(pbr.next())
                for k in range(KD):
                    mm(pb[:, :], hT[:, k, s_ * 128:(s_ + 1) * 128], w_in[:, k, 1024:1536], start=(k == 0), stop=(k == KD - 1))
                tb = (t0 // 128) + s_
                cp(cpr.next(), V[:, tb, :, 0:128], pb[:, :].rearrange("p (h d) -> p h d", d=128))
                if s_ == 0 and pend_rope is not None:
                    rope_tail(*pend_rope)
                    pend_rope = None
            if ctx_kv_only:
                bg_mods(l, pbr.next())
                continue
            for c_ in range(2):
                pb = PS(pbr.next())
                for k in range(KD):
                    mm(pb[:, 0:W], w_in[:, k, 1536 + c_ * 128:1536 + (c_ + 1) * 128], hT[:, k, 0:W], start=(k == 0), stop=(k == KD - 1))
                cp(cpr.next(), ufT[:, c_, t0:t0 + W], pb[:, 0:W])
            for c_ in range(2):
                pa = PS(pbr.next()); pg = PS(pbr.next())
                for k in range(KD):
                    mm(pa[:, 0:W], w_in[:, k, 1792 + c_ * 128:1792 + (c_ + 1) * 128], hT[:, k, 0:W], start=(k == 0), stop=(k == KD - 1))
                for k in range(KD):
                    mm(pg[:, 0:W], w_in[:, k, 2048 + c_ * 128:2048 + (c_ + 1) * 128], hT[:, k, 0:W], start=(k == 0), stop=(k == KD - 1))
                sg = sigs.next()
                act(sg[:, 0:W], pg[:, 0:W], AF.Sigmoid)
                zdst = zc_[:, c_, 15:15 + W] if is_ctx else zl[:, c_, 15 + t0:15 + t0 + W]
                tt("dve", zdst, pa[:, 0:W], sg[:, 0:W], ALU.mult)
            bg_mods(l, pbr.next())

        WOUT_O = 159744
        w_out = AV(WOUT_O, (KD, D), BF16)
        for k in range(KD):
            dma("pool", w_out[:, k, :], w_out_d[l, k * 128:(k + 1) * 128, :], "wout%d" % (k % 2), group="l%d" % l)

        E_O = WIN_O
        QT_W = 256
        Ebufs = _Rot([AV(E_O + 24576 + i * 1024, (512,), BF16) for i in range(4)])
        rrs = _Rot([AV(E_O + 3072 + i * 2048, (512,), F32) for i in range(2)])
        t12s = _Rot([AV(E_O + 7168 + i * 2048, (512,), F32) for i in range(2)])
        ofs = _Rot([AV(E_O + 11264 + i * 1024, (256,), F32) for i in range(3)])
        sqb = _Rot([AV(E_O + 14336 + i * 512, (256,), BF16) for i in range(3)])
        lnb = _Rot([AV(E_O + 15872 + i * 1024, (256,), F32) for i in range(2)])
        rsb = _Rot([AV(E_O + 17920 + i * 1024, (256,), F32) for i in range(2)])
        sbr = _Rot([0, 1, 6])
        ssr = _Rot([7])
        qsets = []
        for q0 in range(0, N, QT_W):
            qsets.append((q0, QT_W, list(range(TB))))
        if not last:
            for q0 in range(0, NC, QT_W):
                qsets.append((N + q0, min(QT_W, NC - q0), list(range(NB, TB))))

        Qz = [[AV(E_O + 20480 + (hpar * 2 + tp_) * 1024, (2, 256), BF16) for tp_ in range(2)] for hpar in range(2)]
        for hpar in range(2):
            for tp_ in range(2):
                mset("pool", Qz[hpar][tp_][:, :, :], 0.0)

        def att_Q(h, q0, QW, tpar):
            hp = slice((h % 2) * 64, (h % 2) * 64 + 64)
            c1, c2 = h // 2, 2 + h // 2
            qz = Qz[h % 2][tpar]
            cp("pool", qz[hp, 0, 0:QW], QT[hp, c1, q0:q0 + QW])
            cp("pool", qz[hp, 1, 0:QW], QT[hp, c2, q0:q0 + QW])

        def att_S(h, q0, QW, kb, tpar):
            c1, c2 = h // 2, 2 + h // 2
            qz = Qz[h % 2][tpar]
            sb = PS(sbr.next())
            mm(sb[:, 0:QW], KT[:, c1, kb * 128:(kb + 1) * 128], qz[:, 0, 0:QW])
            mm(sb[:, 256:256 + QW], KT[:, c2, kb * 128:(kb + 1) * 128], qz[:, 1, 0:QW])
            E = Ebufs.next()
            if QW == 256:
                act(E[:, 0:512], sb[:, 0:512], AF.Exp, scale=0.125)
            else:
                act(E[:, :].rearrange("p (m q) -> p m q", m=2)[:, :, 0:QW], sb[:, :].rearrange("p (m q) -> p m q", m=2)[:, :, 0:QW], AF.Exp, scale=0.125)
            return E

        def att_PV(h, QW, ki, nk, kb, E, par):
            ob = PS(2 + par); sk = PS(4 + par)
            f, l_ = (ki == 0), (ki == nk - 1)
            mm(ob[:, 0:QW], V[:, kb, h, 0:128], E[:, 0:QW], start=f, stop=False)
            mm(ob[:, 256:256 + QW], V[:, kb, h, 0:128], E[:, 256:256 + QW], start=False, stop=l_)
            mm(sk[:, 0:QW], ones_bf, E[:, 0:QW], start=f, stop=False)
            mm(sk[:, 256:256 + QW], ones_bf, E[:, 256:256 + QW], start=False, stop=l_)

        def att_fin_a(h, q0, QW, par):
            ob = PS(2 + par); sk = PS(4 + par)
            rr = rrs.next(); t12 = t12s.next(); of = ofs.next(); sq_ = sqb.next()
            if QW == 256:
                recip(rr[:, 0:512], sk[:, 0:512])
                tt("dve", t12[:, 0:512], ob[:, 0:512], rr[:, 0:512], ALU.mult)
            else:
                v3 = lambda a_: a_[:, :].rearrange("p (m q) -> p m q", m=2)[:, :, 0:QW]
                recip(v3(rr), v3(sk))
                tt("dve", v3(t12), v3(ob), v3(rr), ALU.mult)
            stt(of[:, 0:QW], t12[:, 256:256 + QW], neglam[:, l:l + 1], t12[:, 0:QW], ALU.mult, ALU.add)
            tt("dve", sq_[:, 0:QW], of[:, 0:QW], of[:, 0:QW], ALU.mult)
            return (h, q0, QW, of, sq_)

        def att_fin_b(h, q0, QW, of, sq_):
            ssb = PS(ssr.next())
            mm(ssb[:, 0:QW], ones_bf, sq_[:, 0:QW])
            ln_ = lnb.next(); rs_ = rsb.next()
            act(ln_[:, 0:QW], ssb[:, 0:QW], AF.Ln, scale=1.0 / 128, bias=epsT[:, 0:1])
            act(rs_[:, 0:QW], ln_[:, 0:QW], AF.Exp, scale=-0.5)
            stt(mixT[:, h, q0:q0 + QW], of[:, 0:QW], gcol[:, l:l + 1], rs_[:, 0:QW], ALU.mult, ALU.mult)

        stages = []
        tiles_ = []
        tno = 0
        for h in range(4):
            for (q0, QW, kbs) in qsets:
                tiles_.append((h, q0, QW, tno))
                for ki, kb in enumerate(kbs):
                    stages.append((h, q0, QW, ki, len(kbs), kb, tno))
                tno += 1
        hcnt = [0, 0]
        tpar_of = {}
        for (h, q0, QW, tn) in tiles_:
            tpar_of[tn] = hcnt[h % 2] % 2
            hcnt[h % 2] += 1
        att_Q(tiles_[0][0], tiles_[0][1], tiles_[0][2], tpar_of[0])
        LA = 2
        fifo = []
        pend = []

        def do_pv(it):
            ph, pq0, pQW, pki, pnk, pkb, ppar, pE = it
            att_PV(ph, pQW, pki, pnk, pkb, pE, ppar)
            if pki == pnk - 1:
                pend.append([10, att_fin_a(ph, pq0, pQW, ppar)])

        for st in stages:
            h, q0, QW, ki, nk, kb, tn = st
            par = tn % 2
            if ki == 0 and tn + 1 < len(tiles_):
                nh, nq0, nQW, ntn = tiles_[tn + 1]
                att_Q(nh, nq0, nQW, tpar_of[ntn])
            if ki == 4:
                bg_mods(l + 1, 7)
            E = att_S(h, q0, QW, kb, tpar_of[tn])
            fifo.append((h, q0, QW, ki, nk, kb, par, E))
            if len(fifo) > LA:
                do_pv(fifo.pop(0))
            for it in pend:
                it[0] -= 1
            while pend and pend[0][0] <= 0:
                att_fin_b(*pend.pop(0)[1])
        while fifo:
            do_pv(fifo.pop(0))

        while bg_mods(l + 1, 7):
            pass

        AB = AV(0, (TB, 512), BF16)
        DFB = [(AV(18432 + i * 8192, (NB // 2, 256), BF16), AV(18432 + i * 8192 + 4096, (NB // 2, 256), BF16)) for i in range(2)]
        FTs = _Rot([AV(E_O + i * 1024, (2, 256), BF16) for i in range(2)])
        pbr3 = _Rot(range(8))
        NH = NB // 2
        ABp = AV(34816, (NH, 512), BF16); ABm = AV(34816 + NH * 1024, (NH, 512), BF16)

        def stage_a(tb):
            pb = PS(pbr3.next())
            for c_ in range(2):
                mm(pb[:, c_ * 256:(c_ + 1) * 256], ufT[:, c_, tb * 128:(tb + 1) * 128], cmat[:, 3:5, :].rearrange("p a b -> p (a b)"))
            cp(cpr.next(), AB[:, tb, :], pb[:, :])

        for i in range(NH):
            stage_a(i); stage_a(i + NH)
            tt("dve", ABp[:, i, :], AB[:, i, :], AB[:, i + NH, :], ALU.add)
            tt("pool", ABm[:, i, :], AB[:, i, :], AB[:, i + NH, :], ALU.subtract)
        if not last:
            for tb in range(NB, TB):
                stage_a(tb)
        while pend:
            att_fin_b(*pend.pop(0)[1])

        pend_f = None

        def fourier_out1(pb, Wk, tok0):
            FT = FTs.next()
            cp("act", FT[:, :, 0:Wk], pb[:, :].rearrange("p (c k) -> p c k", c=2)[:, :, 0:Wk])
            return (FT, Wk, tok0)

        def fourier_out(pb, Wk, tok0):
            fourier_out2(*fourier_out1(pb, Wk, tok0))

        def fourier_out2(FT, Wk, tok0):
            pb2 = PS(pbr3.next())
            for c_ in range(2):
                mm(pb2[:, c_ * 256:c_ * 256 + Wk], wfbd[:, c_, :], FT[:, c_, 0:Wk])
            dst = tok0 if not isinstance(tok0, int) else mixT[:, 4:6, tok0:tok0 + Wk]
            cp("dve", dst, pb2[:, :].rearrange("p (c k) -> p c k", c=2)[:, :, 0:Wk])

        step = 0
        for par, ABx in ((0, ABp), (1, ABm)):
            for kc in range(NKC // 2):
                Cb, Sb = DFB[step % 2]
                dma("sp", Cb, dftc_d[par, kc], "dfc%d" % (step % 2)); dma("sp", Sb, dfts_d[par, kc], "dfs%d" % (step % 2))
                step += 1
                pb = PS(pbr3.next())
                for c_ in range(2):
                    for nb in range(NH):
                        mm(pb[:, c_ * 256:(c_ + 1) * 256], ABx[:, nb, c_ * 256:c_ * 256 + 128], Cb[:, nb, :], start=(nb == 0), stop=False)
                        mm(pb[:, c_ * 256:(c_ + 1) * 256], ABx[:, nb, c_ * 256 + 128:c_ * 256 + 256], Sb[:, nb, :], start=False, stop=(nb == NH - 1))
                if pend_f is not None:
                    fourier_out2(*pend_f)
                dst = mixT[:, 4:6, kc * 512:(kc + 1) * 512].rearrange("p c (j two) -> p c j two", two=2)[:, :, :, par]
                pend_f = fourier_out1(pb, 256, dst)
        if pend_f is not None:
            fourier_out2(*pend_f)
            pend_f = None
        if not last:
            for k0 in range(0, NC, 256):
                Wk = min(256, NC - k0)
                pb = PS(pbr3.next())
                for c_ in range(2):
                    for nb in range(NCB):
                        mm(pb[:, c_ * 256:c_ * 256 + Wk], AB[:, NB + nb, c_ * 256:c_ * 256 + 128], dcc[:, nb, k0:k0 + Wk], start=(nb == 0), stop=False)
                        mm(pb[:, c_ * 256:c_ * 256 + Wk], AB[:, NB + nb, c_ * 256 + 128:c_ * 256 + 256], dcs[:, nb, k0:k0 + Wk], start=False, stop=(nb == NCB - 1))
                fourier_out(pb, Wk, N + k0)

        DG_O = WIN_O + 16384
        diag = AV(DG_O, (CONV_K, 2, 128), BF16)
        for j in range(CONV_K):
            for c_ in range(2):
                r = vb + R_CW + j * 2 + c_
                ts("dve", diag[:, j, c_, :], ident_bf, vT[:, r:r + 1], ALU.mult)
        C_O = 51200
        zcs = _Rot([AV(C_O + i * 2048, (512,), F32) for i in range(3)])
        cens = _Rot([AV(C_O + 6144 + i * 2048, (512,), F32) for i in range(3)])
        sqs = _Rot([AV(C_O + 12288 + i * 2048, (512,), F32) for i in range(3)])
        rsd = _Rot([AV(C_O + 18432 + i * 2048, (512,), F32) for i in range(2)])
        sils = _Rot([AV(E_O + 4096 + i * 2048, (2, 512), BF16) for i in range(2)])
        segs = [(zl, t0, W, t0) for (t0, W) in lat_tiles]
        if not last:
            segs.append((zc_, 0, NC, N))
        items = []
        for si, (zz, s0, W, tok0) in enumerate(segs):
            sil = sils.next()
            for c_ in range(2):
                items.append(dict(zz=zz, s0=s0, W=W, tok0=tok0, c=c_, sil=sil))

        def cv_A(it):
            W = it["W"]; c_ = it["c"]
            pb = PS(pbr3.next())
            for j in range(CONV_K):
                mm(pb[:, 0:W], diag[:, j, c_, :], it["zz"][:, c_, it["s0"] + j:it["s0"] + j + W], start=(j == 0), stop=(j == CONV_K - 1))
            it["zcv"] = zcs.next()
            act(it["zcv"][:, 0:W], pb[:, 0:W], AF.Identity, bias=vT[:, vb + R_CB + c_: vb + R_CB + c_ + 1])

        def cv_B(it):
            W = it["W"]
            pm = PS(pbr3.next())
            mm(pm[:, 0:W], gln_f, it["zcv"][:, 0:W])
            it["cen"] = cens.next(); it["sq"] = sqs.next()
            tt("dve", it["cen"][:, 0:W], it["zcv"][:, 0:W], pm[:, 0:W], ALU.subtract)
            act(it["sq"][:, 0:W], it["cen"][:, 0:W], AF.Square)

        def cv_C(it):
            W = it["W"]; c_ = it["c"]
            pv_ = PS(pbr3.next())
            mm(pv_[:, 0:W], gln_f, it["sq"][:, 0:W])
            rs_ = rsd.next()
            act(rs_[:, 0:W], pv_[:, 0:W], AF.Ln, bias=epsT[:, 0:1])
            act(rs_[:, 0:W], rs_[:, 0:W], AF.Exp, scale=-0.5)
            zn = it["sq"]
            tt("dve", zn[:, 0:W], it["cen"][:, 0:W], rs_[:, 0:W], ALU.mult)
            act(it["sil"][:, c_, 0:W], zn[:, 0:W], AF.Silu, scale=vT[:, vb + R_LG + c_: vb + R_LG + c_ + 1], bias=vT[:, vb + R_LB + c_: vb + R_LB + c_ + 1])

        def cv_D(it):
            W = it["W"]
            for dc in range(2):
                pb = PS(pbr3.next())
                for c_ in range(2):
                    mm(pb[:, 0:W], wpw2[:, c_, dc * 128:(dc + 1) * 128], it["sil"][:, c_, 0:W], start=(c_ == 0), stop=(c_ == 1))
                cp(cpr.next(), mixT[:, 6 + dc, it["tok0"]:it["tok0"] + W], pb[:, 0:W])

        ni = len(items)
        for i in range(ni + 4):
            if i < ni:
                cv_A(items[i])
            if 0 <= i - 1 < ni:
                cv_B(items[i - 1])
            if 0 <= i - 2 < ni:
                cv_C(items[i - 2])
            if 0 <= i - 3 < ni and items[i - 3]["c"] == 1:
                cv_D(items[i - 3])

        FW_O = WIN_O
        XIN5 = FW_O + 24576
        SQ5 = XIN5 + 16384
        T5 = 176128
        rstd5 = AV(T5, (512,), F32); sqrtt5 = AV(T5 + 2048, (512,), F32)
        tmps5 = _Rot([AV(T5 + 4096, (512,), F32), AV(T5 + 6144, (512,), F32)])
        sq5 = AV(SQ5, (KD, 512), BF16)
        tiles5 = all_tiles if not last else lat_tiles
        pend5 = None
        for (t0, W) in tiles5:
            col = 1 if t0 >= N else 0
            xin = AV(XIN5, (KD, W), F32)
            for m_ in range(KD):
                dma("sp", xin[:, m_, :], xs_tile(t0, W)[:, m_, :], "xin%d" % (m_ % 2))
            if pend5 is not None:
                norm_sq(XRES[:, :, pend5[0]:pend5[0] + pend5[1]], pend5[1], sq5)
            for m_ in range(KD):
                pb = PS(pbr3.next())
                for k in range(KD):
                    mm(pb[:, 0:W], w_out[:, k, m_ * 128:(m_ + 1) * 128], mixT[:, k, t0:t0 + W], start=(k == 0), stop=(k == KD - 1))
                stt(XRES[:, m_, t0:t0 + W], pb[:, 0:W], modT[:, l, 2 * 8 + m_, col:col + 1], xin[:, m_, :], ALU.mult, ALU.add)
                if m_ == 3 and pend5 is not None:
                    pt0, pW, pcol = pend5
                    norm_rest(l, 1, 3, XRES[:, :, pt0:pt0 + pW], pW, pcol, mixT[:, :, pt0:pt0 + pW], sq5, rstd5, sqrtt5, tmps5, bank=pbr3.next())
                    pend5 = None
            pend5 = (t0, W, col)
        pt0, pW, pcol = pend5
        norm_sq(XRES[:, :, pt0:pt0 + pW], pW, sq5)
        norm_rest(l, 1, 3, XRES[:, :, pt0:pt0 + pW], pW, pcol, mixT[:, :, pt0:pt0 + pW], sq5, rstd5, sqrtt5, tmps5, bank=pbr3.next())
        h2T = mixT

        G = 4
        groups = [(g0, min(G, NFF - g0)) for g0 in range(0, NFF, G)]
        if groups[-1][1] < G:
            groups = [groups[-1]] + groups[:-1]
        fwb = [(AV(FW_O + i * 24576, (KD, 512), BF16), AV(FW_O + i * 24576 + 8192, (KD, 512), BF16), AV(FW_O + i * 24576 + 16384, (G, D), BF16)) for i in range(2)]
        ACT_O = 159744
        actb = _Rot([AV(ACT_O + i * 4096, (G, 512), BF16) for i in range(2)])
        sgs = _Rot([AV(ACT_O + 8192 + i * 2048, (512,), F32) for i in range(2)])
        upb = _Rot([0, 1, 2, 3]); dnb = _Rot([4, 5, 6, 7])
        pending = None

        def down(g0, gn, t0, W, ab, w2b, col, is_last_group=False):
            for m_ in range(KD):
                pb = PS(dnb.next())
                for jj in range(gn):
                    mm(pb[:, 0:W], w2b[:, jj, m_ * 128:(m_ + 1) * 128], ab[:, jj, 0:W], start=(jj == 0), stop=(jj == gn - 1))
                stt(XRES[:, m_, t0:t0 + W], pb[:, 0:W], modT[:, l, 5 * 8 + m_, col:col + 1], XRES[:, m_, t0:t0 + W], ALU.mult, ALU.add)

        for gi, (g0, gn) in enumerate(groups):
            w1b, w3b, w2b = fwb[gi % 2]
            pass
            cw = gn * 128
            dma("pool", w1b[:, :, 0:cw], w1_d[l, :, g0 * 128:g0 * 128 + cw].rearrange("(k p) n -> p k n", p=128), "fw1_%d" % (gi % 2))
            dma("pool", w3b[:, :, 0:cw], w3_d[l, :, g0 * 128:g0 * 128 + cw].rearrange("(k p) n -> p k n", p=128), "fw3_%d" % (gi % 2))
            dma("pool", w2b[:, 0:gn, :], w2_d[l, g0 * 128:g0 * 128 + cw, :].rearrange("(j p) n -> p j n", p=128), "fw2_%d" % (gi % 2))
            for (t0, W) in tiles5:
                col = 1 if t0 >= N else 0
                ab = actb.next()
                for jj in range(gn):
                    pa = PS(upb.next()); pg = PS(upb.next())
                    for k in range(KD):
                        mm(pa[:, 0:W], w1b[:, k, jj * 128:(jj + 1) * 128], h2T[:, k, t0:t0 + W], start=(k == 0), stop=(k == KD - 1))
                    for k in range(KD):
                        mm(pg[:, 0:W], w3b[:, k, jj * 128:(jj + 1) * 128], h2T[:, k, t0:t0 + W], start=(k == 0), stop=(k == KD - 1))
                    sg = sgs.next()
                    act(sg[:, 0:W], pa[:, 0:W], AF.Silu)
                    tt("dve", ab[:, jj, 0:W], pg[:, 0:W], sg[:, 0:W], ALU.mult)
                if pending is not None:
                    down(*pending)
                    if pending[-1] and not last:
                        pt0, pW = pending[2], pending[3]
                        dma("sp", xs_tile(pt0, pW), XRES[:, :, pt0:pt0 + pW], "xst%d" % ((pt0 // 512) % 2))
                pending = (g0, gn, t0, W, ab, w2b, col, gi == len(groups) - 1)
        down(*pending)
        if not last:
            pt0, pW = pending[2], pending[3]
            dma("sp", xs_tile(pt0, pW), XRES[:, :, pt0:pt0 + pW], "xst%d" % ((pt0 // 512) % 2))
        pending = None

    FB = 73728
    fgr = DEPTH * VROWS
    gbc = AV(FB, (D,), F32)
    junkF = AV(FB + 4096, (512,), F32)
    otl = [AV(FB + 6144 + i * 4096, (D,), F32) for i in range(3)]
    ssqF = ST("ssqF", [128, NB, 4], F32)
    dma("sp", gbc, vecs_d[fgr:fgr + 8, :].rearrange("a b -> (a b)").partition_broadcast(128), "c0")
    pbrF = _Rot(range(8))

    def fin_T(blk):
        banks = []
        for kq in range(2):
            pb = PS(pbrF.next())
            for kk in range(4):
                tr(pb[:, kk * 128:(kk + 1) * 128], XRES[:, kq * 4 + kk, blk * 128:(blk + 1) * 128], ident_f)
            act(junkF, pb[:, :], AF.Square, accum=ssqF[:, blk, kq:kq + 1])
            banks.append(pb)
        return banks

    def fin_E(blk, banks):
        tt("dve", ssqF[:, blk, 2:3], ssqF[:, blk, 0:1], ssqF[:, blk, 1:2], ALU.add)
        ts("dve", ssqF[:, blk, 2:3], ssqF[:, blk, 2:3], 1.0 / D, ALU.mult, EPS, ALU.add)
        tt("pool", ssqF[:, blk, 3:4], ssqF[:, blk, 2:3], small[:, 0:1], ALU.pow)
        ot = otl[blk % 3]
        for kq in range(2):
            stt(ot[:, kq * 512:(kq + 1) * 512], banks[kq][:, :], ssqF[:, blk, 3:4], gbc[:, kq * 512:(kq + 1) * 512], ALU.mult, ALU.mult)
        dma("sp", out_d[blk * 128:(blk + 1) * 128, :], ot, "ost%d" % (blk % 3))

    prevF = None
    for blk in range(NB):
        banks = fin_T(blk)
        if prevF is not None:
            fin_E(*prevF)
        prevF = (blk, banks)
    fin_E(*prevF)

    P.finalize(None)
    streams = []
    for o in P.ops:
        s = P._stream(o)
        if s not in streams:
            streams.append(s)
    sems = {s: es.enter_context(nc.semaphore("sem_%s_%s" % s)) for s in streams}
    out_slots = [s for s in streams if s[0] == "slot" and s[1].startswith("ost")]
    block = es.enter_context(nc.Block())

    @block.tensor
    def _(e):
        P.emit_engine("pe", e, sems)

    @block.scalar
    def _(e):
        P.emit_engine("act", e, sems)

    @block.vector
    def _(e):
        P.emit_engine("dve", e, sems)

    @block.gpsimd
    def _(e):
        P.emit_engine("pool", e, sems)

    @block.sync
    def _(e):
        P.emit_engine("sp", e, sems)
        for s in out_slots:
            e.wait_ge(sems[s], P.max_vals[s])

    es.close()
    return nc, P


def _const_tables(N, NC):
    bf = ml_dtypes.bfloat16
    rows = N // GRID_W
    t = np.arange(N)
    row = (t // GRID_W).astype(np.float64)
    colp = (t % GRID_W).astype(np.float64)
    inv_freq = ROPE_BASE ** (-np.arange(16, dtype=np.float64) / 16)
    ropec = np.zeros((128, N)); ropes = np.zeros((128, N))
    perm = np.zeros((128, 128))
    for p in range(128):
        d = p % 64
        axis, half, f = d // 32, (d % 32) // 16, d % 16
        ang = (row if axis == 0 else colp) * inv_freq[f]
        ropec[p] = np.cos(ang)
        ropes[p] = -np.sin(ang) if half == 0 else np.sin(ang)
        partner = p + 16 if half == 0 else p - 16
        perm[partner, p] = 1.0
    ident = np.eye(128)
    ones = np.ones((128, 128))
    cc = np.arange(64)
    a64 = 2 * np.pi * np.outer(cc, cc) / 64
    c64 = np.cos(a64) / 8.0; s64 = np.sin(a64) / 8.0
    c64bd = np.zeros((128, 128)); s64bd = np.zeros((128, 128))
    gln = np.zeros((128, 128))
    for g in range(2):
        c64bd[g * 64:(g + 1) * 64, g * 64:(g + 1) * 64] = c64
        s64bd[g * 64:(g + 1) * 64, g * 64:(g + 1) * 64] = s64
        gln[g * 64:(g + 1) * 64, g * 64:(g + 1) * 64] = 1.0 / 64
    cmat = np.stack([ident, ones, perm, c64bd, s64bd], axis=1).astype(bf)
    cmatf = np.stack([ident, gln], axis=1).astype(np.float32)

    NB = N // 128
    NHt = N // 2
    n_ = np.arange(NHt)
    kp = np.arange(NHt)
    ang_e = 2 * np.pi * (np.outer(n_, kp) % NHt) / NHt
    ang_o = 2 * np.pi * (np.outer(n_, 2 * kp + 1) % N) / N
    sc_ = 1.0 / np.sqrt(N)

    def lay(Mx):
        return np.ascontiguousarray(Mx.reshape(NB // 2, 128, NHt // 256, 256).transpose(2, 1, 0, 3))

    dftc_h = np.stack([lay(np.cos(ang_e) * sc_), lay(np.cos(ang_o) * sc_)], axis=0).astype(bf)
    dfts_h = np.stack([lay(-np.sin(ang_e) * sc_), lay(-np.sin(ang_o) * sc_)], axis=0).astype(bf)

    def dft(M):
        n = np.arange(M)
        ang = 2 * np.pi * (np.outer(n, n) % M) / M
        return np.cos(ang) / np.sqrt(M), -np.sin(ang) / np.sqrt(M)

    Cc, Sc = dft(NC)
    NCB = NC // 128

    def layc(Mx):
        return np.ascontiguousarray(Mx.reshape(NCB, 128, NC).transpose(1, 0, 2)).astype(bf)

    return dict(ropec=ropec.astype(bf), ropes=ropes.astype(bf), cmat=cmat, cmatf=cmatf,
                dftc=dftc_h, dfts=dfts_h, dcc=layc(Cc), dcs=layc(Sc))


def _pack_vecs(inp, DEPTH):
    vecs = np.zeros((384, 128), np.float32)
    for l in range(DEPTH):
        b = l * VROWS
        vecs[b + R_BADA:b + R_BADA + 48] = np.asarray(inp["b_ada"][l], np.float32).reshape(48, 128)
        vecs[b + R_N1:b + R_N1 + 8] = np.asarray(inp["norm1_g"][l], np.float32).reshape(8, 128)
        vecs[b + R_N2:b + R_N2 + 8] = np.asarray(inp["norm2_g"][l], np.float32).reshape(8, 128)
        vecs[b + R_CB:b + R_CB + 2] = np.asarray(inp["conv_b"][l], np.float32).reshape(2, 128)
        vecs[b + R_LG:b + R_LG + 2] = np.asarray(inp["conv_ln_g"][l], np.float32).reshape(2, 128)
        vecs[b + R_LB:b + R_LB + 2] = np.asarray(inp["conv_ln_b"][l], np.float32).reshape(2, 128)
        vecs[b + R_CW:b + R_CW + 62] = np.asarray(inp["conv_w"][l], np.float32).reshape(62, 128)
    vecs[DEPTH * VROWS:DEPTH * VROWS + 8] = np.asarray(inp["final_g"], np.float32).reshape(8, 128)
    return vecs


def make_in_maps(inp, N, NC, DEPTH, B):
    f = lambda a: np.ascontiguousarray(np.asarray(a, np.float32))
    consts = _const_tables(N, NC)
    shared = dict(consts)
    shared["vecs"] = _pack_vecs(inp, DEPTH)
    for k in ("w_ada", "w_in", "subln_g", "w_fourier", "w_conv_out", "w_out", "w_ffn1", "w_ffn3", "w_ffn2"):
        shared[k] = f(inp[k])
    shared["lamv"] = np.ascontiguousarray(np.stack([f(inp["lam_q1"]), f(inp["lam_k1"]), f(inp["lam_q2"]), f(inp["lam_k2"])], axis=1))
    x = f(inp["x"]); ctx = f(inp["ctx"]); c = f(inp["c"]); c_ctx = f(inp["c_ctx"])
    maps = []
    for b in range(B):
        m = dict(shared)
        m["x"] = x[b]; m["ctx"] = ctx[b]
        m["cc"] = np.ascontiguousarray(np.stack([c[b], c_ctx], axis=1))
        maps.append(m)
    return maps


_CACHE = {}


def kernel(**inputs):
    N, NC, DEPTH, B = 2048, 256, 2, 8
    if "nc" not in _CACHE:
        _CACHE["nc"] = build_program(N, NC, DEPTH)[0]
    nc = _CACHE["nc"]
    maps = make_in_maps(inputs, N, NC, DEPTH, B)
    res = run_bass_kernel_spmd(nc, maps, core_ids=list(range(B)))
    return np.stack([np.asarray(r["out"], np.float32) for r in res.results], axis=0)
```

```python
import numpy as np
import ml_dtypes
import concourse.bass as bass
import concourse.mybir as mybir
from concourse.bass_utils import run_bass_kernel_spmd

F32 = mybir.dt.float32
BF16 = mybir.dt.bfloat16
AF = mybir.ActivationFunctionType
ALU = mybir.AluOpType


class _Op:
    __slots__ = ("eng", "fn", "deps", "sem", "val", "signal", "slot", "group", "idx", "dbg")


def _box(ap):
    t = ap.tensor
    shp = tuple(t.shape)
    pat = ap.ap
    off = int(ap.offset)
    sp = str(ap.space)
    esz = mybir.dt.size(ap.dtype)
    if sp == "DRAM":
        lo = off * esz
        hi = (off + sum((c - 1) * abs(s) for s, c in pat) + 1) * esz - 1
        return (ap.name, 0, 0, lo, hi, False)
    pstride = 1
    for d in shp[1:]:
        pstride *= d
    plo = off // pstride
    flo = off % pstride
    assert pat[0][0] == pstride or pat[0][1] == 1, (pat, pstride)
    phi = plo + pat[0][1] - 1
    fhi = flo + sum((c - 1) * abs(s) for s, c in pat[1:])
    lo = flo * esz
    hi = (fhi + 1) * esz - 1
    if sp == "PSUM":
        lo = (lo // 2048) * 2048
        hi = (hi // 2048) * 2048 + 2047
        return (ap.name + "@" + sp, 0, 127, lo, hi, True)
    return (ap.name + "@" + sp, plo, phi, lo, hi, False)


class Prog:
    ENGS = ("pe", "act", "dve", "pool", "sp")

    def __init__(self, nc):
        self.nc = nc
        self.ops = []
        self.track = {}
        self.slot_groups = {}
        self.slot_cur = {}

    def add(self, eng, fn, reads=(), writes=(), slot=None, group=None):
        op = _Op()
        op.eng = eng
        op.fn = fn
        op.deps = set()
        op.signal = False
        op.slot = slot
        op.idx = len(self.ops)
        op.group = None
        op.dbg = None
        if DEBUG_LINES:
            import sys as _sys
            fr = _sys._getframe(2)
            op.dbg = (fr.f_lineno, fr.f_back.f_lineno if fr.f_back else None)
        if slot is not None:
            groups = self.slot_groups.setdefault(slot, [])
            if groups and self.slot_cur.get(slot) == group and group is not None:
                groups[-1].append(op.idx)
            else:
                if groups:
                    op.deps.add(groups[-1][-1])
                groups.append([op.idx])
                self.slot_cur[slot] = group
            op.group = (slot, len(groups) - 1)
        for ap in reads:
            self._access(op, ap, False)
        for ap in writes:
            self._access(op, ap, True)
        self.ops.append(op)
        return op

    def _same_stream(self, a, b):
        if a.slot is not None or b.slot is not None:
            return a.slot is not None and a.slot == b.slot
        return a.eng == b.eng

    def _access(self, op, ap, is_write):
        key, plo, phi, lo, hi, excl = _box(ap)
        is_write = is_write or excl
        lst = self.track.setdefault(key, [])
        keep = []
        for rec in lst:
            rplo, rphi, rlo, rhi, ridx, rw = rec
            overlap = not (rphi < plo or rplo > phi or rhi < lo or rlo > hi)
            if overlap and (is_write or rw) and ridx != op.idx:
                op.deps.add(ridx)
            contained = rplo >= plo and rphi <= phi and rlo >= lo and rhi <= hi
            if contained and ridx != op.idx:
                if is_write:
                    continue
                if (not rw) and self._same_stream(self.ops[ridx], op):
                    continue
            keep.append(rec)
        keep.append([plo, phi, lo, hi, op.idx, is_write])
        self.track[key] = keep

    def finalize(self, block_sems):
        ops = self.ops
        def stream(o):
            return ("slot", o.slot) if o.slot is not None else ("eng", o.eng)
        pos = {}
        cnt = {}
        for o in ops:
            s = stream(o)
            cnt[s] = cnt.get(s, 0) + 1
            pos[o.idx] = cnt[s]
        def target(j):
            o = ops[j]
            if o.slot is not None:
                g = self.slot_groups[o.slot][o.group[1]]
                return g[-1]
            return j
        waited = {e: {} for e in self.ENGS}
        need = {}
        for o in ops:
            w = waited[o.eng]
            req = {}
            for j in o.deps:
                tj = target(j)
                if tj == o.idx:
                    continue
                if tj > o.idx:
                    assert ops[tj].slot is not None and ops[tj].group == o.group, "dep on future op"
                    continue
                s = stream(ops[tj])
                if s == ("eng", o.eng) and o.slot is None and not SAME_ENGINE_SYNC:
                    continue
                if s == ("eng", o.eng) and o.eng == "pe" and o.slot is None:
                    continue
                if s == ("eng", o.eng) and o.slot is not None and ops[tj].slot is None:
                    pass
                p = pos[tj]
                if w.get(s, 0) >= p:
                    continue
                if req.get(s, (0, None))[0] < p:
                    req[s] = (p, tj)
            lst = []
            for s, (p, tj) in req.items():
                w[s] = p
                lst.append((s, tj))
                ops[tj].signal = True
            need[o.idx] = lst
        for o in ops:
            if o.slot is not None:
                o.signal = True
        val = {}
        c = {}
        for o in ops:
            s = stream(o)
            if o.signal:
                c[s] = c.get(s, 0) + (16 if o.slot is not None else 1)
            val[o.idx] = c.get(s, 0)
        self._need, self._val, self._stream = need, val, stream
        self.max_vals = dict(c)
        return need, val

    def emit(self, engines, sems):
        need, val, stream = self._need, self._val, self._stream
        for o in self.ops:
            e = engines[o.eng]
            for s, tj in need[o.idx]:
                e.wait_ge(sems[s], val[tj])
            ins = o.fn(e)
            if o.signal:
                ins.then_inc(sems[stream(o)], 16 if o.slot is not None else 1)

    def emit_engine(self, eng_name, e, sems):
        need, val, stream = self._need, self._val, self._stream
        for o in self.ops:
            if o.eng != eng_name:
                continue
            for s, tj in need[o.idx]:
                e.wait_ge(sems[s], val[tj])
            ins = o.fn(e)
            if DEBUG_LINES:
                NAMES[getattr(ins.ins, "name", None)] = (o.idx, o.dbg)
            if o.signal:
                ins.then_inc(sems[stream(o)], 16 if o.slot is not None else 1)


SAME_ENGINE_SYNC = True
DEBUG_LINES = False
NAMES = {}


D = 1024
KD = D // 128
GRID_W = 64
INW = 2304
DFF = 2816
NFF = DFF // 128
CONV_K = 31
EPS = 1e-6
ROPE_BASE = 10000.0
VROWS = 132
R_BADA, R_N1, R_N2, R_CB, R_LG, R_LB, R_CW = 0, 48, 56, 64, 66, 68, 70


class _Rot:
    def __init__(self, items):
        self.items = list(items)
        self.i = 0

    def next(self):
        v = self.items[self.i % len(self.items)]
        self.i += 1
        return v


def build_program(N, NC, DEPTH):
    T = N + NC
    NB = N // 128
    NCB = NC // 128
    TB = NB + NCB
    lat_tiles = [(t0, 512) for t0 in range(0, N, 512)]
    all_tiles = lat_tiles + [(N, NC)]
    nc = bass.Bass("TRN2", target_bir_lowering=False)
    P = Prog(nc)

    def din(name, shape, dt=F32):
        return nc.dram_tensor(name, list(shape), dt, kind="ExternalInput").ap()

    x_d = din("x", [N, D]); ctx_d = din("ctx", [NC, D]); cc_d = din("cc", [D, 2])
    w_ada_d = din("w_ada", [DEPTH, D, 6 * D]); vecs_d = din("vecs", [384, 128])
    w_in_d = din("w_in", [DEPTH, D, INW]); lamv_d = din("lamv", [DEPTH, 4, 64])
    subln_d = din("subln_g", [DEPTH, 128]); wf_d = din("w_fourier", [DEPTH, 4, 64, 64])
    wco_d = din("w_conv_out", [DEPTH, 256, 256]); w_out_d = din("w_out", [DEPTH, D, D])
    w1_d = din("w_ffn1", [DEPTH, D, DFF]); w3_d = din("w_ffn3", [DEPTH, D, DFF]); w2_d = din("w_ffn2", [DEPTH, DFF, D])
    ropec_d = din("ropec", [128, N], BF16); ropes_d = din("ropes", [128, N], BF16)
    NKC = N // 256
    dftc_d = din("dftc", [2, NKC // 2, 128, NB // 2, 256], BF16); dfts_d = din("dfts", [2, NKC // 2, 128, NB // 2, 256], BF16)
    dcc_d = din("dcc", [128, NCB, NC], BF16); dcs_d = din("dcs", [128, NCB, NC], BF16)
    cmat_d = din("cmat", [128, 5, 128], BF16)
    cmatf_d = din("cmatf", [128, 2, 128], F32)
    out_d = nc.dram_tensor("out", [N, D], F32, kind="ExternalOutput").ap()
    NT_ALL = len(all_tiles)
    xs_d = nc.dram_tensor("xs", [NT_ALL, 128, KD, 512], F32).ap()

    def xs_tile(t0, W):
        ti_ = (t0 // 512) if t0 < N else (NT_ALL - 1)
        return xs_d[ti_, :, :, 0:W]
    modrow_d = nc.dram_tensor("modrow", [DEPTH, 2, 6 * D], F32).ap()

    from contextlib import ExitStack
    es = ExitStack()
    ARENA = 184320
    arena = es.enter_context(nc.sbuf_tensor("arena", [128, ARENA // 4], F32))
    psum = es.enter_context(nc.psum_tensor("ps", [128, 8, 512], F32))

    def AV(off, shape, dt, parts=128):
        esz = mybir.dt.size(dt)
        n = 1
        for d_ in shape:
            n *= d_
        assert off % 4 == 0 and off + n * esz <= ARENA, (off, shape)
        a = arena[:, off // 4:(off + n * esz + 3) // 4]
        if dt != F32:
            a = a.bitcast(dt)
        a = a[0:parts, 0:n]
        if len(shape) == 2:
            a = a.rearrange("p (a b) -> p a b", b=shape[1])
        elif len(shape) == 3:
            a = a.rearrange("p (a b c) -> p a b c", b=shape[1], c=shape[2])
        elif len(shape) == 4:
            a = a.rearrange("p (a b c d) -> p a b c d", b=shape[1], c=shape[2], d=shape[3])
        return a

    def PS(bank, dt=F32):
        a = psum[:, bank, :]
        return a if dt == F32 else a.bitcast(dt)

    def ST(name, shape, dt):
        return es.enter_context(nc.sbuf_tensor("s_" + name, list(shape), dt))

    cmat = ST("cmat", [128, 5, 128], BF16); cmatf = ST("cmatf", [128, 2, 128], F32)
    ident_bf, ones_bf, perm_bf = cmat[:, 0, :], cmat[:, 1, :], cmat[:, 2, :]
    ident_f, gln_f = cmatf[:, 0, :], cmatf[:, 1, :]
    ropec = ST("ropec", [128, N], BF16); ropes = ST("ropes", [128, N], BF16)
    vT = ST("vT", [128, 384], F32)
    modT = ST("modT", [128, DEPTH, 48, 2], F32)
    gm = ST("gm", [128, DEPTH, 2, KD, 2], F32)
    dcc = ST("dcc", [128, NCB, NC], BF16); dcs = ST("dcs", [128, NCB, NC], BF16)
    gcol = ST("gcol", [128, DEPTH], F32)
    lamb = ST("lamb", [128, DEPTH, 4, 64], F32); lamt = ST("lamt", [128, 2, 64], F32)
    lams = ST("lams", [128, 8], F32); neglam = ST("neglam", [128, DEPTH], F32)
    sc_f = ST("sc_f", [128, KD, 2], F32); sc_bf = ST("sc_bf", [128, KD, 2], BF16)
    small = ST("small", [128, 64], F32)
    wfbd = ST("wfbd", [128, 2, 128], BF16); wfst = ST("wfst", [128, 2, 128], F32)
    wpw2 = ST("wpw2", [128, 2, 256], BF16)
    epsT = ST("epsT", [128, 1], F32)

    def mm(out, lhsT, rhs, start=True, stop=True):
        P.add("pe", lambda e: e.matmul(out, lhsT=lhsT, rhs=rhs, start=start, stop=stop), [lhsT, rhs], [out])

    def tr(out, in_, ident):
        P.add("pe", lambda e: e.transpose(out, in_, ident), [in_, ident], [out])

    def act(out, in_, func, scale=1.0, bias=None, accum=None, eng="act"):
        rd = [in_] + [a for a in (scale, bias) if not isinstance(a, (int, float, type(None)))]
        wr = [out] + ([accum] if accum is not None else [])
        kw = {}
        if bias is not None:
            kw["bias"] = bias
        if accum is not None:
            kw["accum_out"] = accum
        P.add("act", lambda e: e.activation(out=out, in_=in_, func=func, scale=scale, **kw), rd, wr)

    def tt(eng, out, a, b, op):
        P.add(eng, lambda e: e.tensor_tensor(out=out, in0=a, in1=b, op=op), [a, b], [out])

    def ts(eng, out, a, s1, op0, s2=None, op1=None):
        rd = [a] + [s for s in (s1, s2) if not isinstance(s, (int, float, type(None)))]
        if op1 is None:
            P.add(eng, lambda e: e.tensor_scalar(out=out, in0=a, scalar1=s1, scalar2=None, op0=op0), rd, [out])
        else:
            P.add(eng, lambda e: e.tensor_scalar(out=out, in0=a, scalar1=s1, scalar2=s2, op0=op0, op1=op1), rd, [out])

    def stt(out, a, s, b, op0, op1):
        rd = [a, b] + ([s] if not isinstance(s, (int, float)) else [])
        P.add("dve", lambda e: e.scalar_tensor_tensor(out=out, in0=a, scalar=s, in1=b, op0=op0, op1=op1), rd, [out])

    def cp(eng, out, in_):
        if eng == "act":
            P.add("act", lambda e: e.activation(out=out, in_=in_, func=AF.Copy), [in_], [out])
        else:
            P.add(eng, lambda e: e.tensor_copy(out=out, in_=in_), [in_], [out])

    def recip(out, in_):
        P.add("dve", lambda e: e.reciprocal(out=out, in_=in_), [in_], [out])

    def mset(eng, ap, val):
        P.add(eng, lambda e: e.memset(ap, val), [], [ap])

    def dma(eng, out, in_, slot, group=None):
        P.add(eng, lambda e: e.dma_start(out=out, in_=in_), [in_], [out], slot=slot, group=group)

    cpr = _Rot(["act", "dve"])

    dma("sp", cmat[:], cmat_d, "c0"); dma("sp", cmatf[:], cmatf_d, "c1")
    dma("sp", ropec[:], ropec_d, "c2"); dma("sp", ropes[:], ropes_d, "c3")
    dma("sp", dcc[:], dcc_d, "c0"); dma("sp", dcs[:], dcs_d, "c1")
    mset("dve", epsT[:], EPS)
    mset("dve", small[:, 0:8], -0.5)
    vst = AV(0, (3, 128), F32)
    dma("sp", vst, vecs_d.rearrange("(j p) f -> p j f", p=128), "c2")
    for j in range(3):
        tr(PS(0)[:, j * 128:(j + 1) * 128], vst[:, j, :], ident_f)
    cp("dve", vT[:], PS(0)[:, 0:384])
    for l in range(DEPTH):
        dma("sp", lamb[:, l], lamv_d[l].partition_broadcast(128), "c3")
        dma("sp", gcol[:, l:l + 1], subln_d[l].rearrange("(p o) -> p o", o=1), "c0")
        lam_init = 0.8 - 0.6 * float(np.exp(-0.3 * l))
        tt("dve", lamt[:, 0, :], lamb[:, l, 0, :], lamb[:, l, 1, :], ALU.mult)
        tt("dve", lamt[:, 1, :], lamb[:, l, 2, :], lamb[:, l, 3, :], ALU.mult)
        P.add("dve", lambda e: e.tensor_reduce(out=lams[:, 0:2], in_=lamt[:], axis=mybir.AxisListType.X, op=ALU.add), [lamt[:]], [lams[:, 0:2]])
        act(lams[:, 2:4], lams[:, 0:2], AF.Exp)
        tt("dve", lams[:, 4:5], lams[:, 3:4], lams[:, 2:3], ALU.subtract)
        ts("dve", neglam[:, l:l + 1], lams[:, 4:5], -lam_init, ALU.add)
        ts("dve", gcol[:, l:l + 1], gcol[:, l:l + 1], 1.0 - lam_init, ALU.mult)
    dma("sp", sc_f[:], cc_d.rearrange("(k p) t -> p k t", p=128), "c1")
    act(sc_bf[:], sc_f[:], AF.Silu)
    WA_O = 64800
    wa_bufs = [AV(WA_O + i * 4096, (KD, 256), BF16) for i in range(2)]
    mstage_t = ST("mstage", [2, 256], F32)
    mstage = mstage_t[:, :]
    mrow_t = ST("mrow", [96, 128], F32)
    mrow = mrow_t[:, :]
    pbr = _Rot(range(1, 8))
    all_pieces = [(0, pc) for pc in range(24)]
    for l_ in range(1, DEPTH):
        all_pieces += [(l_, pc) for pc in range(24)]
    mstate = {"dma": 0, "mm": 0}

    def mods_dma():
        i = mstate["dma"]
        if i >= len(all_pieces):
            return
        l_, pc = all_pieces[i]
        dma("pool", wa_bufs[i % 2], w_ada_d[l_, :, pc * 256:(pc + 1) * 256].rearrange("(k p) n -> p k n", p=128), "wa%d" % (i % 2))
        mstate["dma"] = i + 1

    def mods_piece(bank):
        i = mstate["mm"]
        while mstate["dma"] <= min(i + 1, len(all_pieces) - 1):
            mods_dma()
        l_, pc = all_pieces[i]
        wa = wa_bufs[i % 2]
        pb = PS(bank)
        for k in range(KD):
            mm(pb[0:2, 0:256], sc_bf[:, k, :], wa[:, k, :], start=(k == 0), stop=(k == KD - 1))
        cp("dve", mstage, pb[0:2, 0:256])
        dma("sp", modrow_d[l_, :, pc * 256:(pc + 1) * 256], mstage, "mst")
        mstate["mm"] = i + 1

    def mods_finish(l, s0, s1, bank):
        r0, r1 = s0 * 8, s1 * 8
        nr = r1 - r0
        for v in range(2):
            dma("sp", mrow[v * nr:(v + 1) * nr, :], modrow_d[l, v, r0 * 128:r1 * 128].rearrange("(r p) -> r p", p=128), "mld", group="mf%d_%d" % (l, s0))
        pb = PS(bank)
        tr(pb[:, 0:2 * nr], mrow[0:2 * nr, :], ident_f[0:2 * nr, 0:2 * nr])
        for v in range(2):
            tt("dve", modT[:, l, r0:r1, v], pb[:, v * nr:(v + 1) * nr], vT[:, l * VROWS + R_BADA + r0: l * VROWS + R_BADA + r1], ALU.add)
        for n_i, (sec_sc, rg) in enumerate(((1, R_N1), (4, R_N2))):
            if not (s0 <= sec_sc < s1):
                continue
            for k in range(KD):
                ts("dve", gm[:, l, n_i, k, :], modT[:, l, sec_sc * 8 + k, :], 1.0, ALU.add)
                ts("dve", gm[:, l, n_i, k, :], gm[:, l, n_i, k, :], vT[:, l * VROWS + rg + k: l * VROWS + rg + k + 1], ALU.mult)

    def load_w_in(l_):
        w_in_ = AV(110592, (KD, INW), BF16)
        if l_ == 0:
            order = [(hf, k) for hf in range(2) for k in range(KD)]
        else:
            order = [(hf, k) for hf in range(2) for k in range(5)] + [(hf, k) for hf in range(2) for k in range(5, KD)]
        for (hf, k) in order:
            dma("pool", w_in_[:, k, hf * 1152:(hf + 1) * 1152], w_in_d[l_, k * 128:(k + 1) * 128, hf * 1152:(hf + 1) * 1152],
                "win%d" % ((hf if l_ == 0 else (0 if k < 5 else 1))), group="l%d" % l_)

    load_w_in(0)

    def bg_mods(max_layer, bank):
        i = mstate["mm"]
        if i < len(all_pieces) and all_pieces[i][0] <= max_layer:
            mods_piece(bank)
            if i == 23:
                mods_finish(0, 2, 6, bank)
            elif i > 23 and (i + 1) % 24 == 0:
                mods_finish(all_pieces[i][0], 0, 6, bank)
            return True
        return False

    XIN = 90112
    xtoks = [AV(16384, (4, D), F32), AV(0, (4, D), F32)]
    xins = [AV(XIN, (KD, 512), F32), AV(32768, (KD, 512), F32)]
    xin_t0 = AV(73728, (KD, 512), F32)

    def x_load(ti):
        t0, W = all_tiles[ti]
        src = x_d[t0:t0 + W, :] if t0 < N else ctx_d[:, :]
        dma("sp", xtoks[ti % 2][:, 0:W // 128, :], src.rearrange("(s p) d -> p s d", p=128), "xl%d" % (ti % 2))

    x_load(0)
    mods_done = 0
    for ti, (t0, W) in enumerate(all_tiles):
        nsub = W // 128
        if ti + 1 < len(all_tiles):
            x_load(ti + 1)
        xtok = xtoks[ti % 2]
        xin = (xin_t0 if ti == 0 else xins[ti % 2])[:, :, 0:W]
        for k in range(KD):
            pb = PS(pbr.next())
            for s_ in range(nsub):
                tr(pb[:, s_ * 128:(s_ + 1) * 128], xtok[:, s_, k * 128:(k + 1) * 128], ident_f)
            cp(cpr.next(), xin[:, k, :], pb[:, 0:W])
        dma("sp", xs_tile(t0, W), xin, "xst%d" % ((t0 // 512) % 2))
        for _ in range(2):
            if mods_done < 8:
                mods_piece(pbr.next()); mods_done += 1
    while mods_done < 8:
        mods_piece(pbr.next()); mods_done += 1
    mods_finish(0, 0, 2, pbr.next())

    QT_O, KT_O, V_O, UF_O = 0, 18432, 36864, 55584
    MIX_O = 73728
    WIN_O = 110592
    Z_O = 147456
    QT = AV(QT_O, (4, T), BF16); KT = AV(KT_O, (4, T), BF16)
    V = AV(V_O, (TB, 4, 130), BF16)
    ufT = AV(UF_O, (2, T), BF16)
    mixT = AV(MIX_O, (KD, T), BF16)
    zl = AV(Z_O, (2, N + 30), BF16)
    zc_ = AV(Z_O + 2 * (N + 30) * 2, (2, NC + 30), BF16)
    XRES = AV(0, (KD, T), F32)

    def norm_sq(xin, W, sq):
        act(sq[:, :, 0:W], xin, AF.Square)

    def norm_rest(l, n_i, sec_sh, xin, W, col, hT, sq, rstd, sqrtt, tmps, bank=None):
        pb = PS(pbr.next() if bank is None else bank)
        for k in range(KD):
            mm(pb[:, 0:W], ones_bf, sq[:, k, 0:W], start=(k == 0), stop=(k == KD - 1))
        act(sqrtt[:, 0:W], pb[:, 0:W], AF.Ln, scale=1.0 / D, bias=epsT[:, 0:1])
        act(rstd[:, 0:W], sqrtt[:, 0:W], AF.Exp, scale=-0.5)
        for k in range(KD):
            tmp = tmps.next()
            stt(tmp[:, 0:W], xin[:, k, :], gm[:, l, n_i, k, col:col + 1], rstd[:, 0:W], ALU.mult, ALU.mult)
            act(hT[:, k, 0:W], tmp[:, 0:W], AF.Identity, bias=modT[:, l, sec_sh * 8 + k, col:col + 1])

    def norm_tile(l, n_i, sec_sh, xin, W, col, hT, sq, rstd, sqrtt, tmps):
        norm_sq(xin, W, sq)
        norm_rest(l, n_i, sec_sh, xin, W, col, hT, sq, rstd, sqrtt, tmps)

    for l in range(DEPTH):
        last = (l == DEPTH - 1)
        vb = l * VROWS
        w_in = AV(WIN_O, (KD, INW), BF16)
        if l > 0:
            load_w_in(l)
        mset("pool", wfst[:], 0.0)
        for g in range(4):
            c_, o_ = g // 2, (g % 2) * 64
            dma("sp", wfst[o_:o_ + 64, c_, o_:o_ + 64], wf_d[l, g], "c2", group="wf%d" % l)
        cp("dve", wfbd[:], wfst[:])
        dma("pool", wpw2[:], wco_d[l].rearrange("(c p) d -> p c d", p=128), "wpw")
        for zz, W_ in ((zl, N), (zc_, NC)):
            mset("pool", zz[:, :, 0:15], 0.0)
            mset("pool", zz[:, :, W_ + 15:W_ + 30], 0.0)
        sq = AV(MIX_O + 16384, (KD, 512), BF16)
        hTs = [AV(MIX_O + 24576, (KD, 512), BF16), AV(Z_O + 9472, (KD, 512), BF16)]
        rstd = AV(MIX_O + 32768, (512,), F32); sqrtt = AV(MIX_O + 34816, (512,), F32)
        TO = Z_O + 9472 + 8192
        tmpsA = _Rot([AV(TO, (512,), F32), AV(TO + 2048, (512,), F32)])
        qraws = _Rot([AV(TO + 4096, (512,), BF16), AV(TO + 5120, (512,), BF16)])
        t1s = _Rot([AV(TO + 6144, (512,), F32), AV(TO + 8192, (512,), F32)])
        t2s = _Rot([AV(TO + 10240, (512,), F32), AV(TO + 12288, (512,), F32)])
        sigs = _Rot([AV(TO + 14336, (512,), F32), AV(TO + 16384, (512,), F32)])
        def p1_load(ti):
            t0, W = all_tiles[ti]
            xin = AV(MIX_O, (KD, W), F32)
            dma("sp", xin, xs_tile(t0, W), "xin")

        def p1_norm(ti):
            t0, W = all_tiles[ti]
            col = 1 if t0 >= N else 0
            xin = AV(MIX_O, (KD, W), F32)
            norm_tile(l, 0, 0, xin, W, col, hTs[ti % 2], sq, rstd, sqrtt, tmpsA)

        def p1_norm_sq(ti):
            t0, W = all_tiles[ti]
            norm_sq(AV(MIX_O, (KD, W), F32), W, sq)

        def p1_norm_rest(ti):
            t0, W = all_tiles[ti]
            col = 1 if t0 >= N else 0
            norm_rest(l, 0, 0, AV(MIX_O, (KD, W), F32), W, col, hTs[ti % 2], sq, rstd, sqrtt, tmpsA)

        def rope_tail(qraw, t1, dst, t0, W):
            t2 = t2s.next()
            pb2 = PS(pbr.next())
            mm(pb2[:, 0:W], perm_bf, qraw[:, 0:W])
            tt("dve", t2[:, 0:W], pb2[:, 0:W], ropes[:, t0:t0 + W], ALU.mult)
            tt("pool", dst, t1[:, 0:W], t2[:, 0:W], ALU.add)

        nt_ = len(all_tiles)
        if l == 0:
            p1_norm(0)
        else:
            t0_, W_ = all_tiles[0]
            norm_tile(l, 0, 0, XRES[:, :, t0_:t0_ + W_], W_, 0, hTs[0], sq, rstd, sqrtt, tmpsA)
        if nt_ > 1:
            p1_load(1)
        for ti, (t0, W) in enumerate(all_tiles):
            is_ctx = t0 >= N
            col = 1 if is_ctx else 0
            hT = hTs[ti % 2]
            pend_rope = None
            ctx_kv_only = is_ctx and last
            for ch in range(8):
                if ctx_kv_only and ch < 4:
                    continue
                pb = PS(pbr.next())
                for k in range(KD):
                    mm(pb[:, 0:W], w_in[:, k, ch * 128:(ch + 1) * 128], hT[:, k, 0:W], start=(k == 0), stop=(k == KD - 1))
                dst = (QT if ch < 4 else KT)[:, ch % 4, t0:t0 + W]
                if ch == 3 and ti + 1 < nt_:
                    p1_norm_sq(ti + 1)
                if is_ctx:
                    cp(cpr.next(), dst, pb[:, 0:W])
                else:
                    qraw = qraws.next(); t1 = t1s.next()
                    cp("act", qraw[:, 0:W], pb[:, 0:W])
                    tt("dve", t1[:, 0:W], pb[:, 0:W], ropec[:, t0:t0 + W], ALU.mult)
                    if pend_rope is not None:
                        rope_tail(*pend_rope)
                    pend_rope = (qraw, t1, dst, t0, W)
            bg_mods(l, pbr.next())
            if ti + 1 < nt_:
                p1_norm_rest(ti + 1)
                if ti + 2 < nt_:
                    p1_load(ti + 2)
            for s_ in range(W // 128):
                pb = PS(pbr.next())
                for k in range(KD):
                    mm(pb[:, :], hT[:, k, s_ * 128:(s_ + 1) * 128], w_in[:, k, 1024:1536], start=(k == 0), stop=(k == KD - 1))
                tb = (t0 // 128) + s_
                cp(cpr.next(), V[:, tb, :, 0:128], pb[:, :].rearrange("p (h d) -> p h d", d=128))
                if s_ == 0 and pend_rope is not None:
                    rope_tail(*pend_rope)
                    pend_rope = None
            if ctx_kv_only:
                bg_mods(l, pbr.next())
                continue
            for c_ in range(2):
                pb = PS(pbr.next())
                for k in range(KD):
                    mm(pb[:, 0:W], w_in[:, k, 1536 + c_ * 128:1536 + (c_ + 1) * 128], hT[:, k, 0:W], start=(k == 0), stop=(k == KD - 1))
                cp(cpr.next(), ufT[:, c_, t0:t0 + W], pb[:, 0:W])
            for c_ in range(2):
                pa = PS(pbr.next()); pg = PS(pbr.next())
                for k in range(KD):
                    mm(pa[:, 0:W], w_in[:, k, 1792 + c_ * 128:1792 + (c_ + 1) * 128], hT[:, k, 0:W], start=(k == 0), stop=(k == KD - 1))
                for k in range(KD):
                    mm(pg[:, 0:W], w_in[:, k, 2048 + c_ * 128:2048 + (c_ + 1) * 128], hT[:, k, 0:W], start=(k == 0), stop=(k == KD - 1))
                sg = sigs.next()
                act(sg[:, 0:W], pg[:, 0:W], AF.Sigmoid)
                zdst = zc_[:, c_, 15:15 + W] if is_ctx else zl[:, c_, 15 + t0:15 + t0 + W]
                tt("dve", zdst, pa[:, 0:W], sg[:, 0:W], ALU.mult)
            bg_mods(l, pbr.next())

        WOUT_O = 159744
        w_out = AV(WOUT_O, (KD, D), BF16)
        for k in range(KD):
            dma("pool", w_out[:, k, :], w_out_d[l, k * 128:(k + 1) * 128, :], "wout%d" % (k % 2), group="l%d" % l)

        E_O = WIN_O
        QT_W = 256
        Ebufs = _Rot([AV(E_O + 24576 + i * 1024, (512,), BF16) for i in range(4)])
        rrs = _Rot([AV(E_O + 3072 + i * 2048, (512,), F32) for i in range(2)])
        t12s = _Rot([AV(E_O + 7168 + i * 2048, (512,), F32) for i in range(2)])
        ofs = _Rot([AV(E_O + 11264 + i * 1024, (256,), F32) for i in range(3)])
        sqb = _Rot([AV(E_O + 14336 + i * 512, (256,), BF16) for i in range(3)])
        lnb = _Rot([AV(E_O + 15872 + i * 1024, (256,), F32) for i in range(2)])
        rsb = _Rot([AV(E_O + 17920 + i * 1024, (256,), F32) for i in range(2)])
        sbr = _Rot([0, 1, 6])
        ssr = _Rot([7])
        qsets = []
        for q0 in range(0, N, QT_W):
            qsets.append((q0, QT_W, list(range(TB))))
        if not last:
            for q0 in range(0, NC, QT_W):
                qsets.append((N + q0, min(QT_W, NC - q0), list(range(NB, TB))))

        Qz = [[AV(E_O + 20480 + (hpar * 2 + tp_) * 1024, (2, 256), BF16) for tp_ in range(2)] for hpar in range(2)]
        for hpar in range(2):
            for tp_ in range(2):
                mset("pool", Qz[hpar][tp_][:, :, :], 0.0)

        def att_Q(h, q0, QW, tpar):
            hp = slice((h % 2) * 64, (h % 2) * 64 + 64)
            c1, c2 = h // 2, 2 + h // 2
            qz = Qz[h % 2][tpar]
            cp("pool", qz[hp, 0, 0:QW], QT[hp, c1, q0:q0 + QW])
            cp("pool", qz[hp, 1, 0:QW], QT[hp, c2, q0:q0 + QW])

        def att_S(h, q0, QW, kb, tpar):
            c1, c2 = h // 2, 2 + h // 2
            qz = Qz[h % 2][tpar]
            sb = PS(sbr.next())
            mm(sb[:, 0:QW], KT[:, c1, kb * 128:(kb + 1) * 128], qz[:, 0, 0:QW])
            mm(sb[:, 256:256 + QW], KT[:, c2, kb * 128:(kb + 1) * 128], qz[:, 1, 0:QW])
            E = Ebufs.next()
            if QW == 256:
                act(E[:, 0:512], sb[:, 0:512], AF.Exp, scale=0.125)
            else:
                act(E[:, :].rearrange("p (m q) -> p m q", m=2)[:, :, 0:QW], sb[:, :].rearrange("p (m q) -> p m q", m=2)[:, :, 0:QW], AF.Exp, scale=0.125)
            return E

        def att_PV(h, QW, ki, nk, kb, E, par):
            ob = PS(2 + par); sk = PS(4 + par)
            f, l_ = (ki == 0), (ki == nk - 1)
            mm(ob[:, 0:QW], V[:, kb, h, 0:128], E[:, 0:QW], start=f, stop=False)
            mm(ob[:, 256:256 + QW], V[:, kb, h, 0:128], E[:, 256:256 + QW], start=False, stop=l_)
            mm(sk[:, 0:QW], ones_bf, E[:, 0:QW], start=f, stop=False)
            mm(sk[:, 256:256 + QW], ones_bf, E[:, 256:256 + QW], start=False, stop=l_)

        def att_fin_a(h, q0, QW, par):
            ob = PS(2 + par); sk = PS(4 + par)
            rr = rrs.next(); t12 = t12s.next(); of = ofs.next(); sq_ = sqb.next()
            if QW == 256:
                recip(rr[:, 0:512], sk[:, 0:512])
                tt("dve", t12[:, 0:512], ob[:, 0:512], rr[:, 0:512], ALU.mult)
            else:
                v3 = lambda a_: a_[:, :].rearrange("p (m q) -> p m q", m=2)[:, :, 0:QW]
                recip(v3(rr), v3(sk))
                tt("dve", v3(t12), v3(ob), v3(rr), ALU.mult)
            stt(of[:, 0:QW], t12[:, 256:256 + QW], neglam[:, l:l + 1], t12[:, 0:QW], ALU.mult, ALU.add)
            tt("dve", sq_[:, 0:QW], of[:, 0:QW], of[:, 0:QW], ALU.mult)
            return (h, q0, QW, of, sq_)

        def att_fin_b(h, q0, QW, of, sq_):
            ssb = PS(ssr.next())
            mm(ssb[:, 0:QW], ones_bf, sq_[:, 0:QW])
            ln_ = lnb.next(); rs_ = rsb.next()
            act(ln_[:, 0:QW], ssb[:, 0:QW], AF.Ln, scale=1.0 / 128, bias=epsT[:, 0:1])
            act(rs_[:, 0:QW], ln_[:, 0:QW], AF.Exp, scale=-0.5)
            stt(mixT[:, h, q0:q0 + QW], of[:, 0:QW], gcol[:, l:l + 1], rs_[:, 0:QW], ALU.mult, ALU.mult)

        stages = []
        tiles_ = []
        tno = 0
        for h in range(4):
            for (q0, QW, kbs) in qsets:
                tiles_.append((h, q0, QW, tno))
                for ki, kb in enumerate(kbs):
                    stages.append((h, q0, QW, ki, len(kbs), kb, tno))
                tno += 1
        hcnt = [0, 0]
        tpar_of = {}
        for (h, q0, QW, tn) in tiles_:
            tpar_of[tn] = hcnt[h % 2] % 2
            hcnt[h % 2] += 1
        att_Q(tiles_[0][0], tiles_[0][1], tiles_[0][2], tpar_of[0])
        LA = 2
        fifo = []
        pend = []

        def do_pv(it):
            ph, pq0, pQW, pki, pnk, pkb, ppar, pE = it
            att_PV(ph, pQW, pki, pnk, pkb, pE, ppar)
            if pki == pnk - 1:
                pend.append([10, att_fin_a(ph, pq0, pQW, ppar)])

        for st in stages:
            h, q0, QW, ki, nk, kb, tn = st
            par = tn % 2
            if ki == 0 and tn + 1 < len(tiles_):
                nh, nq0, nQW, ntn = tiles_[tn + 1]
                att_Q(nh, nq0, nQW, tpar_of[ntn])
            if ki == 4:
                bg_mods(l + 1, 7)
            E = att_S(h, q0, QW, kb, tpar_of[tn])
            fifo.append((h, q0, QW, ki, nk, kb, par, E))
            if len(fifo) > LA:
                do_pv(fifo.pop(0))
            for it in pend:
                it[0] -= 1
            while pend and pend[0][0] <= 0:
                att_fin_b(*pend.pop(0)[1])
        while fifo:
            do_pv(fifo.pop(0))

        while bg_mods(l + 1, 7):
            pass

        AB = AV(0, (TB, 512), BF16)
        DFB = [(AV(18432 + i * 8192, (NB // 2, 256), BF16), AV(18432 + i * 8192 + 4096, (NB // 2, 256), BF16)) for i in range(2)]
        FTs = _Rot([AV(E_O + i * 1024, (2, 256), BF16) for i in range(2)])
        pbr3 = _Rot(range(8))
        NH = NB // 2
        ABp = AV(34816, (NH, 512), BF16); ABm = AV(34816 + NH * 1024, (NH, 512), BF16)

        def stage_a(tb):
            pb = PS(pbr3.next())
            for c_ in range(2):
                mm(pb[:, c_ * 256:(c_ + 1) * 256], ufT[:, c_, tb * 128:(tb + 1) * 128], cmat[:, 3:5, :].rearrange("p a b -> p (a b)"))
            cp(cpr.next(), AB[:, tb, :], pb[:, :])

        for i in range(NH):
            stage_a(i); stage_a(i + NH)
            tt("dve", ABp[:, i, :], AB[:, i, :], AB[:, i + NH, :], ALU.add)
            tt("pool", ABm[:, i, :], AB[:, i, :], AB[:, i + NH, :], ALU.subtract)
        if not last:
            for tb in range(NB, TB):
                stage_a(tb)
        while pend:
            att_fin_b(*pend.pop(0)[1])

        pend_f = None

        def fourier_out1(pb, Wk, tok0):
            FT = FTs.next()
            cp("act", FT[:, :, 0:Wk], pb[:, :].rearrange("p (c k) -> p c k", c=2)[:, :, 0:Wk])
            return (FT, Wk, tok0)

        def fourier_out(pb, Wk, tok0):
            fourier_out2(*fourier_out1(pb, Wk, tok0))

        def fourier_out2(FT, Wk, tok0):
            pb2 = PS(pbr3.next())
            for c_ in range(2):
                mm(pb2[:, c_ * 256:c_ * 256 + Wk], wfbd[:, c_, :], FT[:, c_, 0:Wk])
            dst = tok0 if not isinstance(tok0, int) else mixT[:, 4:6, tok0:tok0 + Wk]
            cp("dve", dst, pb2[:, :].rearrange("p (c k) -> p c k", c=2)[:, :, 0:Wk])

        step = 0
        for par, ABx in ((0, ABp), (1, ABm)):
            for kc in range(NKC // 2):
                Cb, Sb = DFB[step % 2]
                dma("sp", Cb, dftc_d[par, kc], "dfc%d" % (step % 2)); dma("sp", Sb, dfts_d[par, kc], "dfs%d" % (step % 2))
                step += 1
                pb = PS(pbr3.next())
                for c_ in range(2):
                    for nb in range(NH):
                        mm(pb[:, c_ * 256:(c_ + 1) * 256], ABx[:, nb, c_ * 256:c_ * 256 + 128], Cb[:, nb, :], start=(nb == 0), stop=False)
                        mm(pb[:, c_ * 256:(c_ + 1) * 256], ABx[:, nb, c_ * 256 + 128:c_ * 256 + 256], Sb[:, nb, :], start=False, stop=(nb == NH - 1))
                if pend_f is not None:
                    fourier_out2(*pend_f)
                dst = mixT[:, 4:6, kc * 512:(kc + 1) * 512].rearrange("p c (j two) -> p c j two", two=2)[:, :, :, par]
                pend_f = fourier_out1(pb, 256, dst)
        if pend_f is not None:
            fourier_out2(*pend_f)
            pend_f = None
        if not last:
            for k0 in range(0, NC, 256):
                Wk = min(256, NC - k0)
                pb = PS(pbr3.next())
                for c_ in range(2):
                    for nb in range(NCB):
                        mm(pb[:, c_ * 256:c_ * 256 + Wk], AB[:, NB + nb, c_ * 256:c_ * 256 + 128], dcc[:, nb, k0:k0 + Wk], start=(nb == 0), stop=False)
                        mm(pb[:, c_ * 256:c_ * 256 + Wk], AB[:, NB + nb, c_ * 256 + 128:c_ * 256 + 256], dcs[:, nb, k0:k0 + Wk], start=False, stop=(nb == NCB - 1))
                fourier_out(pb, Wk, N + k0)

        DG_O = WIN_O + 16384
        diag = AV(DG_O, (CONV_K, 2, 128), BF16)
        for j in range(CONV_K):
            for c_ in range(2):
                r = vb + R_CW + j * 2 + c_
                ts("dve", diag[:, j, c_, :], ident_bf, vT[:, r:r + 1], ALU.mult)
        C_O = 51200
        zcs = _Rot([AV(C_O + i * 2048, (512,), F32) for i in range(3)])
        cens = _Rot([AV(C_O + 6144 + i * 2048, (512,), F32) for i in range(3)])
        sqs = _Rot([AV(C_O + 12288 + i * 2048, (512,), F32) for i in range(3)])
        rsd = _Rot([AV(C_O + 18432 + i * 2048, (512,), F32) for i in range(2)])
        sils = _Rot([AV(E_O + 4096 + i * 2048, (2, 512), BF16) for i in range(2)])
        segs = [(zl, t0, W, t0) for (t0, W) in lat_tiles]
        if not last:
            segs.append((zc_, 0, NC, N))
        items = []
        for si, (zz, s0, W, tok0) in enumerate(segs):
            sil = sils.next()
            for c_ in range(2):
                items.append(dict(zz=zz, s0=s0, W=W, tok0=tok0, c=c_, sil=sil))

        def cv_A(it):
            W = it["W"]; c_ = it["c"]
            pb = PS(pbr3.next())
            for j in range(CONV_K):
                mm(pb[:, 0:W], diag[:, j, c_, :], it["zz"][:, c_, it["s0"] + j:it["s0"] + j + W], start=(j == 0), stop=(j == CONV_K - 1))
            it["zcv"] = zcs.next()
            act(it["zcv"][:, 0:W], pb[:, 0:W], AF.Identity, bias=vT[:, vb + R_CB + c_: vb + R_CB + c_ + 1])

        def cv_B(it):
            W = it["W"]
            pm = PS(pbr3.next())
            mm(pm[:, 0:W], gln_f, it["zcv"][:, 0:W])
            it["cen"] = cens.next(); it["sq"] = sqs.next()
            tt("dve", it["cen"][:, 0:W], it["zcv"][:, 0:W], pm[:, 0:W], ALU.subtract)
            act(it["sq"][:, 0:W], it["cen"][:, 0:W], AF.Square)

        def cv_C(it):
            W = it["W"]; c_ = it["c"]
            pv_ = PS(pbr3.next())
            mm(pv_[:, 0:W], gln_f, it["sq"][:, 0:W])
            rs_ = rsd.next()
            act(rs_[:, 0:W], pv_[:, 0:W], AF.Ln, bias=epsT[:, 0:1])
            act(rs_[:, 0:W], rs_[:, 0:W], AF.Exp, scale=-0.5)
            zn = it["sq"]
            tt("dve", zn[:, 0:W], it["cen"][:, 0:W], rs_[:, 0:W], ALU.mult)
            act(it["sil"][:, c_, 0:W], zn[:, 0:W], AF.Silu, scale=vT[:, vb + R_LG + c_: vb + R_LG + c_ + 1], bias=vT[:, vb + R_LB + c_: vb + R_LB + c_ + 1])

        def cv_D(it):
            W = it["W"]
            for dc in range(2):
                pb = PS(pbr3.next())
                for c_ in range(2):
                    mm(pb[:, 0:W], wpw2[:, c_, dc * 128:(dc + 1) * 128], it["sil"][:, c_, 0:W], start=(c_ == 0), stop=(c_ == 1))
                cp(cpr.next(), mixT[:, 6 + dc, it["tok0"]:it["tok0"] + W], pb[:, 0:W])

        ni = len(items)
        for i in range(ni + 4):
            if i < ni:
                cv_A(items[i])
            if 0 <= i - 1 < ni:
                cv_B(items[i - 1])
            if 0 <= i - 2 < ni:
                cv_C(items[i - 2])
            if 0 <= i - 3 < ni and items[i - 3]["c"] == 1:
                cv_D(items[i - 3])

        FW_O = WIN_O
        XIN5 = FW_O + 24576
        SQ5 = XIN5 + 16384
        T5 = 176128
        rstd5 = AV(T5, (512,), F32); sqrtt5 = AV(T5 + 2048, (512,), F32)
        tmps5 = _Rot([AV(T5 + 4096, (512,), F32), AV(T5 + 6144, (512,), F32)])
        sq5 = AV(SQ5, (KD, 512), BF16)
        tiles5 = all_tiles if not last else lat_tiles
        pend5 = None
        for (t0, W) in tiles5:
            col = 1 if t0 >= N else 0
            xin = AV(XIN5, (KD, W), F32)
            for m_ in range(KD):
                dma("sp", xin[:, m_, :], xs_tile(t0, W)[:, m_, :], "xin%d" % (m_ % 2))
            if pend5 is not None:
                norm_sq(XRES[:, :, pend5[0]:pend5[0] + pend5[1]], pend5[1], sq5)
            for m_ in range(KD):
                pb = PS(pbr3.next())
                for k in range(KD):
                    mm(pb[:, 0:W], w_out[:, k, m_ * 128:(m_ + 1) * 128], mixT[:, k, t0:t0 + W], start=(k == 0), stop=(k == KD - 1))
                stt(XRES[:, m_, t0:t0 + W], pb[:, 0:W], modT[:, l, 2 * 8 + m_, col:col + 1], xin[:, m_, :], ALU.mult, ALU.add)
                if m_ == 3 and pend5 is not None:
                    pt0, pW, pcol = pend5
                    norm_rest(l, 1, 3, XRES[:, :, pt0:pt0 + pW], pW, pcol, mixT[:, :, pt0:pt0 + pW], sq5, rstd5, sqrtt5, tmps5, bank=pbr3.next())
                    pend5 = None
            pend5 = (t0, W, col)
        pt0, pW, pcol = pend5

        def late_norm5(pt0=pt0, pW=pW, pcol=pcol):
            norm_sq(XRES[:, :, pt0:pt0 + pW], pW, sq5)
            norm_rest(l, 1, 3, XRES[:, :, pt0:pt0 + pW], pW, pcol, mixT[:, :, pt0:pt0 + pW], sq5, rstd5, sqrtt5, tmps5, bank=pbr3.next())

        late5 = [late_norm5]
        if len(tiles5) == 1:
            late5.pop()()
        h2T = mixT

        G = 4
        groups = [(g0, min(G, NFF - g0)) for g0 in range(0, NFF, G)]
        if groups[-1][1] < G:
            groups = [groups[-1]] + groups[:-1]
        fwb = [(AV(FW_O + i * 24576, (KD, 512), BF16), AV(FW_O + i * 24576 + 8192, (KD, 512), BF16), AV(FW_O + i * 24576 + 16384, (G, D), BF16)) for i in range(2)]
        ACT_O = 159744
        actb = _Rot([AV(ACT_O + i * 4096, (G, 512), BF16) for i in range(2)])
        sgs = _Rot([AV(ACT_O + 8192 + i * 2048, (512,), F32) for i in range(2)])
        upb = _Rot([0, 1, 2, 3]); dnb = _Rot([4, 5, 6, 7])
        pending = None

        def down(g0, gn, t0, W, ab, w2b, col, is_last_group=False):
            for m_ in range(KD):
                pb = PS(dnb.next())
                for jj in range(gn):
                    mm(pb[:, 0:W], w2b[:, jj, m_ * 128:(m_ + 1) * 128], ab[:, jj, 0:W], start=(jj == 0), stop=(jj == gn - 1))
                stt(XRES[:, m_, t0:t0 + W], pb[:, 0:W], modT[:, l, 5 * 8 + m_, col:col + 1], XRES[:, m_, t0:t0 + W], ALU.mult, ALU.add)

        for gi, (g0, gn) in enumerate(groups):
            w1b, w3b, w2b = fwb[gi % 2]
            pass
            cw = gn * 128
            dma("pool", w1b[:, :, 0:cw], w1_d[l, :, g0 * 128:g0 * 128 + cw].rearrange("(k p) n -> p k n", p=128), "fw1_%d" % (gi % 2))
            dma("pool", w3b[:, :, 0:cw], w3_d[l, :, g0 * 128:g0 * 128 + cw].rearrange("(k p) n -> p k n", p=128), "fw3_%d" % (gi % 2))
            dma("pool", w2b[:, 0:gn, :], w2_d[l, g0 * 128:g0 * 128 + cw, :].rearrange("(j p) n -> p j n", p=128), "fw2_%d" % (gi % 2))
            for (t0, W) in tiles5:
                col = 1 if t0 >= N else 0
                ab = actb.next()
                for jj in range(gn):
                    pa = PS(upb.next()); pg = PS(upb.next())
                    for k in range(KD):
                        mm(pa[:, 0:W], w1b[:, k, jj * 128:(jj + 1) * 128], h2T[:, k, t0:t0 + W], start=(k == 0), stop=(k == KD - 1))
                    for k in range(KD):
                        mm(pg[:, 0:W], w3b[:, k, jj * 128:(jj + 1) * 128], h2T[:, k, t0:t0 + W], start=(k == 0), stop=(k == KD - 1))
                    sg = sgs.next()
                    act(sg[:, 0:W], pa[:, 0:W], AF.Silu)
                    tt("dve", ab[:, jj, 0:W], pg[:, 0:W], sg[:, 0:W], ALU.mult)
                if late5:
                    late5.pop()()
                if pending is not None:
                    down(*pending)
                    if pending[-1] and not last:
                        pt0, pW = pending[2], pending[3]
                        dma("sp", xs_tile(pt0, pW), XRES[:, :, pt0:pt0 + pW], "xst%d" % ((pt0 // 512) % 2))
                pending = (g0, gn, t0, W, ab, w2b, col, gi == len(groups) - 1)
        down(*pending)
        if not last:
            pt0, pW = pending[2], pending[3]
            dma("sp", xs_tile(pt0, pW), XRES[:, :, pt0:pt0 + pW], "xst%d" % ((pt0 // 512) % 2))
        pending = None

    FB = 73728
    fgr = DEPTH * VROWS
    gbc = AV(FB, (D,), F32)
    junkF = AV(FB + 4096, (512,), F32)
    otl = [AV(FB + 6144 + i * 4096, (D,), F32) for i in range(3)]
    ssqF = ST("ssqF", [128, NB, 4], F32)
    dma("sp", gbc, vecs_d[fgr:fgr + 8, :].rearrange("a b -> (a b)").partition_broadcast(128), "c0")
    pbrF = _Rot(range(8))

    def fin_T(blk):
        banks = []
        for kq in range(2):
            pb = PS(pbrF.next())
            for kk in range(4):
                tr(pb[:, kk * 128:(kk + 1) * 128], XRES[:, kq * 4 + kk, blk * 128:(blk + 1) * 128], ident_f)
            act(junkF, pb[:, :], AF.Square, accum=ssqF[:, blk, kq:kq + 1])
            banks.append(pb)
        return banks

    def fin_E(blk, banks):
        tt("dve", ssqF[:, blk, 2:3], ssqF[:, blk, 0:1], ssqF[:, blk, 1:2], ALU.add)
        ts("dve", ssqF[:, blk, 2:3], ssqF[:, blk, 2:3], 1.0 / D, ALU.mult, EPS, ALU.add)
        tt("pool", ssqF[:, blk, 3:4], ssqF[:, blk, 2:3], small[:, 0:1], ALU.pow)
        ot = otl[blk % 3]
        for kq in range(2):
            stt(ot[:, kq * 512:(kq + 1) * 512], banks[kq][:, :], ssqF[:, blk, 3:4], gbc[:, kq * 512:(kq + 1) * 512], ALU.mult, ALU.mult)
        dma("sp", out_d[blk * 128:(blk + 1) * 128, :], ot, "ost%d" % (blk % 3))

    prevF = None
    for blk in range(NB):
        banks = fin_T(blk)
        if prevF is not None:
            fin_E(*prevF)
        prevF = (blk, banks)
    fin_E(*prevF)

    P.finalize(None)
    streams = []
    for o in P.ops:
        s = P._stream(o)
        if s not in streams:
            streams.append(s)
    sems = {s: es.enter_context(nc.semaphore("sem_%s_%s" % s)) for s in streams}
    out_slots = [s for s in streams if s[0] == "slot" and s[1].startswith("ost")]
    block = es.enter_context(nc.Block())

    @block.tensor
    def _(e):
        P.emit_engine("pe", e, sems)

    @block.scalar
    def _(e):
        P.emit_engine("act", e, sems)

    @block.vector
    def _(e):
        P.emit_engine("dve", e, sems)

    @block.gpsimd
    def _(e):
        P.emit_engine("pool", e, sems)

    @block.sync
    def _(e):
        P.emit_engine("sp", e, sems)
        for s in out_slots:
            e.wait_ge(sems[s], P.max_vals[s])

    es.close()
    return nc, P


def _const_tables(N, NC):
    bf = ml_dtypes.bfloat16
    rows = N // GRID_W
    t = np.arange(N)
    row = (t // GRID_W).astype(np.float64)
    colp = (t % GRID_W).astype(np.float64)
    inv_freq = ROPE_BASE ** (-np.arange(16, dtype=np.float64) / 16)
    ropec = np.zeros((128, N)); ropes = np.zeros((128, N))
    perm = np.zeros((128, 128))
    for p in range(128):
        d = p % 64
        axis, half, f = d // 32, (d % 32) // 16, d % 16
        ang = (row if axis == 0 else colp) * inv_freq[f]
        ropec[p] = np.cos(ang)
        ropes[p] = -np.sin(ang) if half == 0 else np.sin(ang)
        partner = p + 16 if half == 0 else p - 16
        perm[partner, p] = 1.0
    ident = np.eye(128)
    ones = np.ones((128, 128))
    cc = np.arange(64)
    a64 = 2 * np.pi * np.outer(cc, cc) / 64
    c64 = np.cos(a64) / 8.0; s64 = np.sin(a64) / 8.0
    c64bd = np.zeros((128, 128)); s64bd = np.zeros((128, 128))
    gln = np.zeros((128, 128))
    for g in range(2):
        c64bd[g * 64:(g + 1) * 64, g * 64:(g + 1) * 64] = c64
        s64bd[g * 64:(g + 1) * 64, g * 64:(g + 1) * 64] = s64
        gln[g * 64:(g + 1) * 64, g * 64:(g + 1) * 64] = 1.0 / 64
    cmat = np.stack([ident, ones, perm, c64bd, s64bd], axis=1).astype(bf)
    cmatf = np.stack([ident, gln], axis=1).astype(np.float32)

    NB = N // 128
    NHt = N // 2
    n_ = np.arange(NHt)
    kp = np.arange(NHt)
    ang_e = 2 * np.pi * (np.outer(n_, kp) % NHt) / NHt
    ang_o = 2 * np.pi * (np.outer(n_, 2 * kp + 1) % N) / N
    sc_ = 1.0 / np.sqrt(N)

    def lay(Mx):
        return np.ascontiguousarray(Mx.reshape(NB // 2, 128, NHt // 256, 256).transpose(2, 1, 0, 3))

    dftc_h = np.stack([lay(np.cos(ang_e) * sc_), lay(np.cos(ang_o) * sc_)], axis=0).astype(bf)
    dfts_h = np.stack([lay(-np.sin(ang_e) * sc_), lay(-np.sin(ang_o) * sc_)], axis=0).astype(bf)

    def dft(M):
        n = np.arange(M)
        ang = 2 * np.pi * (np.outer(n, n) % M) / M
        return np.cos(ang) / np.sqrt(M), -np.sin(ang) / np.sqrt(M)

    Cc, Sc = dft(NC)
    NCB = NC // 128

    def layc(Mx):
        return np.ascontiguousarray(Mx.reshape(NCB, 128, NC).transpose(1, 0, 2)).astype(bf)

    return dict(ropec=ropec.astype(bf), ropes=ropes.astype(bf), cmat=cmat, cmatf=cmatf,
                dftc=dftc_h, dfts=dfts_h, dcc=layc(Cc), dcs=layc(Sc))


def _pack_vecs(inp, DEPTH):
    vecs = np.zeros((384, 128), np.float32)
    for l in range(DEPTH):
        b = l * VROWS
        vecs[b + R_BADA:b + R_BADA + 48] = np.asarray(inp["b_ada"][l], np.float32).reshape(48, 128)
        vecs[b + R_N1:b + R_N1 + 8] = np.asarray(inp["norm1_g"][l], np.float32).reshape(8, 128)
        vecs[b + R_N2:b + R_N2 + 8] = np.asarray(inp["norm2_g"][l], np.float32).reshape(8, 128)
        vecs[b + R_CB:b + R_CB + 2] = np.asarray(inp["conv_b"][l], np.float32).reshape(2, 128)
        vecs[b + R_LG:b + R_LG + 2] = np.asarray(inp["conv_ln_g"][l], np.float32).reshape(2, 128)
        vecs[b + R_LB:b + R_LB + 2] = np.asarray(inp["conv_ln_b"][l], np.float32).reshape(2, 128)
        vecs[b + R_CW:b + R_CW + 62] = np.asarray(inp["conv_w"][l], np.float32).reshape(62, 128)
    vecs[DEPTH * VROWS:DEPTH * VROWS + 8] = np.asarray(inp["final_g"], np.float32).reshape(8, 128)
    return vecs


def make_in_maps(inp, N, NC, DEPTH, B):
    f = lambda a: np.ascontiguousarray(np.asarray(a, np.float32))
    consts = _const_tables(N, NC)
    shared = dict(consts)
    shared["vecs"] = _pack_vecs(inp, DEPTH)
    for k in ("w_ada", "w_in", "subln_g", "w_fourier", "w_conv_out", "w_out", "w_ffn1", "w_ffn3", "w_ffn2"):
        shared[k] = f(inp[k])
    shared["lamv"] = np.ascontiguousarray(np.stack([f(inp["lam_q1"]), f(inp["lam_k1"]), f(inp["lam_q2"]), f(inp["lam_k2"])], axis=1))
    x = f(inp["x"]); ctx = f(inp["ctx"]); c = f(inp["c"]); c_ctx = f(inp["c_ctx"])
    maps = []
    for b in range(B):
        m = dict(shared)
        m["x"] = x[b]; m["ctx"] = ctx[b]
        m["cc"] = np.ascontiguousarray(np.stack([c[b], c_ctx], axis=1))
        maps.append(m)
    return maps


_CACHE = {}


def kernel(**inputs):
    N, NC, DEPTH, B = 2048, 256, 2, 8
    if "nc" not in _CACHE:
        _CACHE["nc"] = build_program(N, NC, DEPTH)[0]
    nc = _CACHE["nc"]
    maps = make_in_maps(inputs, N, NC, DEPTH, B)
    res = run_bass_kernel_spmd(nc, maps, core_ids=list(range(B)))
    return np.stack([np.asarray(r["out"], np.float32) for r in res.results], axis=0)
```
